# Optimizing a Trainium2 kernel written in Bass

```python
import jax, jax.numpy as jnp
from jax import lax
import numpy as np

D_MODEL = 1024
BATCH = 8
SEQ = 2048
DEPTH = 4

HEAD_DIM = 64
GRID_W = 64
Q_BLOCK = 128
EPS = 1e-6
NEG_BIG = -1e30
F_MIN = 1e-6
A_HEADS = 6
A_KV_HEADS = 2
ROPE_THETA = 10000.0
B_HEADS = 4
B_KEY_DIM = 64
B_CHUNK = 64
C_HEADS = 6
C_KV_HEADS = 2
C_BRANCHES = ((128, 1), (512, 4), (2048, 16))
D_FF = 2816
CONV_W = 3

A_W = A_HEADS * HEAD_DIM
B_W = B_HEADS * HEAD_DIM
B_K = B_HEADS * B_KEY_DIM
C_W = C_HEADS * HEAD_DIM
MIX_W = A_W + B_W + C_W
SPLITS = (A_W, A_KV_HEADS * HEAD_DIM, A_KV_HEADS * HEAD_DIM,
          B_K, B_K, B_K, B_W, B_W,
          C_W, C_KV_HEADS * HEAD_DIM, C_KV_HEADS * HEAD_DIM)
IN_W = sum(SPLITS)

kernel_name = "hybrid_parallel_rope_hgrn2_dilated_encoder"


def rms_norm(x, g):
    xf = x.astype(jnp.float32)
    y = xf * lax.rsqrt(jnp.mean(xf * xf, axis=-1, keepdims=True) + EPS)
    return (y * g.astype(jnp.float32)).astype(x.dtype)


def _rope_half(x, cos, sin):
    h = x.shape[-1] // 2
    x1, x2 = x[..., :h], x[..., h:]
    return jnp.concatenate([x1 * cos - x2 * sin, x2 * cos + x1 * sin], axis=-1)


def axial_rope(x):
    S = x.shape[1]
    n_rows = S // GRID_W
    row = jnp.repeat(jnp.arange(n_rows), GRID_W).astype(jnp.float32)
    col = jnp.tile(jnp.arange(GRID_W), n_rows).astype(jnp.float32)
    half = HEAD_DIM // 2
    inv = ROPE_THETA ** (-jnp.arange(0, half, 2, dtype=jnp.float32) / half)
    ang_r = (row[:, None] * inv)[:, None, :]
    ang_c = (col[:, None] * inv)[:, None, :]
    xf = x.astype(jnp.float32)
    xr = _rope_half(xf[..., :half], jnp.cos(ang_r), jnp.sin(ang_r))
    xc = _rope_half(xf[..., half:], jnp.cos(ang_c), jnp.sin(ang_c))
    return jnp.concatenate([xr, xc], axis=-1).astype(x.dtype)


def _to_query_blocks(q, n_kv):
    B_, S, H, D = q.shape
    nb = S // Q_BLOCK
    return q.reshape(B_, nb, Q_BLOCK, n_kv, H // n_kv, D).transpose(1, 0, 3, 4, 2, 5)


def _from_query_blocks(o):
    nb, B_, hk, g, qb, d = o.shape
    return o.transpose(1, 0, 4, 2, 3, 5).reshape(B_, nb * qb, hk * g * d)


def mixer_a(q, k, v, gq, gk):
    q = axial_rope(rms_norm(q, gq))
    k = axial_rope(rms_norm(k, gk))
    qb = _to_query_blocks(q, A_KV_HEADS)
    kt = k.transpose(0, 2, 1, 3)
    vt = v.transpose(0, 2, 1, 3)
    scale = HEAD_DIM ** -0.5

    def block(qi):
        s = jnp.einsum('bhgqd,bhkd->bhgqk', qi, kt).astype(jnp.float32) * scale
        p = jax.nn.softmax(s, axis=-1).astype(vt.dtype)
        return jnp.einsum('bhgqk,bhkd->bhgqd', p, vt)

    return _from_query_blocks(lax.map(block, qb))


def mixer_c(q, k, v, gq, gk):
    S = q.shape[1]
    q = rms_norm(q, gq)
    k = rms_norm(k, gk)
    qb = _to_query_blocks(q, C_KV_HEADS)
    kt = k.transpose(0, 2, 1, 3)
    vt = v.transpose(0, 2, 1, 3)
    nb = S // Q_BLOCK
    starts = jnp.arange(nb) * Q_BLOCK
    scale = HEAD_DIM ** -0.5
    slopes = (2.0 ** (-8.0 * np.arange(1, C_HEADS + 1) / C_HEADS)).astype(np.float32)
    slopes = jnp.asarray(slopes).reshape(C_KV_HEADS, C_HEADS // C_KV_HEADS, 1, 1)
    offsets = [r * np.arange(-(w // (2 * r)), w // (2 * r) + 1) for (w, r) in C_BRANCHES]

    def block(args):
        qi, t0 = args
        t = t0 + jnp.arange(Q_BLOCK)
        lses, outs = [], []
        for off in offsets:
            idx = t[:, None] + jnp.asarray(off)[None, :]
            valid = (idx >= 0) & (idx < S)
            idx = jnp.clip(idx, 0, S - 1)
            kg = kt[:, :, idx, :]
            vg = vt[:, :, idx, :]
            dist = jnp.asarray(np.abs(off).astype(np.float32))
            s = jnp.einsum('bhgqd,bhqkd->bhgqk', qi, kg).astype(jnp.float32) * scale - slopes * dist
            s = jnp.where(valid, s, NEG_BIG)
            lse = jax.nn.logsumexp(s, axis=-1)
            p = jnp.exp(s - lse[..., None]).astype(vg.dtype)
            outs.append(jnp.einsum('bhgqk,bhqkd->bhgqd', p, vg).astype(jnp.float32))
            lses.append(lse)
        w = jax.nn.softmax(jnp.stack(lses, axis=0), axis=0)
        o = jnp.einsum('nbhgq,nbhgqd->bhgqd', w, jnp.stack(outs, axis=0))
        return o.astype(qi.dtype)

    return _from_query_blocks(lax.map(block, (qb, starts)))


def hgrn2_scan(q, k, v, logf):
    B_, S, H, dk = q.shape
    dv = v.shape[-1]
    nc = S // B_CHUNK

    def to_chunks(a):
        return a.astype(jnp.float32).reshape(B_, nc, B_CHUNK, H, a.shape[-1]).transpose(1, 0, 3, 2, 4)

    qc, kc, vc, lc = to_chunks(q), to_chunks(k), to_chunks(v), to_chunks(logf)
    mask = jnp.tril(jnp.ones((B_CHUNK, B_CHUNK), dtype=bool))[:, :, None]

    def step(state, inp):
        q_, k_, v_, l_ = inp
        b = jnp.cumsum(l_, axis=2)
        o_inter = jnp.einsum('bhtk,bhkv->bhtv', q_ * jnp.exp(b), state)
        diff = b[:, :, :, None, :] - b[:, :, None, :, :]
        dec = jnp.where(mask, jnp.exp(jnp.where(mask, diff, 0.0)), 0.0)
        attn = jnp.einsum('bhtk,bhsk,bhtsk->bhts', q_, k_, dec)
        o_intra = jnp.einsum('bhts,bhsv->bhtv', attn, v_)
        b_last = b[:, :, -1:, :]
        new_state = jnp.exp(b_last[:, :, 0, :])[..., None] * state + \
            jnp.einsum('bhsk,bhsv->bhkv', k_ * jnp.exp(b_last - b), v_)
        return new_state, o_inter + o_intra

    state0 = jnp.zeros((B_, H, dk, dv), jnp.float32)
    _, o = lax.scan(step, state0, (qc, kc, vc, lc))
    return o.transpose(1, 0, 3, 2, 4).reshape(B_, S, H, dv).astype(v.dtype)


def mixer_b(q, f_fwd, f_bwd, i, g, lb_fwd, lb_bwd, g_norm):
    def log_forget(fpre, lb):
        lb = lb.reshape(B_HEADS, B_KEY_DIM)
        f = lb + (1.0 - lb) * jax.nn.sigmoid(fpre.astype(jnp.float32))
        return jnp.log(jnp.maximum(f, F_MIN))

    lff = log_forget(f_fwd, lb_fwd)
    lfb = log_forget(f_bwd, lb_bwd)
    o_f = hgrn2_scan(q, -jnp.expm1(lff), i, lff)
    flip = lambda a: jnp.flip(a, axis=1)
    o_b = flip(hgrn2_scan(flip(q), flip(-jnp.expm1(lfb)), flip(i), flip(lfb)))
    o = rms_norm(o_f + o_b, g_norm) * jax.nn.silu(g)
    return o.reshape(o.shape[0], o.shape[1], B_W)


def conv_ffn(h, w_up, conv_w, conv_b, w_down):
    u = h @ w_up
    up = jnp.pad(u, ((0, 0), (1, 1), (0, 0)))
    u = conv_w[0] * up[:, :-2] + conv_w[1] * up[:, 1:-1] + conv_w[2] * up[:, 2:] + conv_b
    a, b = jnp.split(u, 2, axis=-1)
    return (jax.nn.silu(a) * b) @ w_down


def setup_inputs(seed: int = 0) -> dict:
    key = jax.random.key(seed)
    ks = jax.random.split(key, 20)
    nrm = lambda k, shape, s: jax.random.normal(k, shape, jnp.float32) * s
    return {
        "x": nrm(ks[0], (BATCH, SEQ, D_MODEL), 1.0),
        "c": nrm(ks[1], (BATCH, D_MODEL), 1.0),
        "w_ada": nrm(ks[2], (DEPTH, D_MODEL, 6 * D_MODEL), D_MODEL ** -0.5),
        "b_ada": nrm(ks[3], (DEPTH, 6 * D_MODEL), 0.01),
        "norm_g": 1.0 + nrm(ks[4], (DEPTH, 2, D_MODEL), 0.01),
        "w_in": nrm(ks[5], (DEPTH, D_MODEL, IN_W), D_MODEL ** -0.5),
        "a_q_norm": 1.0 + nrm(ks[6], (DEPTH, HEAD_DIM), 0.01),
        "a_k_norm": 1.0 + nrm(ks[7], (DEPTH, HEAD_DIM), 0.01),
        "b_lb": nrm(ks[8], (2, DEPTH, B_K), 0.5),
        "b_out_norm": 1.0 + nrm(ks[9], (DEPTH, HEAD_DIM), 0.01),
        "c_q_norm": 1.0 + nrm(ks[10], (DEPTH, HEAD_DIM), 0.01),
        "c_k_norm": 1.0 + nrm(ks[11], (DEPTH, HEAD_DIM), 0.01),
        "w_out": nrm(ks[12], (DEPTH, MIX_W, D_MODEL), MIX_W ** -0.5),
        "w_up": nrm(ks[13], (DEPTH, D_MODEL, 2 * D_FF), D_MODEL ** -0.5),
        "conv_w": nrm(ks[14], (DEPTH, CONV_W, 2 * D_FF), CONV_W ** -0.5),
        "conv_b": nrm(ks[15], (DEPTH, 2 * D_FF), 0.01),
        "w_down": nrm(ks[16], (DEPTH, D_FF, D_MODEL), D_FF ** -0.5),
    }


def reference(x, c, w_ada, b_ada, norm_g, w_in, a_q_norm, a_k_norm, b_lb, b_out_norm,
              c_q_norm, c_k_norm, w_out, w_up, conv_w, conv_b, w_down):
    B_, S, _ = x.shape
    sm = jax.nn.softmax(b_lb.astype(jnp.float32), axis=1)
    lb_all = jnp.cumsum(sm, axis=1) - sm[:, :1]
    split_idx = np.cumsum(SPLITS)[:-1].tolist()
    for l in range(DEPTH):
        mod = jax.nn.silu(c) @ w_ada[l] + b_ada[l]
        sh1, sc1, g1, sh2, sc2, g2 = jnp.split(mod[:, None, :], 6, axis=-1)
        h = rms_norm(x, norm_g[l, 0]) * (1.0 + sc1) + sh1
        parts = jnp.split(h @ w_in[l], split_idx, axis=-1)
        hd = lambda a, n: a.reshape(B_, S, n, a.shape[-1] // n)
        aq, ak, av, bq, bff, bfb, bi, bg, cq, ck, cv = parts
        o_a = mixer_a(hd(aq, A_HEADS), hd(ak, A_KV_HEADS), hd(av, A_KV_HEADS), a_q_norm[l], a_k_norm[l])
        o_b = mixer_b(hd(bq, B_HEADS), hd(bff, B_HEADS), hd(bfb, B_HEADS), hd(bi, B_HEADS), hd(bg, B_HEADS),
                      lb_all[0, l], lb_all[1, l], b_out_norm[l])
        o_c = mixer_c(hd(cq, C_HEADS), hd(ck, C_KV_HEADS), hd(cv, C_KV_HEADS), c_q_norm[l], c_k_norm[l])
        mix = jnp.concatenate([o_a, o_b, o_c], axis=-1) @ w_out[l]
        x = x + g1 * mix
        h = rms_norm(x, norm_g[l, 1]) * (1.0 + sc2) + sh2
        x = x + g2 * conv_ffn(h, w_up[l], conv_w[l], conv_b[l], w_down[l])
    return x
```

```python
import numpy as np
import ml_dtypes
from contextlib import ExitStack
import concourse.bass as bass
import concourse.mybir as mybir
from concourse.bass_utils import run_bass_kernel_spmd

F32 = mybir.dt.float32
BF16 = mybir.dt.bfloat16
AF = mybir.ActivationFunctionType
ALU = mybir.AluOpType
AX = mybir.AxisListType

ENGS = ("pe", "act", "dve", "pool", "sp")
T = 2048
D = 1024
NL = 4
EPS = 1e-6
SLOPES = [2.0 ** (-8.0 * i / 6) for i in range(1, 7)]
BR = (1, 4, 16)


class _Rec:
    def __getattr__(self, name):
        def f(*a, **k):
            self.call = (name, a, k)
            return None
        return f


class Sched:
    def __init__(self, nc, self_sync=True):
        self.nc = nc
        self.self_sync = self_sync
        self.ops = {e: [] for e in ENGS}
        self.lastw = {}
        self.readers = {}
        self.dma_tot = {}

    def _cur(self, ev):
        if ev[0] == 'D':
            return ('D', ev[1], self.dma_tot[ev[1]])
        return ev

    def _add(self, eng, fn, reads, writes, dma=None, extra=()):
        deps = set(extra)
        for k in reads:
            ev = self.lastw.get(k)
            if ev is not None:
                deps.add(self._cur(ev))
        for k in writes:
            ev = self.lastw.get(k)
            if ev is not None:
                deps.add(self._cur(ev))
            for r in self.readers.get(k, ()):
                deps.add(self._cur(r))
        idx = len(self.ops[eng])
        if dma is not None:
            self.dma_tot[dma] = self.dma_tot.get(dma, 0) + 16
            myev = ('D', dma, None)
        else:
            myev = ('E', eng, idx)
        d2 = set()
        for d in deps:
            if d[0] == 'E' and d[1] == eng:
                if eng == 'pe' or not self.self_sync or dma is not None:
                    continue
            d2.add(d)
        rec = _Rec()
        fn(rec)
        self.ops[eng].append(dict(call=rec.call, deps=d2, dma=dma))
        for k in reads:
            self.readers.setdefault(k, []).append(myev)
        for k in writes:
            self.lastw[k] = myev
            self.readers[k] = []
        return idx

    def op(self, eng, fn, reads=(), writes=()):
        return self._add(eng, fn, list(reads), list(writes))

    def dma(self, eng, dsem, fn, reads=(), writes=()):
        return self._add(eng, fn, list(reads), list(writes), dma=dsem)

    def barrier(self, engs=("pe", "act", "dve", "sp")):
        evs = []
        for e in engs:
            for i in range(len(self.ops[e]) - 1, -1, -1):
                if self.ops[e][i]['dma'] is None:
                    evs.append(('E', e, i))
                    break
        for name in self.dma_tot:
            if not name.startswith("w"):
                evs.append(('D', name, self.dma_tot[name]))
        for e in engs:
            self._add(e, lambda eng: eng.nop(), [], [], extra=[v for v in evs if not (v[0] == 'E' and v[1] == e)])

    def emit(self, final_dsems=()):
        nc = self.nc
        need = {e: set() for e in ENGS}
        for e in ENGS:
            for o in self.ops[e]:
                for d in o['deps']:
                    if d[0] == 'E':
                        need[d[1]].add(d[2])
        LIM = 30000
        count_at = {e: {} for e in ENGS}
        n_epochs = {}
        for e in ENGS:
            c = 0
            ep = 0
            for i in range(len(self.ops[e])):
                if i in need[e]:
                    c += 1
                    if c > LIM:
                        ep += 1
                        c = 1
                    count_at[e][i] = (ep, c)
            n_epochs[e] = ep + 1
        with ExitStack() as st:
            esem = {}
            for e in ENGS:
                for ep in range(n_epochs[e]):
                    esem[(e, ep)] = st.enter_context(nc.semaphore(f"s_{e}_{ep}"))
            dsem = {}
            for name in self.dma_tot:
                dsem[name] = st.enter_context(nc.semaphore(f"d_{name}"))
            block = st.enter_context(nc.Block())
            engobj = {"pe": "tensor", "act": "scalar", "dve": "vector", "pool": "gpsimd", "sp": "sync"}

            def make(e):
                def body(eng):
                    known = {}
                    for i, o in enumerate(self.ops[e]):
                        w = {}
                        for d in o['deps']:
                            if d[0] == 'E':
                                ep, v = count_at[d[1]][d[2]]
                                kk = ('E', d[1], ep)
                                s = esem[(d[1], ep)]
                            else:
                                kk = ('D', d[1])
                                s = dsem[d[1]]
                                v = d[2]
                            if known.get(kk, 0) >= v:
                                continue
                            if kk not in w or w[kk][1] < v:
                                w[kk] = (s, v)
                        for kk, (s, v) in w.items():
                            eng.wait_ge(s, v)
                            known[kk] = v
                        cname, ca, ck = o['call']
                        ins = getattr(eng, cname)(*ca, **ck)
                        if o['dma'] is not None:
                            ins.then_inc(dsem[o['dma']], 16)
                        elif i in need[e]:
                            ep, v = count_at[e][i]
                            ins.then_inc(esem[(e, ep)], 1)
                    if e == 'sp':
                        for name in final_dsems:
                            eng.wait_ge(dsem[name], self.dma_tot[name])
                return body
            for e in ENGS:
                getattr(block, engobj[e])(make(e))


def _consts():
    bf = ml_dtypes.bfloat16
    c = {}
    c["ident"] = np.eye(128, dtype=np.float32).astype(bf)
    c["onesD"] = np.full((128, 128), 1.0 / 1024, np.float32).astype(bf)
    c["ones64"] = np.full((64, 64), 1.0 / 64, np.float32).astype(bf)
    b = np.zeros((128, 128), np.float32)
    b[:64, :64] = 1.0 / 64
    b[64:, 64:] = 1.0 / 64
    c["blk64"] = b.astype(bf)
    R = np.zeros((64, 64), np.float32)
    for d in list(range(0, 16)) + list(range(32, 48)):
        R[d + 16, d] = -1.0
    for d in list(range(16, 32)) + list(range(48, 64)):
        R[d - 16, d] = 1.0
    R2 = np.zeros((128, 128), np.float32)
    R2[:64, :64] = R
    R2[64:, 64:] = R
    c["rrot"] = R2.astype(bf)
    t = np.arange(T)
    row = (t // 64).astype(np.float64)
    col = (t % 64).astype(np.float64)
    inv = 10000.0 ** (-np.arange(0, 32, 2, dtype=np.float64) / 32)
    ang = np.zeros((64, T))
    ang[0:16] = row[None, :] * inv[:, None]
    ang[16:32] = row[None, :] * inv[:, None]
    ang[32:48] = col[None, :] * inv[:, None]
    ang[48:64] = col[None, :] * inv[:, None]
    cs1 = np.stack([np.cos(ang), np.sin(ang)], 1).astype(np.float32)
    c["cossin"] = np.concatenate([cs1, cs1], 0).astype(bf)
    s = np.arange(128)[:, None]
    q = np.arange(128)[None, :]
    same = (s // 64) == (q // 64)
    same32 = (s // 32) == (q // 32)
    s_lo = (s % 64) < 32
    q_lo = (q % 64) < 32
    c["bmask"] = np.stack([(same32 & (s <= q)), (same & s_lo & ~q_lo),
                           (same32 & (s >= q)), (same & ~s_lo & q_lo)], 1).astype(np.float32).astype(bf)
    rst = np.ones((128, 512), np.float32)
    rst[:, ::64] = 0.0
    c["rst"] = rst
    cm = np.zeros((128, 6, 3, 3, 128), np.float32)
    for h in range(6):
        for ri, r in enumerate(BR):
            for di, dl in enumerate((-1, 0, 1)):
                dd = 128 * dl + s - q
                cm[:, h, ri, di, :] = np.where(np.abs(dd) <= 64, np.exp(-SLOPES[h] * r * np.abs(dd)), 0.0)
    c["cmask"] = cm.reshape(128, 6 * 9 * 128).astype(bf)
    return c


_CONST_SHAPES = {"ident": [128, 128], "onesD": [128, 128], "ones64": [64, 64], "blk64": [128, 128], "rrot": [128, 128],
                 "cossin": [128, 2, T], "bmask": [128, 4, 128], "cmask": [128, 6 * 9 * 128]}


def build(depth=NL, do_a=True, do_b=True, do_c=True, do_ffn=True, self_sync=True, dbg=None):
    nc = bass.Bass("TRN2", target_bir_lowering=False)

    def dram(name, shape, dt=F32, kind="ExternalInput"):
        return nc.dram_tensor(name, list(shape), dt, kind=kind).ap()

    d_xT = dram("xT", [D, T])
    d_out = dram("outT", [D, T], kind="ExternalOutput")
    d_c = dram("c128", [128, 8])
    d_wada = dram("w_ada", [NL, D, 6 * D])
    d_bada = dram("b_ada_l", [128, NL, 48])
    d_ng = dram("norm_g_l", [128, NL, 2, 8])
    d_win = dram("w_in", [NL, D, 2560])
    d_hn = dram("hn", [128, NL, 4])
    d_bon = dram("bon", [128, NL])
    d_blb = dram("b_lb_l", [128, 4, NL])
    d_wout = dram("w_out", [NL, D, D])
    d_wup = dram("w_up", [NL, D, 5632])
    d_cw = dram("conv_w_l", [128, NL, 3, 44])
    d_cb = dram("conv_b_l", [128, NL, 44])
    d_wdn = dram("w_down", [NL, 2816, D])
    d_rst = dram("rst", [128, 512])
    d_k = {k: dram("k_" + k, v, BF16) for k, v in _CONST_SHAPES.items()}

    S = Sched(nc, self_sync=self_sync)
    st = ExitStack()
    with st:
        def sb(name, shape, dt=F32):
            return st.enter_context(nc.sbuf_tensor(name, list(shape), dt))

        xT = sb("xT_sb", [128, 8, T])
        hT = sb("hT_sb", [128, 8, T + 2], BF16)
        NSLOT = 2 if dbg else 3
        wring = [sb(f"wring{i}", [128, 4096], BF16) for i in range(NSLOT)]
        mod = sb("mod", [128, NL, 48])
        bada = sb("bada", [128, NL, 48])
        ng = sb("ng", [128, NL, 2, 8])
        gs = sb("gs", [128, NL, 2, 8])
        cw = sb("cw", [128, NL, 3, 44])
        cbias = sb("cbias", [128, NL, 44])
        hn = sb("hn_sb", [128, NL, 4])
        bon = sb("bon_sb", [128, NL])
        blb = sb("blb", [128, 4, NL])
        lbv = sb("lbv", [128, 4, NL])
        oml = sb("oml", [128, 4, NL])
        lbs = sb("lbs", [128, 4])
        c_sb = sb("c_sb", [128, 8])
        sc_f = sb("sc_f", [128, 8])
        sc_b = sb("sc_b", [128, 8], BF16)
        ident = sb("ident", [128, 128], BF16)
        onesD = sb("onesD", [128, 128], BF16)
        ones64 = sb("ones64", [64, 64], BF16)
        blk64 = sb("blk64", [128, 128], BF16)
        rrot = sb("rrot", [128, 128], BF16)
        bmask = sb("bmask", [128, 4, 128], BF16)
        rst = sb("rst_sb", [128, 512])
        eps_t = sb("eps_t", [128, 1])
        one_t = sb("one_t", [128, 1])
        if dbg:
            d_dbg = dram("dbg", [128, T], kind="ExternalOutput")
            dbgt = sb("dbgt", [128, T])
            S.op("dve", lambda e: e.memset(dbgt[:], 0.0), writes=["dbgt"])

        def tap(name, ap, key, parts=128, n=T):
            if dbg != name:
                return
            S.op("act", lambda e: e.activation(out=dbgt[0:parts, 0:n], in_=ap, func=AF.Copy), reads=[key, "dbgt"], writes=["dbgt"])
            S.dma("sp", "st", lambda e: e.dma_start(out=d_dbg, in_=dbgt[:]), reads=["dbgt"])
        RW = nc.sbuf_bytes_remaining // 4 - 64
        assert RW * 4 >= 70000, RW
        R = sb("R", [128, RW])
        P = st.enter_context(nc.psum_tensor("P", [128, 4096], F32))

        def bank(i):
            return P[:, 512 * i:512 * (i + 1)]

        def pk(i):
            return ("ps", i)

        class Carver:
            def __init__(self, tag):
                self.off = 0
                self.tag = tag

            def get(self, name, shape, dt=F32, parts=128):
                n = int(np.prod(shape[1:]))
                nb = n * (4 if dt == F32 else 2)
                nb4 = (nb + 3) // 4
                ap = R[0:shape[0], self.off:self.off + nb4]
                if dt == BF16:
                    ap = ap.bitcast(BF16)[:, 0:n]
                self.off += nb4
                assert self.off <= RW, (self.tag, name, self.off * 4)
                if len(shape) == 3:
                    ap = ap.rearrange("p (a b) -> p a b", a=shape[1])
                elif len(shape) == 4:
                    ap = ap.rearrange("p (a b c) -> p a b c", a=shape[1], b=shape[2])
                return ap

        wctr = [0]

        def wload(src, shape):
            i = wctr[0] % NSLOT
            wctr[0] += 1
            a, b = shape
            view = wring[i][:, 0:a * b].rearrange("p (a b) -> p a b", a=a)
            key = ("w", i)
            S.dma("pool", f"w{i}", lambda e: e.dma_start(out=view, in_=src, max_dma_last_dim=4096), writes=[key])
            return view, key

        def ld(dst, src, key):
            S.dma("sp", "ld", lambda e: e.dma_start(out=dst, in_=src), writes=[key])

        ld(xT[:], d_xT.rearrange("(c p) t -> p c t", p=128), "xT")
        for nm, dst, src in (("bada", bada, d_bada), ("ng", ng, d_ng), ("cw", cw, d_cw), ("cbias", cbias, d_cb),
                             ("hn", hn, d_hn), ("bon", bon, d_bon), ("blb", blb, d_blb), ("c", c_sb, d_c),
                             ("ident", ident, d_k["ident"]), ("onesD", onesD, d_k["onesD"]),
                             ("ones64", ones64, d_k["ones64"]), ("blk64", blk64, d_k["blk64"]),
                             ("rrot", rrot, d_k["rrot"]), ("bmask", bmask, d_k["bmask"]), ("rst", rst, d_rst)):
            ld(dst[:], src, nm)
        S.op("dve", lambda e: e.memset(eps_t[:], EPS), writes=["eps"])
        S.op("dve", lambda e: e.memset(one_t[:], 1.0), writes=["one"])
        S.op("dve", lambda e: e.memset(hT[:, :, 0:1], 0.0), writes=["hT"])
        S.op("dve", lambda e: e.memset(hT[:, :, T + 1:T + 2], 0.0), writes=["hT"])

        S.op("act", lambda e: e.activation(out=lbv[:], in_=blb[:], func=AF.Exp), reads=["blb"], writes=["lbv"])
        S.op("dve", lambda e: e.tensor_reduce(out=lbs[:], in_=lbv[:], axis=AX.X, op=ALU.add), reads=["lbv"], writes=["lbs"])
        S.op("dve", lambda e: e.reciprocal(out=lbs[:], in_=lbs[:]), reads=["lbs"], writes=["lbs"])
        for l in range(NL):
            S.op("dve", lambda e, l=l: e.tensor_tensor(out=lbv[:, :, l], in0=lbv[:, :, l], in1=lbs[:], op=ALU.mult),
                 reads=["lbv", "lbs"], writes=["lbv"])
        S.op("dve", lambda e: e.memset(lbv[:, :, 0:1], 0.0), reads=["lbv"], writes=["lbv"])
        for l in range(2, NL):
            S.op("dve", lambda e, l=l: e.tensor_tensor(out=lbv[:, :, l], in0=lbv[:, :, l], in1=lbv[:, :, l - 1], op=ALU.add),
                 reads=["lbv"], writes=["lbv"])
        S.op("dve", lambda e: e.tensor_scalar(out=oml[:], in0=lbv[:], scalar1=-1.0, scalar2=1.0, op0=ALU.mult, op1=ALU.add),
             reads=["lbv"], writes=["oml"])

        S.op("act", lambda e: e.activation(out=sc_f[:], in_=c_sb[:], func=AF.Exp, scale=-1.0), reads=["c"], writes=["scf"])
        S.op("dve", lambda e: e.tensor_scalar_add(out=sc_f[:], in0=sc_f[:], scalar1=1.0), reads=["scf"], writes=["scf"])
        S.op("dve", lambda e: e.reciprocal(out=sc_f[:], in_=sc_f[:]), reads=["scf"], writes=["scf"])
        S.op("dve", lambda e: e.tensor_tensor(out=sc_b[:], in0=sc_f[:], in1=c_sb[:], op=ALU.mult), reads=["scf", "c"], writes=["scb"])
        for l in range(depth):
            for og in range(12):
                wv, wk = wload(d_wada[l, :, 512 * og:512 * (og + 1)].rearrange("(c p) n -> p c n", p=128), (8, 512))
                for m in range(4):
                    col = og * 4 + m
                    for k in range(8):
                        S.op("pe", lambda e, wv=wv, m=m, k=k, col=col: e.matmul(
                            P[:, col:col + 1], lhsT=wv[:, k, 128 * m:128 * (m + 1)], rhs=sc_b[:, k:k + 1],
                            start=(k == 0), stop=(k == 7)), reads=[wk, "scb"], writes=[pk(0)])
            S.op("dve", lambda e, l=l: e.tensor_tensor(out=mod[:, l, :], in0=P[:, 0:48], in1=bada[:, l, :], op=ALU.add),
                 reads=[pk(0), "bada"], writes=["mod"])
            for w_, c0 in ((0, 8), (1, 32)):
                S.op("dve", lambda e, l=l, w_=w_, c0=c0: e.scalar_tensor_tensor(
                    out=gs[:, l, w_, :], in0=mod[:, l, c0:c0 + 8], scalar=1.0, in1=ng[:, l, w_, :], op0=ALU.add, op1=ALU.mult),
                    reads=["mod", "ng"], writes=["gs"])

        def norm_mod(l, w_):
            S.barrier()
            cv = Carver("norm")
            sq = [cv.get(f"sq{i}", [128, 512], BF16) for i in range(2)]
            rs = [cv.get(f"rs{i}", [128, 512]) for i in range(2)]
            tm = [cv.get(f"tm{i}", [128, 512]) for i in range(2)]
            sh0 = 0 if w_ == 0 else 24
            for blk in range(4):
                tsl = slice(512 * blk, 512 * (blk + 1))
                pb = 6 + (blk % 2)
                for c in range(8):
                    i = (blk * 8 + c) % 2
                    S.op("act", lambda e, i=i, c=c: e.activation(out=sq[i], in_=xT[:, c, tsl], func=AF.Square),
                         reads=["xT"], writes=[("nsq", i)])
                    S.op("pe", lambda e, i=i, c=c, pb=pb: e.matmul(bank(pb), lhsT=onesD[:], rhs=sq[i], start=(c == 0), stop=(c == 7)),
                         reads=[("nsq", i), "onesD"], writes=[pk(pb)])
                r = rs[blk % 2]
                S.op("act", lambda e, r=r, pb=pb: e.activation(out=r, in_=bank(pb), func=AF.Ln, bias=eps_t[:, 0:1], scale=1.0),
                     reads=[pk(pb), "eps"], writes=[("nrs", blk % 2)])
                S.op("act", lambda e, r=r: e.activation(out=r, in_=r, func=AF.Exp, scale=-0.5),
                     reads=[("nrs", blk % 2)], writes=[("nrs", blk % 2)])
                for c in range(8):
                    i = (blk * 8 + c) % 2
                    S.op("dve", lambda e, i=i, c=c, r=r: e.scalar_tensor_tensor(
                        out=tm[i], in0=xT[:, c, tsl], scalar=gs[:, l, w_, c:c + 1], in1=r, op0=ALU.mult, op1=ALU.mult),
                        reads=["xT", "gs", ("nrs", blk % 2)], writes=[("ntm", i)])
                    S.op("act", lambda e, i=i, c=c: e.activation(
                        out=hT[:, c, 1 + 512 * blk:1 + 512 * (blk + 1)], in_=tm[i], func=AF.Identity,
                        bias=mod[:, l, sh0 + c:sh0 + c + 1], scale=1.0),
                        reads=[("ntm", i), "mod"], writes=["hT"])
            S.barrier()
            tap("mod", mod[:, l, :], "mod", n=48)
            tap("hT0", hT[:, 0, 1:T + 1], "hT")
            tap("hT7", hT[:, 7, 1:T + 1], "hT")

        def proj_fm(wv, wk, c0, blk, pb):
            for k in range(8):
                S.op("pe", lambda e, k=k: e.matmul(bank(pb), lhsT=wv[:, k, c0:c0 + 128],
                                                   rhs=hT[:, k, 1 + 512 * blk:1 + 512 * (blk + 1)],
                                                   start=(k == 0), stop=(k == 7)),
                     reads=[wk, "hT"], writes=[pk(pb)])

        def outproj_partial(l, row0, nk, src, srckey, g_col0):
            wv, wk = wload(d_wout[l, row0:row0 + 128 * nk, :].rearrange("(c p) n -> p c n", p=128), (nk, 1024))
            i = 0
            for m in range(8):
                for blk in range(4):
                    pb = 6 + (i % 2)
                    i += 1
                    tsl = slice(512 * blk, 512 * (blk + 1))
                    for k in range(nk):
                        S.op("pe", lambda e, k=k, m=m, pb=pb, tsl=tsl: e.matmul(
                            bank(pb), lhsT=wv[:, k, 128 * m:128 * (m + 1)], rhs=src[:, k, tsl], start=(k == 0), stop=(k == nk - 1)),
                            reads=[wk, srckey], writes=[pk(pb)])
                    S.op("dve", lambda e, m=m, pb=pb, tsl=tsl: e.scalar_tensor_tensor(
                        out=xT[:, m, tsl], in0=bank(pb), scalar=mod[:, l, g_col0 + m:g_col0 + m + 1], in1=xT[:, m, tsl],
                        op0=ALU.mult, op1=ALU.add), reads=[pk(pb), "mod", "xT"], writes=["xT"])

        prep_ctr = [0]

        def qk_prep(l, wv, wk, c0, nchunks, gcol, dst_f, dstkey, tsets, cs=None):
            for hc in range(nchunks):
                for blk in range(4):
                    tsl = slice(512 * blk, 512 * (blk + 1))
                    i = prep_ctr[0] % 2
                    prep_ctr[0] += 1
                    qg, sq, r, t1, t2 = tsets[i]
                    pb, mb, rb = i, 2 + i, 4 + i
                    proj_fm(wv, wk, c0 + 128 * hc, blk, pb)
                    S.op("act", lambda e: e.activation(out=sq, in_=bank(pb), func=AF.Square), reads=[pk(pb)], writes=[("p_sq", i)])
                    S.op("pe", lambda e: e.matmul(bank(mb), lhsT=blk64[:], rhs=sq, start=True, stop=True),
                         reads=[("p_sq", i), "blk64"], writes=[pk(mb)])
                    if cs is not None:
                        S.op("act", lambda e: e.activation(out=qg, in_=bank(pb), func=AF.Identity, scale=hn[:, l, gcol:gcol + 1]),
                             reads=[pk(pb), "hn"], writes=[("p_qg", i)])
                        S.op("pe", lambda e: e.matmul(bank(rb), lhsT=rrot[:], rhs=qg, start=True, stop=True),
                             reads=[("p_qg", i), "rrot"], writes=[pk(rb)])
                    S.op("act", lambda e: e.activation(out=r, in_=bank(mb), func=AF.Ln, bias=eps_t[:, 0:1], scale=1.0),
                         reads=[pk(mb), "eps"], writes=[("p_r", i)])
                    S.op("act", lambda e: e.activation(out=r, in_=r, func=AF.Exp, scale=-0.5), reads=[("p_r", i)], writes=[("p_r", i)])
                    if cs is not None:
                        S.op("dve", lambda e: e.tensor_tensor(out=t1, in0=qg, in1=cs[:, 0, tsl], op=ALU.mult),
                             reads=[("p_qg", i), "cs"], writes=[("p_t1", i)])
                        S.op("dve", lambda e: e.tensor_tensor(out=t2, in0=bank(rb), in1=cs[:, 1, tsl], op=ALU.mult),
                             reads=[pk(rb), "cs"], writes=[("p_t2", i)])
                        S.op("dve", lambda e: e.tensor_tensor(out=t1, in0=t1, in1=t2, op=ALU.add),
                             reads=[("p_t1", i), ("p_t2", i)], writes=[("p_t1", i)])
                        for hh in range(2):
                            S.op("dve", lambda e: e.tensor_tensor(out=dst_f[0:64, 2 * hc + hh, tsl], in0=t1[64 * hh:64 * hh + 64],
                                                                  in1=r[64 * hh:64 * hh + 64], op=ALU.mult),
                                 reads=[("p_t1", i), ("p_r", i)], writes=[dstkey])
                    else:
                        for hh in range(2):
                            S.op("dve", lambda e: e.scalar_tensor_tensor(
                                out=dst_f[0:64, 2 * hc + hh, tsl], in0=P[64 * hh:64 * hh + 64, 512 * pb:512 * (pb + 1)],
                                scalar=hn[64 * hh:64 * hh + 64, l, gcol:gcol + 1], in1=r[64 * hh:64 * hh + 64], op0=ALU.mult, op1=ALU.mult),
                                reads=[pk(pb), "hn", ("p_r", i)], writes=[dstkey])

        def build_vext(wv, wk, c0, vext, r):
            S.op("dve", lambda e: e.memset(vext[:, :, :, 64:128], 1.0), writes=["vext"])
            for g4 in range(4):
                pb = 4 + (g4 % 2)
                for j in range(4):
                    tb = 4 * g4 + j
                    nb = 16 // r
                    c, b = tb // nb, tb % nb
                    start = 1 + c + r * 128 * b
                    for k in range(8):
                        S.op("pe", lambda e, k=k, j=j, pb=pb, start=start: e.matmul(
                            P[:, 512 * pb + 128 * j:512 * pb + 128 * (j + 1)],
                            lhsT=hT[:, k, start:start + 128 * r:r] if r > 1 else hT[:, k, start:start + 128],
                            rhs=wv[:, k, c0:c0 + 128], start=(k == 0), stop=(k == 7)),
                            reads=[wk, "hT"], writes=[pk(pb)])
                S.op("act", lambda e, g4=g4, pb=pb: e.activation(
                    out=vext[:, 4 * g4:4 * g4 + 4, :, 0:64],
                    in_=bank(pb).rearrange("p (a k d) -> p a k d", a=4, k=2), func=AF.Copy),
                    reads=[pk(pb)], writes=["vext"])

        def finalize_head(src_num, src_den, srckeys, dst_fn, dstkey, tmp):
            den, tmpo = tmp
            for blk in range(4):
                S.op("act", lambda e, blk=blk: e.activation(out=den, in_=src_den(blk), func=AF.Copy), reads=srckeys(blk), writes=["f_den"])
                S.op("dve", lambda e: e.reciprocal(out=den, in_=den), reads=["f_den"], writes=["f_den"])
                S.op("dve", lambda e, blk=blk: e.tensor_tensor(out=tmpo, in0=src_num(blk), in1=den, op=ALU.mult),
                     reads=srckeys(blk) + ["f_den"], writes=["f_tmp"])
                S.op("act", lambda e, blk=blk: e.activation(out=dst_fn(blk), in_=tmpo, func=AF.Copy), reads=["f_tmp"], writes=[dstkey])

        def mixer_a(l):
            cv = Carver("A")
            qr_f = cv.get("qr", [128, 6, T], BF16)
            kr_f = cv.get("kr", [128, 2, T], BF16)
            qr, kr = qr_f[0:64], kr_f[0:64]
            S.op("dve", lambda e: e.memset(qr_f[64:128], 0.0), writes=["qr"])
            S.op("dve", lambda e: e.memset(kr_f[64:128], 0.0), writes=["kr"])
            vext = cv.get("vext", [128, 16, 2, 128], BF16)
            o_a = cv.get("o_a", [128, 3, T], BF16)
            off0 = cv.off
            cs = cv.get("cs", [128, 2, T], BF16)
            tsets = [(cv.get(f"qg{i}", [128, 512], BF16), cv.get(f"sq{i}", [128, 512], BF16), cv.get(f"r{i}", [128, 512]),
                      cv.get(f"t1{i}", [128, 512]), cv.get(f"t2{i}", [128, 512])) for i in range(2)]
            S.dma("sp", "ld", lambda e: e.dma_start(out=cs, in_=d_k["cossin"]), writes=["cs"])
            wv, wk = wload(d_win[l, :, 0:512].rearrange("(c p) n -> p c n", p=128), (8, 512))
            qk_prep(l, wv, wk, 0, 3, 0, qr_f, "qr", tsets, cs=cs)
            qk_prep(l, wv, wk, 384, 1, 1, kr_f, "kr", tsets, cs=cs)
            wv2, wk2 = wload(d_win[l, :, 512:640].rearrange("(c p) n -> p c n", p=128), (8, 128))
            build_vext(wv2, wk2, 0, vext, 1)
            tap("qr0", qr[:, 0, :], "qr", parts=64)
            tap("qr5", qr[:, 5, :], "qr", parts=64)
            tap("kr1", kr[:, 1, :], "kr", parts=64)
            S.barrier()
            cv.off = off0
            pT = [cv.get(f"pT{i}", [128, 1024], BF16) for i in range(2)]
            ftmp = [cv.get(f"den{i}", [64, 512]) for i in range(2)]
            pending = []
            it = 0
            for h in range(6):
                kv = h // 3
                for qb in range(4):
                    qsl = slice(512 * qb, 512 * (qb + 1))
                    ob = 4 + (it % 2)
                    it += 1

                    def score(pp):
                        for u in range(2):
                            kc = 2 * pp + u
                            sbk = 2 * (pp % 2) + u
                            S.op("pe", lambda e: e.matmul(bank(sbk), lhsT=kr_f[:, kv, 128 * kc:128 * (kc + 1)], rhs=qr_f[:, h, qsl],
                                                          start=True, stop=True), reads=["kr", "qr"], writes=[pk(sbk)])

                    def pv(pp):
                        sb0 = 2 * (pp % 2)
                        pi = pp % 2
                        S.op("act", lambda e: e.activation(out=pT[pi], in_=P[:, 512 * sb0:512 * sb0 + 1024], func=AF.Exp, scale=0.125),
                             reads=[pk(sb0), pk(sb0 + 1)], writes=[("pT", pi)])
                        for u in range(2):
                            kc = 2 * pp + u
                            S.op("pe", lambda e: e.matmul(bank(ob), lhsT=vext[:, kc, kv, :], rhs=pT[pi][:, 512 * u:512 * (u + 1)],
                                                          start=(kc == 0), stop=(kc == 15)),
                                 reads=[("pT", pi), "vext"], writes=[pk(ob)])
                    def fin(ob=ob, h=h, qsl=qsl, fi=it % 2):
                        po = 64 * (h % 2)
                        S.op("act", lambda e: e.activation(out=ftmp[fi], in_=P[64:128, 512 * ob:512 * (ob + 1)], func=AF.Ln),
                             reads=[pk(ob)], writes=[("f_den", fi)])
                        S.op("act", lambda e: e.activation(out=ftmp[fi], in_=ftmp[fi], func=AF.Exp, scale=-1.0),
                             reads=[("f_den", fi)], writes=[("f_den", fi)])
                        S.op("dve", lambda e: e.tensor_tensor(out=o_a[po:po + 64, h // 2, qsl], in0=P[0:64, 512 * ob:512 * (ob + 1)], in1=ftmp[fi],
                                                              op=ALU.mult), reads=[pk(ob), ("f_den", fi)], writes=["o_a"])
                    score(0)
                    score(1)
                    for pp in range(8):
                        pv(pp)
                        if pp + 2 < 8:
                            score(pp + 2)
                        if pp == 1 and pending:
                            pending.pop()()
                    pending.append(fin)
            pending.pop()()
            tap("oa0", o_a[:, 0, :], "o_a")
            tap("oa2", o_a[:, 2, :], "o_a")
            outproj_partial(l, 0, 3, o_a, "o_a", 16)
            S.barrier()


        def mixer_b(l):
            cv = Carver("B")
            o_b = cv.get("o_b", [128, 2, T], BF16)
            qb_ = cv.get("qb", [128, T], BF16)
            qh = cv.get("qh", [128, T], BF16)
            q1 = cv.get("q1", [128, T], BF16)
            q2 = cv.get("q2", [128, T], BF16)
            k1 = cv.get("k1", [128, T], BF16)
            k2 = cv.get("k2", [128, T], BF16)
            vtok = cv.get("vtok", [128, 16, 128], BF16)
            khT = cv.get("khT", [128, T], BF16)
            khtok = cv.get("khtok", [128, 16, 128], BF16)
            Sbf = cv.get("Sbf", [128, 32, 64], BF16)
            Dd = cv.get("Dd", [128, 32])
            S32 = [cv.get(f"S32{i}", [128, 64]) for i in range(2)]
            f1 = cv.get("f1", [128, 512])
            lf = cv.get("lf", [128, 512])
            bb = cv.get("bb", [128, 512])
            cc = cv.get("cc", [128, 512])
            bm = cv.get("bm", [128, 512])
            ex = cv.get("ex", [128, 512])
            kk = cv.get("kk", [128, 512], BF16)
            att = [[cv.get(f"att{m}{i}", [128, 128], BF16) for i in range(2)] for m in range(2)]
            o32 = cv.get("o32", [128, T])
            sqb = cv.get("sqb", [128, 512], BF16)
            rsd, on, gg = f1, lf, bb
            wA, kA = wload(d_win[l, :, 640:1152].rearrange("(c p) n -> p c n", p=128), (8, 512))
            wB, kB = wload(d_win[l, :, 1152:1664].rearrange("(c p) n -> p c n", p=128), (8, 512))
            wC, kC = wload(d_win[l, :, 1664:1920].rearrange("(c p) n -> p c n", p=128), (8, 256))
            v64 = lambda ap: ap.rearrange("p (n i) -> p n i", i=64)
            v32 = lambda ap: ap.rearrange("p (n i) -> p n i", i=32)
            si = 0
            for hp in range(2):
                for blk in range(4):
                    pb = blk % 2
                    proj_fm(wA, kA, 128 * hp, blk, pb)
                    S.op("act", lambda e: e.activation(out=qb_[:, 512 * blk:512 * (blk + 1)], in_=bank(pb), func=AF.Copy),
                         reads=[pk(pb)], writes=["qb"])
                for g4 in range(4):
                    pb = 4 + (g4 % 2)
                    for j in range(4):
                        tb = 4 * g4 + j
                        for k in range(8):
                            S.op("pe", lambda e: e.matmul(P[:, 512 * pb + 128 * j:512 * pb + 128 * (j + 1)], lhsT=hT[:, k, 1 + 128 * tb:1 + 128 * (tb + 1)],
                                                          rhs=wB[:, k, 256 + 128 * hp:256 + 128 * (hp + 1)], start=(k == 0), stop=(k == 7)),
                                 reads=[kB, "hT"], writes=[pk(pb)])
                    S.op("act", lambda e: e.activation(out=vtok[:, 4 * g4:4 * g4 + 4, :], in_=bank(pb).rearrange("p (a d) -> p a d", a=4), func=AF.Copy),
                         reads=[pk(pb)], writes=["vtok"])
                for d in range(2):
                    wz, kz, zc0 = (wA, kA, 256 + 128 * hp) if d == 0 else (wB, kB, 128 * hp)
                    lbi = d * 2 + hp
                    r32, r64, last = (15, 31, 63) if d == 0 else (16, 32, 0)
                    for blk in range(4):
                        tsl = slice(512 * blk, 512 * (blk + 1))
                        csl = slice(8 * blk, 8 * blk + 8)
                        pb = blk % 2
                        proj_fm(wz, kz, zc0, blk, pb)
                        S.op("act", lambda e: e.activation(out=f1, in_=bank(pb), func=AF.Exp, scale=-1.0), reads=[pk(pb)], writes=["f1"])
                        S.op("act", lambda e: e.activation(out=f1, in_=f1, func=AF.Ln, bias=one_t[:, 0:1], scale=1.0), reads=["f1", "one"], writes=["f1"])
                        S.op("act", lambda e: e.activation(out=f1, in_=f1, func=AF.Exp, scale=-1.0), reads=["f1"], writes=["f1"])
                        S.op("dve", lambda e: e.tensor_scalar(out=f1, in0=f1, scalar1=oml[:, lbi, l:l + 1], scalar2=lbv[:, lbi, l:l + 1],
                                                              op0=ALU.mult, op1=ALU.add), reads=["f1", "oml", "lbv"], writes=["f1"])
                        S.op("dve", lambda e: e.tensor_scalar_max(out=f1, in0=f1, scalar1=1e-6), reads=["f1"], writes=["f1"])
                        S.op("dve", lambda e: e.tensor_scalar(out=kk, in0=f1, scalar1=-1.0, scalar2=1.0, op0=ALU.mult, op1=ALU.add),
                             reads=["f1"], writes=["kk"])
                        S.op("act", lambda e: e.activation(out=lf, in_=f1, func=AF.Ln), reads=["f1"], writes=["lf"])
                        S.op("dve", lambda e: e.tensor_tensor_scan(out=bb, data0=rst[:], data1=lf, initial=0.0, op0=ALU.mult, op1=ALU.add),
                             reads=["lf", "rst"], writes=["bb"])
                        if d == 0:
                            cur, curk = bb, "bb"
                        else:
                            S.op("dve", lambda e: e.tensor_tensor(out=cc, in0=lf, in1=bb, op=ALU.subtract), reads=["lf", "bb"], writes=["cc"])
                            S.op("dve", lambda e: e.tensor_tensor(out=v64(cc), in0=v64(cc), in1=v64(bb)[:, :, 63:64].to_broadcast([128, 8, 64]), op=ALU.add),
                                 reads=["cc", "bb"], writes=["cc"])
                            cur, curk = cc, "cc"
                        c64 = v64(cur)
                        c32 = v32(cur)
                        S.op("act", lambda e: e.activation(out=ex, in_=cur, func=AF.Exp), reads=[curk], writes=["ex"])
                        S.op("dve", lambda e: e.tensor_tensor(out=qh[:, tsl], in0=qb_[:, tsl], in1=ex, op=ALU.mult), reads=["qb", "ex"], writes=["qh"])
                        S.op("dve", lambda e: e.tensor_tensor(out=v32(bm), in0=c32, in1=c32[:, :, r32:r32 + 1].to_broadcast([128, 16, 32]), op=ALU.subtract),
                             reads=[curk], writes=["bm"])
                        S.op("dve", lambda e: e.tensor_scalar(out=bm, in0=bm, scalar1=40.0, scalar2=-40.0, op0=ALU.min, op1=ALU.max),
                             reads=["bm"], writes=["bm"])
                        S.op("act", lambda e: e.activation(out=ex, in_=bm, func=AF.Exp), reads=["bm"], writes=["ex"])
                        S.op("dve", lambda e: e.tensor_tensor(out=q1[:, tsl], in0=qb_[:, tsl], in1=ex, op=ALU.mult), reads=["qb", "ex"], writes=["q1"])
                        S.op("act", lambda e: e.activation(out=ex, in_=bm, func=AF.Exp, scale=-1.0), reads=["bm"], writes=["ex"])
                        S.op("dve", lambda e: e.tensor_tensor(out=k1[:, tsl], in0=kk, in1=ex, op=ALU.mult), reads=["kk", "ex"], writes=["k1"])
                        S.op("dve", lambda e: e.tensor_tensor(out=v64(bm), in0=c64, in1=c64[:, :, r64:r64 + 1].to_broadcast([128, 8, 64]), op=ALU.subtract),
                             reads=[curk], writes=["bm"])
                        S.op("dve", lambda e: e.tensor_scalar_min(out=f1, in0=bm, scalar1=0.0), reads=["bm"], writes=["f1"])
                        S.op("act", lambda e: e.activation(out=ex, in_=f1, func=AF.Exp), reads=["f1"], writes=["ex"])
                        S.op("dve", lambda e: e.tensor_tensor(out=q2[:, tsl], in0=qb_[:, tsl], in1=ex, op=ALU.mult), reads=["qb", "ex"], writes=["q2"])
                        S.op("dve", lambda e: e.tensor_scalar_max(out=f1, in0=bm, scalar1=0.0), reads=["bm"], writes=["f1"])
                        S.op("act", lambda e: e.activation(out=ex, in_=f1, func=AF.Exp, scale=-1.0), reads=["f1"], writes=["ex"])
                        S.op("dve", lambda e: e.tensor_tensor(out=k2[:, tsl], in0=kk, in1=ex, op=ALU.mult), reads=["kk", "ex"], writes=["k2"])
                        S.op("dve", lambda e: e.tensor_tensor(out=v64(bm), in0=c64[:, :, last:last + 1].to_broadcast([128, 8, 64]), in1=c64, op=ALU.subtract),
                             reads=[curk], writes=["bm"])
                        S.op("act", lambda e: e.activation(out=ex, in_=bm, func=AF.Exp), reads=["bm"], writes=["ex"])
                        S.op("dve", lambda e: e.tensor_tensor(out=khT[:, tsl], in0=kk, in1=ex, op=ALU.mult), reads=["kk", "ex"], writes=["khT"])
                        S.op("act", lambda e: e.activation(out=Dd[:, csl], in_=c64[:, :, last], func=AF.Exp), reads=[curk], writes=["Dd"])
                    for g in range(2):
                        pb = 4 + (g % 2)
                        bkb = bank(pb).bitcast(BF16)
                        for j in range(8):
                            tb = 8 * g + j
                            S.op("pe", lambda e: e.transpose(bkb[:, 128 * j:128 * (j + 1)], khT[:, 128 * tb:128 * (tb + 1)], ident[:]),
                                 reads=["khT", "ident"], writes=[pk(pb)])
                        S.op("act", lambda e: e.activation(out=khtok[:, 8 * g:8 * g + 8, :], in_=bkb.rearrange("p (a d) -> p a d", a=8), func=AF.Copy),
                             reads=[pk(pb)], writes=["khtok"])
                    order = list(range(32)) if d == 0 else list(range(31, -1, -1))
                    for idx, n in enumerate(order):
                        ub = 2 + (idx // 8) % 2
                        ucol = 512 * ub + 64 * (idx % 8)
                        tb, hf = n // 2, n % 2
                        for hh in range(2):
                            S.op("pe", lambda e: e.matmul(P[64 * hh:64 * hh + 64, ucol:ucol + 64], lhsT=khtok[64 * hf:64 * hf + 64, tb, 64 * hh:64 * hh + 64],
                                                          rhs=vtok[64 * hf:64 * hf + 64, tb, 64 * hh:64 * hh + 64], start=True, stop=True),
                                 reads=["khtok", "vtok"], writes=[pk(ub)])
                        cs_, ps_ = S32[idx % 2], S32[(idx + 1) % 2]
                        if idx == 0:
                            S.op("dve", lambda e: e.tensor_copy(out=cs_, in_=P[:, ucol:ucol + 64]), reads=[pk(ub)], writes=[("S32", idx % 2)])
                        else:
                            S.op("dve", lambda e: e.scalar_tensor_tensor(out=cs_, in0=ps_, scalar=Dd[:, n:n + 1], in1=P[:, ucol:ucol + 64],
                                                                         op0=ALU.mult, op1=ALU.add),
                                 reads=[pk(ub), ("S32", (idx + 1) % 2), "Dd"], writes=[("S32", idx % 2)])
                        nxt = n + 1 if d == 0 else n - 1
                        if 0 <= nxt < 32:
                            S.op("act", lambda e: e.activation(out=Sbf[:, nxt, :], in_=cs_, func=AF.Copy),
                                 reads=[("S32", idx % 2)], writes=["Sbf"])
                    for q4 in range(4):
                        ob = 4 + (q4 % 2)
                        for j in range(4):
                            J = 4 * q4 + j
                            for hh in range(2):
                                po = 64 * hh
                                ai = si % 2
                                si += 1
                                for m, (ka, qa, kkey, qkey) in enumerate(((k1, q1, "k1", "q1"), (k2, q2, "k2", "q2"))):
                                    sbk = m
                                    S.op("pe", lambda e: e.matmul(P[:, 512 * sbk:512 * sbk + 128], lhsT=ka[po:po + 64, 128 * J:128 * (J + 1)],
                                                                  rhs=qa[po:po + 64, 128 * J:128 * (J + 1)], start=True, stop=True),
                                         reads=[kkey, qkey], writes=[pk(sbk)])
                                    S.op("dve", lambda e: e.tensor_tensor(out=att[m][ai], in0=P[:, 512 * sbk:512 * sbk + 128],
                                                                          in1=bmask[:, 2 * d + m, :], op=ALU.mult),
                                         reads=[pk(sbk), "bmask"], writes=[("att", m, ai)])
                                inter = []
                                for hf in range(2):
                                    n = 2 * J + hf
                                    if (d == 0 and n >= 1) or (d == 1 and n <= 30):
                                        inter.append((n, hf))
                                c0 = 512 * ob + 128 * j
                                S.op("pe", lambda e: e.matmul(P[po:po + 64, c0:c0 + 128], lhsT=vtok[:, J, po:po + 64], rhs=att[0][ai], start=True, stop=False),
                                     reads=["vtok", ("att", 0, ai)], writes=[pk(ob)])
                                S.op("pe", lambda e: e.matmul(P[po:po + 64, c0:c0 + 128], lhsT=vtok[:, J, po:po + 64], rhs=att[1][ai], start=False,
                                                              stop=(len(inter) == 0)),
                                     reads=["vtok", ("att", 1, ai)], writes=[pk(ob)])
                                for ii, (n, hf) in enumerate(inter):
                                    S.op("pe", lambda e: e.matmul(P[po:po + 64, c0 + 64 * hf:c0 + 64 * hf + 64], lhsT=Sbf[po:po + 64, n, :],
                                                                  rhs=qh[po:po + 64, 128 * J + 64 * hf:128 * J + 64 * hf + 64],
                                                                  start=False, stop=(ii == len(inter) - 1)),
                                         reads=["Sbf", "qh"], writes=[pk(ob)])
                        tsl = slice(512 * q4, 512 * (q4 + 1))
                        if d == 0:
                            S.op("act", lambda e: e.activation(out=o32[:, tsl], in_=bank(ob), func=AF.Copy), reads=[pk(ob)], writes=["o32"])
                            continue
                        S.op("dve", lambda e: e.tensor_tensor(out=o32[:, tsl], in0=o32[:, tsl], in1=bank(ob), op=ALU.add), reads=[pk(ob), "o32"], writes=["o32"])
                        S.op("act", lambda e: e.activation(out=sqb, in_=o32[:, tsl], func=AF.Square), reads=["o32"], writes=["sqb"])
                        S.op("pe", lambda e: e.matmul(bank(2), lhsT=blk64[:], rhs=sqb, start=True, stop=True), reads=["sqb", "blk64"], writes=[pk(2)])
                        S.op("act", lambda e: e.activation(out=rsd, in_=bank(2), func=AF.Ln, bias=eps_t[:, 0:1], scale=1.0), reads=[pk(2), "eps"], writes=["f1"])
                        S.op("act", lambda e: e.activation(out=rsd, in_=rsd, func=AF.Exp, scale=-0.5), reads=["f1"], writes=["f1"])
                        S.op("dve", lambda e: e.scalar_tensor_tensor(out=on, in0=o32[:, tsl], scalar=bon[:, l:l + 1], in1=rsd, op0=ALU.mult, op1=ALU.mult),
                             reads=["o32", "bon", "f1"], writes=["lf"])
                        proj_fm(wC, kC, 128 * hp, q4, 3)
                        S.op("act", lambda e: e.activation(out=gg, in_=bank(3), func=AF.Exp, scale=-1.0), reads=[pk(3)], writes=["bb"])
                        S.op("act", lambda e: e.activation(out=gg, in_=gg, func=AF.Ln, bias=one_t[:, 0:1], scale=1.0), reads=["bb", "one"], writes=["bb"])
                        S.op("act", lambda e: e.activation(out=gg, in_=gg, func=AF.Exp, scale=-1.0), reads=["bb"], writes=["bb"])
                        S.op("dve", lambda e: e.tensor_tensor(out=gg, in0=bank(3), in1=gg, op=ALU.mult), reads=[pk(3), "bb"], writes=["bb"])
                        S.op("dve", lambda e: e.tensor_tensor(out=o_b[:, hp, tsl], in0=on, in1=gg, op=ALU.mult), reads=["lf", "bb"], writes=["o_b"])
            outproj_partial(l, 384, 2, o_b, "o_b", 16)
            S.barrier()

        def ss(start, r):
            return slice(start, start + 127 * r + 1, r)

        def mixer_c(l):
            cv = Carver("C")
            cm = [cv.get(f"cm{i}", [128, 9, 128], BF16) for i in range(2)]
            qn_f = cv.get("qn", [128, 6, T], BF16)
            kn_f = cv.get("kn", [128, 2, T], BF16)
            qn, kn = qn_f[0:64], kn_f[0:64]
            S.op("dve", lambda e: e.memset(qn_f[64:128], 0.0), writes=["qn"])
            S.op("dve", lambda e: e.memset(kn_f[64:128], 0.0), writes=["kn"])
            vx = [cv.get(f"vx{ri}", [128, 16, 128], BF16) for ri in range(3)]
            o_c = cv.get("o_c", [128, 3, T], BF16)
            acc_off = cv.off
            acc = cv.get("acc", [128, T])
            NPB = 4
            pT = [cv.get(f"pT{i}", [128, 384], BF16) for i in range(NPB)]
            pm = [cv.get(f"pm{i}", [128, 384], BF16) for i in range(NPB)]
            den = cv.get("den", [64, 512])
            off1 = cv.off
            cv.off = acc_off
            tsets = [(None, cv.get(f"sq{i}", [128, 512], BF16), cv.get(f"r{i}", [128, 512]), None, None) for i in range(2)]
            cv.off = off1
            wv, wk = wload(d_win[l, :, 1920:2432].rearrange("(c p) n -> p c n", p=128), (8, 512))
            qk_prep(l, wv, wk, 0, 3, 2, qn_f, "qn", tsets)
            qk_prep(l, wv, wk, 384, 1, 3, kn_f, "kn", tsets)
            S.barrier()
            wv2, wk2 = wload(d_win[l, :, 2432:2560].rearrange("(c p) n -> p c n", p=128), (8, 128))
            cmask_d = d_k["cmask"].rearrange("p (h a m) -> p h a m", h=6, a=9)
            it = 0
            si = 0
            pi = 0
            for kv in range(2):
                for ri, r in enumerate(BR):
                    nb = 16 // r
                    S.op("dve", lambda e: e.memset(vx[ri][:, :, 64:128], 1.0), writes=[("vx", ri)])
                    for g in range(2):
                        pb = 4 + (g % 2)
                        for j in range(8):
                            tb = 8 * g + j
                            c, b = tb // nb, tb % nb
                            st_ = 1 + c + r * 128 * b
                            for k in range(8):
                                S.op("pe", lambda e: e.matmul(P[:, 512 * pb + 64 * j:512 * pb + 64 * (j + 1)], lhsT=hT[:, k, ss(st_, r)],
                                                              rhs=wv2[:, k, 64 * kv:64 * (kv + 1)], start=(k == 0), stop=(k == 7)),
                                     reads=[wk2, "hT"], writes=[pk(pb)])
                        S.op("act", lambda e: e.activation(out=vx[ri][:, 8 * g:8 * g + 8, 0:64],
                                                           in_=bank(pb).rearrange("p (a d) -> p a d", a=8), func=AF.Copy),
                             reads=[pk(pb)], writes=[("vx", ri)])
                items = []
                for hh in range(3):
                    h = 3 * kv + hh
                    for ri, r in enumerate(BR):
                        nb = 16 // r
                        for g4 in range(4):
                            ob = 4 + (it % 2)
                            it += 1
                            for jj in range(4):
                                tbq = 4 * g4 + jj
                                c, qb = tbq // nb, tbq % nb
                                kbs = [kb for kb in (qb - 1, qb, qb + 1) if 0 <= kb < nb]
                                items.append(dict(h=h, ri=ri, r=r, nb=nb, g4=g4, jj=jj, c=c, qb=qb, kbs=kbs, ob=ob,
                                                  first_h=(ri == 0 and g4 == 0 and jj == 0), last_g=(jj == 3),
                                                  last_h=(ri == 2 and g4 == 3 and jj == 3), sbk=(0, 1, 2, 3)[si % 4], p_i=si % NPB))
                                si += 1

                def c_scores(I):
                    h, r, c, qb = I["h"], I["r"], I["c"], I["qb"]
                    if I["first_h"]:
                        S.dma("sp", "ld", lambda e: e.dma_start(out=cm[h % 2], in_=cmask_d[:, h, :, :]), writes=[("cm", h % 2)])
                    qs = c + r * 128 * qb
                    for ii, kb in enumerate(I["kbs"]):
                        ks = c + r * 128 * kb
                        S.op("pe", lambda e: e.matmul(P[:, 512 * I["sbk"] + 128 * ii:512 * I["sbk"] + 128 * (ii + 1)], lhsT=kn_f[:, kv, ss(ks, r)],
                                                      rhs=qn_f[:, h, ss(qs, r)], start=True, stop=True), reads=["kn", "qn"], writes=[pk(I["sbk"])])

                def c_rest(I):
                    h, ri, r, nb, g4, jj, c, qb, kbs, ob, sbk, p_i = (I[k] for k in ("h", "ri", "r", "nb", "g4", "jj", "c", "qb", "kbs", "ob", "sbk", "p_i"))
                    nk = len(kbs)
                    d0 = kbs[0] - qb + 1
                    S.op("act", lambda e: e.activation(out=pT[p_i][:, 0:128 * nk], in_=P[:, 512 * sbk:512 * sbk + 128 * nk], func=AF.Exp, scale=0.125),
                         reads=[pk(sbk)], writes=[("cpT", p_i)])
                    S.op("dve", lambda e: e.tensor_tensor(out=pm[p_i][:, 0:128 * nk].rearrange("p (a m) -> p a m", a=nk),
                                                          in0=pT[p_i][:, 0:128 * nk].rearrange("p (a m) -> p a m", a=nk),
                                                          in1=cm[h % 2][:, ri * 3 + d0:ri * 3 + d0 + nk, :], op=ALU.mult),
                         reads=[("cpT", p_i), ("cm", h % 2)], writes=[("cpm", p_i)])
                    for ii, kb in enumerate(kbs):
                        S.op("pe", lambda e: e.matmul(P[:, 512 * ob + 128 * jj:512 * ob + 128 * (jj + 1)], lhsT=vx[ri][:, c * nb + kb, :],
                                                      rhs=pm[p_i][:, 128 * ii:128 * (ii + 1)], start=(ii == 0), stop=(ii == nk - 1)),
                             reads=[("cpm", p_i), ("vx", ri)], writes=[pk(ob)])
                    if I["last_g"]:
                        if ri == 0:
                            S.op("act", lambda e: e.activation(out=acc[:, 512 * g4:512 * (g4 + 1)], in_=bank(ob), func=AF.Copy),
                                 reads=[pk(ob)], writes=["acc"])
                        elif r == 4:
                            av = acc.rearrange("p (i c) -> p c i", c=4)[:, g4, :]
                            S.op("dve", lambda e: e.tensor_tensor(out=av, in0=av, in1=bank(ob), op=ALU.add), reads=[pk(ob), "acc"], writes=["acc"])
                        else:
                            av = acc.rearrange("p (i c) -> p c i", c=16)[:, 4 * g4:4 * g4 + 4, :]
                            S.op("dve", lambda e: e.tensor_tensor(out=av, in0=av, in1=bank(ob).rearrange("p (a i) -> p a i", a=4), op=ALU.add),
                                 reads=[pk(ob), "acc"], writes=["acc"])
                    if I["last_h"]:
                        po = 64 * (h % 2)
                        for blk in range(4):
                            tsl = slice(512 * blk, 512 * (blk + 1))
                            S.op("act", lambda e: e.activation(out=den, in_=acc[64:128, tsl], func=AF.Ln), reads=["acc"], writes=["f_den"])
                            S.op("act", lambda e: e.activation(out=den, in_=den, func=AF.Exp, scale=-1.0), reads=["f_den"], writes=["f_den"])
                            S.op("dve", lambda e: e.tensor_tensor(out=o_c[po:po + 64, h // 2, tsl], in0=acc[0:64, tsl], in1=den, op=ALU.mult),
                                 reads=["acc", "f_den"], writes=["o_c"])
                LA = 2
                for k in range(len(items) + LA):
                    if k < len(items):
                        c_scores(items[k])
                    if k >= LA:
                        c_rest(items[k - LA])
            outproj_partial(l, 640, 3, o_c, "o_c", 16)
            S.barrier()

        def ffn(l):
            wup = d_wup[l].rearrange("(c p) (g n) -> p c g n", p=128, g=2)
            for half in range(2):
                cv = Carver("F")
                gT = cv.get("gT", [128, 22, 1024], BF16)
                cab = [[cv.get(f"c{ab}{i}", [128, 1024]) for i in range(2)] for ab in range(2)]
                sa = [cv.get(f"sa{i}", [128, 1024]) for i in range(2)]
                t0 = 1024 * half
                for jg in range(11):
                    i = wctr[0] % NSLOT
                    wctr[0] += 1
                    wv = wring[i][:, 0:4096].rearrange("p (c g n) -> p c g n", c=8, g=2)
                    wk = ("w", i)
                    for ab in range(2):
                        S.dma("pool", f"w{i}", lambda e: e.dma_start(out=wv[:, :, ab, :], in_=wup[:, :, ab, 256 * jg:256 * (jg + 1)],
                                                                   max_dma_last_dim=4096), writes=[wk])
                    for jj in range(2):
                        j = 2 * jg + jj
                        bi = j % 2
                        for ab, base in ((0, 0), (1, 1536)):
                            bks = [pk(base // 512 + q) for q in range(3)]
                            for q, (c0, n) in enumerate(((0, 512), (512, 512), (1024, 2))):
                                for k in range(8):
                                    S.op("pe", lambda e: e.matmul(P[:, base + c0:base + c0 + n], lhsT=wv[:, k, ab, 128 * jj:128 * (jj + 1)],
                                                                  rhs=hT[:, k, t0 + c0:t0 + c0 + n], start=(k == 0), stop=(k == 7)),
                                         reads=[wk, "hT"], writes=[bks[q]])
                            ch = 22 * ab + j
                            cbuf = cab[ab][bi]
                            ck_ = ("cab", ab, bi)
                            S.op("act", lambda e: e.activation(out=cbuf, in_=P[:, base + 1:base + 1025], func=AF.Identity,
                                                               scale=cw[:, l, 1, ch:ch + 1], bias=cbias[:, l, ch:ch + 1]),
                                 reads=bks + ["cw", "cbias"], writes=[ck_])
                            S.op("dve", lambda e: e.scalar_tensor_tensor(out=cbuf, in0=P[:, base:base + 1024], scalar=cw[:, l, 0, ch:ch + 1],
                                                                         in1=cbuf, op0=ALU.mult, op1=ALU.add),
                                 reads=bks + ["cw", ck_], writes=[ck_])
                            S.op("dve", lambda e: e.scalar_tensor_tensor(out=cbuf, in0=P[:, base + 2:base + 1026], scalar=cw[:, l, 2, ch:ch + 1],
                                                                         in1=cbuf, op0=ALU.mult, op1=ALU.add),
                                 reads=bks + ["cw", ck_], writes=[ck_])
                        S.op("act", lambda e: e.activation(out=sa[bi], in_=cab[0][bi], func=AF.Silu), reads=[("cab", 0, bi)], writes=[("sa", bi)])
                        S.op("dve", lambda e: e.tensor_tensor(out=gT[:, j, :], in0=sa[bi], in1=cab[1][bi], op=ALU.mult),
                             reads=[("sa", bi), ("cab", 1, bi)], writes=["gT"])
                it = 0
                for m in range(8):
                    wv, wk = wload(d_wdn[l, :, 128 * m:128 * (m + 1)].rearrange("(c p) n -> p c n", p=128), (22, 128))
                    for n2 in range(2):
                        pb = 6 + (it % 2)
                        it += 1
                        for j in range(22):
                            S.op("pe", lambda e: e.matmul(bank(pb), lhsT=wv[:, j, :], rhs=gT[:, j, 512 * n2:512 * (n2 + 1)],
                                                          start=(j == 0), stop=(j == 21)), reads=[wk, "gT"], writes=[pk(pb)])
                        tsl = slice(t0 + 512 * n2, t0 + 512 * (n2 + 1))
                        S.op("dve", lambda e: e.scalar_tensor_tensor(out=xT[:, m, tsl], in0=bank(pb), scalar=mod[:, l, 40 + m:41 + m],
                                                                     in1=xT[:, m, tsl], op0=ALU.mult, op1=ALU.add),
                             reads=[pk(pb), "mod", "xT"], writes=["xT"])
                S.barrier()

        for l in range(depth):
            norm_mod(l, 0)
            if do_a:
                mixer_a(l)
            if do_b:
                mixer_b(l)
            if do_c:
                mixer_c(l)
            if do_ffn:
                norm_mod(l, 1)
                ffn(l)

        S.dma("sp", "st", lambda e: e.dma_start(out=d_out.rearrange("(c p) t -> p c t", p=128), in_=xT[:]), reads=["xT"])
        S.emit(final_dsems=["st"])
    return nc


def _prep_shared(inp):
    f = lambda a: np.ascontiguousarray(np.asarray(a, dtype=np.float32))
    sh = {}
    sh["w_ada"] = f(inp["w_ada"])
    sh["b_ada_l"] = f(np.asarray(inp["b_ada"]).reshape(NL, 48, 128).transpose(2, 0, 1))
    sh["norm_g_l"] = f(np.asarray(inp["norm_g"]).reshape(NL, 2, 8, 128).transpose(3, 0, 1, 2))
    sh["w_in"] = f(inp["w_in"])
    hn = np.stack([np.asarray(inp[k]) for k in ("a_q_norm", "a_k_norm", "c_q_norm", "c_k_norm")], -1)
    sh["hn"] = f(np.concatenate([hn.transpose(1, 0, 2)] * 2, 0))
    bo = np.asarray(inp["b_out_norm"])
    sh["bon"] = f(np.concatenate([bo, bo], 1).T)
    bl = np.asarray(inp["b_lb"]).reshape(2, NL, 2, 128)
    sh["b_lb_l"] = f(bl.transpose(3, 0, 2, 1).reshape(128, 4, NL))
    sh["w_out"] = f(inp["w_out"])
    sh["w_up"] = f(inp["w_up"])
    sh["conv_w_l"] = f(np.asarray(inp["conv_w"]).reshape(NL, 3, 44, 128).transpose(3, 0, 1, 2))
    sh["conv_b_l"] = f(np.asarray(inp["conv_b"]).reshape(NL, 44, 128).transpose(2, 0, 1))
    sh["w_down"] = f(inp["w_down"])
    cst = _consts()
    sh["rst"] = cst.pop("rst")
    for k, v in cst.items():
        sh["k_" + k] = np.ascontiguousarray(v)
    return sh


def run(inputs, ncores=8, **bk):
    x = np.asarray(inputs["x"], dtype=np.float32)
    c = np.asarray(inputs["c"], dtype=np.float32)
    nc = build(**bk)
    sh = _prep_shared(inputs)
    in_maps = []
    for b in range(ncores):
        m = dict(sh)
        m["xT"] = np.ascontiguousarray(x[b].T)
        m["c128"] = np.ascontiguousarray(c[b].reshape(8, 128).T)
        in_maps.append(m)
    res = run_bass_kernel_spmd(nc, in_maps, core_ids=list(range(ncores)))
    if bk.get("dbg"):
        return res.results[0]["dbg"]
    return np.stack([np.ascontiguousarray(r["outT"].T) for r in res.results]).astype(np.float32)


def kernel(**inputs):
    return run(inputs, ncores=8)
```

```python
import numpy as np
import ml_dtypes
from contextlib import ExitStack
import concourse.bass as bass
import concourse.mybir as mybir
from concourse.bass_utils import run_bass_kernel_spmd

F32 = mybir.dt.float32
BF16 = mybir.dt.bfloat16
AF = mybir.ActivationFunctionType
ALU = mybir.AluOpType
AX = mybir.AxisListType

ENGS = ("pe", "act", "dve", "pool", "sp")
T = 2048
D = 1024
NL = 4
EPS = 1e-6
SLOPES = [2.0 ** (-8.0 * i / 6) for i in range(1, 7)]
BR = (1, 4, 16)


class _Rec:
    def __getattr__(self, name):
        def f(*a, **k):
            self.call = (name, a, k)
            return None
        return f


class Sched:
    def __init__(self, nc, self_sync=True):
        self.nc = nc
        self.self_sync = self_sync
        self.ops = {e: [] for e in ENGS}
        self.lastw = {}
        self.readers = {}
        self.dma_tot = {}

    def _cur(self, ev):
        if ev[0] == 'D':
            return ('D', ev[1], self.dma_tot[ev[1]])
        return ev

    def _add(self, eng, fn, reads, writes, dma=None, extra=()):
        deps = set(extra)
        for k in reads:
            ev = self.lastw.get(k)
            if ev is not None:
                deps.add(self._cur(ev))
        for k in writes:
            ev = self.lastw.get(k)
            if ev is not None:
                deps.add(self._cur(ev))
            for r in self.readers.get(k, ()):
                deps.add(self._cur(r))
        idx = len(self.ops[eng])
        if dma is not None:
            self.dma_tot[dma] = self.dma_tot.get(dma, 0) + 16
            myev = ('D', dma, None)
        else:
            myev = ('E', eng, idx)
        d2 = set()
        for d in deps:
            if d[0] == 'E' and d[1] == eng:
                if eng == 'pe' or not self.self_sync or dma is not None:
                    continue
            d2.add(d)
        rec = _Rec()
        fn(rec)
        self.ops[eng].append(dict(call=rec.call, deps=d2, dma=dma))
        for k in reads:
            self.readers.setdefault(k, []).append(myev)
        for k in writes:
            self.lastw[k] = myev
            self.readers[k] = []
        return idx

    def op(self, eng, fn, reads=(), writes=()):
        return self._add(eng, fn, list(reads), list(writes))

    def dma(self, eng, dsem, fn, reads=(), writes=()):
        return self._add(eng, fn, list(reads), list(writes), dma=dsem)

    def barrier(self, engs=("pe", "act", "dve", "sp")):
        evs = []
        for e in engs:
            for i in range(len(self.ops[e]) - 1, -1, -1):
                if self.ops[e][i]['dma'] is None:
                    evs.append(('E', e, i))
                    break
        for name in self.dma_tot:
            if not name.startswith("w"):
                evs.append(('D', name, self.dma_tot[name]))
        for e in engs:
            self._add(e, lambda eng: eng.nop(), [], [], extra=[v for v in evs if not (v[0] == 'E' and v[1] == e)])

    def emit(self, final_dsems=()):
        nc = self.nc
        need = {e: set() for e in ENGS}
        for e in ENGS:
            for o in self.ops[e]:
                for d in o['deps']:
                    if d[0] == 'E':
                        need[d[1]].add(d[2])
        LIM = 30000
        count_at = {e: {} for e in ENGS}
        n_epochs = {}
        for e in ENGS:
            c = 0
            ep = 0
            for i in range(len(self.ops[e])):
                if i in need[e]:
                    c += 1
                    if c > LIM:
                        ep += 1
                        c = 1
                    count_at[e][i] = (ep, c)
            n_epochs[e] = ep + 1
        with ExitStack() as st:
            esem = {}
            for e in ENGS:
                for ep in range(n_epochs[e]):
                    esem[(e, ep)] = st.enter_context(nc.semaphore(f"s_{e}_{ep}"))
            dsem = {}
            for name in self.dma_tot:
                dsem[name] = st.enter_context(nc.semaphore(f"d_{name}"))
            block = st.enter_context(nc.Block())
            engobj = {"pe": "tensor", "act": "scalar", "dve": "vector", "pool": "gpsimd", "sp": "sync"}

            def make(e):
                def body(eng):
                    known = {}
                    for i, o in enumerate(self.ops[e]):
                        w = {}
                        for d in o['deps']:
                            if d[0] == 'E':
                                ep, v = count_at[d[1]][d[2]]
                                kk = ('E', d[1], ep)
                                s = esem[(d[1], ep)]
                            else:
                                kk = ('D', d[1])
                                s = dsem[d[1]]
                                v = d[2]
                            if known.get(kk, 0) >= v:
                                continue
                            if kk not in w or w[kk][1] < v:
                                w[kk] = (s, v)
                        for kk, (s, v) in w.items():
                            eng.wait_ge(s, v)
                            known[kk] = v
                        cname, ca, ck = o['call']
                        ins = getattr(eng, cname)(*ca, **ck)
                        if o['dma'] is not None:
                            ins.then_inc(dsem[o['dma']], 16)
                        elif i in need[e]:
                            ep, v = count_at[e][i]
                            ins.then_inc(esem[(e, ep)], 1)
                    if e == 'sp':
                        for name in final_dsems:
                            eng.wait_ge(dsem[name], self.dma_tot[name])
                return body
            for e in ENGS:
                getattr(block, engobj[e])(make(e))


def _consts():
    bf = ml_dtypes.bfloat16
    c = {}
    c["ident"] = np.eye(128, dtype=np.float32).astype(bf)
    c["onesD"] = np.full((128, 128), 1.0 / 1024, np.float32).astype(bf)
    c["ones64"] = np.full((64, 64), 1.0 / 64, np.float32).astype(bf)
    b = np.zeros((128, 128), np.float32)
    b[:64, :64] = 1.0 / 64
    b[64:, 64:] = 1.0 / 64
    c["blk64"] = b.astype(bf)
    R = np.zeros((64, 64), np.float32)
    for d in list(range(0, 16)) + list(range(32, 48)):
        R[d + 16, d] = -1.0
    for d in list(range(16, 32)) + list(range(48, 64)):
        R[d - 16, d] = 1.0
    R2 = np.zeros((128, 128), np.float32)
    R2[:64, :64] = R
    R2[64:, 64:] = R
    c["rrot"] = R2.astype(bf)
    t = np.arange(T)
    row = (t // 64).astype(np.float64)
    col = (t % 64).astype(np.float64)
    inv = 10000.0 ** (-np.arange(0, 32, 2, dtype=np.float64) / 32)
    ang = np.zeros((64, T))
    ang[0:16] = row[None, :] * inv[:, None]
    ang[16:32] = row[None, :] * inv[:, None]
    ang[32:48] = col[None, :] * inv[:, None]
    ang[48:64] = col[None, :] * inv[:, None]
    cs1 = np.stack([np.cos(ang), np.sin(ang)], 1).astype(np.float32)
    c["cossin"] = np.concatenate([cs1, cs1], 0).astype(bf)
    s = np.arange(128)[:, None]
    q = np.arange(128)[None, :]
    same = (s // 64) == (q // 64)
    same32 = (s // 32) == (q // 32)
    s_lo = (s % 64) < 32
    q_lo = (q % 64) < 32
    c["bmask"] = np.stack([(same32 & (s <= q)), (same & s_lo & ~q_lo),
                           (same32 & (s >= q)), (same & ~s_lo & q_lo)], 1).astype(np.float32).astype(bf)
    rst = np.ones((128, 512), np.float32)
    rst[:, ::64] = 0.0
    c["rst"] = rst
    cm = np.zeros((128, 6, 3, 3, 128), np.float32)
    for h in range(6):
        for ri, r in enumerate(BR):
            for di, dl in enumerate((-1, 0, 1)):
                dd = 128 * dl + s - q
                cm[:, h, ri, di, :] = np.where(np.abs(dd) <= 64, np.exp(-SLOPES[h] * r * np.abs(dd)), 0.0)
    c["cmask"] = cm.reshape(128, 6 * 9 * 128).astype(bf)
    return c


_CONST_SHAPES = {"ident": [128, 128], "onesD": [128, 128], "ones64": [64, 64], "blk64": [128, 128], "rrot": [128, 128],
                 "cossin": [128, 2, T], "bmask": [128, 4, 128], "cmask": [128, 6 * 9 * 128]}


def build(depth=NL, do_a=True, do_b=True, do_c=True, do_ffn=True, self_sync=True, dbg=None):
    nc = bass.Bass("TRN2", target_bir_lowering=False)

    def dram(name, shape, dt=F32, kind="ExternalInput"):
        return nc.dram_tensor(name, list(shape), dt, kind=kind).ap()

    d_xT = dram("xT", [D, T])
    d_out = dram("outT", [D, T], kind="ExternalOutput")
    d_c = dram("c128", [128, 8])
    d_wada = dram("w_ada", [NL, D, 6 * D])
    d_bada = dram("b_ada_l", [128, NL, 48])
    d_ng = dram("norm_g_l", [128, NL, 2, 8])
    d_win = dram("w_in", [NL, D, 2560])
    d_hn = dram("hn", [128, NL, 4])
    d_bon = dram("bon", [128, NL])
    d_blb = dram("b_lb_l", [128, 4, NL])
    d_wout = dram("w_out", [NL, D, D])
    d_wup = dram("w_up", [NL, D, 5632])
    d_cw = dram("conv_w_l", [128, NL, 3, 44])
    d_cb = dram("conv_b_l", [128, NL, 44])
    d_wdn = dram("w_down", [NL, 2816, D])
    d_rst = dram("rst", [128, 512])
    d_k = {k: dram("k_" + k, v, BF16) for k, v in _CONST_SHAPES.items()}

    S = Sched(nc, self_sync=self_sync)
    st = ExitStack()
    with st:
        def sb(name, shape, dt=F32):
            return st.enter_context(nc.sbuf_tensor(name, list(shape), dt))

        xT = sb("xT_sb", [128, 8, T])
        hT = sb("hT_sb", [128, 8, T + 2], BF16)
        NSLOT = 2 if dbg else 3
        wring = [sb(f"wring{i}", [128, 4096], BF16) for i in range(NSLOT)]
        mod = sb("mod", [128, NL, 48])
        bada = sb("bada", [128, NL, 48])
        ng = sb("ng", [128, NL, 2, 8])
        gs = sb("gs", [128, NL, 2, 8])
        cw = sb("cw", [128, NL, 3, 44])
        cbias = sb("cbias", [128, NL, 44])
        hn = sb("hn_sb", [128, NL, 4])
        bon = sb("bon_sb", [128, NL])
        blb = sb("blb", [128, 4, NL])
        lbv = sb("lbv", [128, 4, NL])
        oml = sb("oml", [128, 4, NL])
        lbs = sb("lbs", [128, 4])
        c_sb = sb("c_sb", [128, 8])
        sc_f = sb("sc_f", [128, 8])
        sc_b = sb("sc_b", [128, 8], BF16)
        ident = sb("ident", [128, 128], BF16)
        onesD = sb("onesD", [128, 128], BF16)
        ones64 = sb("ones64", [64, 64], BF16)
        blk64 = sb("blk64", [128, 128], BF16)
        rrot = sb("rrot", [128, 128], BF16)
        bmask = sb("bmask", [128, 4, 128], BF16)
        rst = sb("rst_sb", [128, 512])
        eps_t = sb("eps_t", [128, 1])
        one_t = sb("one_t", [128, 1])
        if dbg:
            d_dbg = dram("dbg", [128, T], kind="ExternalOutput")
            dbgt = sb("dbgt", [128, T])
            S.op("dve", lambda e: e.memset(dbgt[:], 0.0), writes=["dbgt"])

        def tap(name, ap, key, parts=128, n=T):
            if dbg != name:
                return
            S.op("act", lambda e: e.activation(out=dbgt[0:parts, 0:n], in_=ap, func=AF.Copy), reads=[key, "dbgt"], writes=["dbgt"])
            S.dma("sp", "st", lambda e: e.dma_start(out=d_dbg, in_=dbgt[:]), reads=["dbgt"])
        RW = nc.sbuf_bytes_remaining // 4 - 64
        assert RW * 4 >= 70000, RW
        R = sb("R", [128, RW])
        P = st.enter_context(nc.psum_tensor("P", [128, 4096], F32))

        def bank(i):
            return P[:, 512 * i:512 * (i + 1)]

        def pk(i):
            return ("ps", i)

        class Carver:
            def __init__(self, tag):
                self.off = 0
                self.tag = tag

            def get(self, name, shape, dt=F32, parts=128):
                n = int(np.prod(shape[1:]))
                nb = n * (4 if dt == F32 else 2)
                nb4 = (nb + 3) // 4
                ap = R[0:shape[0], self.off:self.off + nb4]
                if dt == BF16:
                    ap = ap.bitcast(BF16)[:, 0:n]
                self.off += nb4
                assert self.off <= RW, (self.tag, name, self.off * 4)
                if len(shape) == 3:
                    ap = ap.rearrange("p (a b) -> p a b", a=shape[1])
                elif len(shape) == 4:
                    ap = ap.rearrange("p (a b c) -> p a b c", a=shape[1], b=shape[2])
                return ap

        wctr = [0]

        def wload(src, shape):
            i = wctr[0] % NSLOT
            wctr[0] += 1
            a, b = shape
            view = wring[i][:, 0:a * b].rearrange("p (a b) -> p a b", a=a)
            key = ("w", i)
            S.dma("pool", f"w{i}", lambda e: e.dma_start(out=view, in_=src, max_dma_last_dim=4096), writes=[key])
            return view, key

        def ld(dst, src, key):
            S.dma("sp", "ld", lambda e: e.dma_start(out=dst, in_=src), writes=[key])

        ld(xT[:], d_xT.rearrange("(c p) t -> p c t", p=128), "xT")
        for nm, dst, src in (("bada", bada, d_bada), ("ng", ng, d_ng), ("cw", cw, d_cw), ("cbias", cbias, d_cb),
                             ("hn", hn, d_hn), ("bon", bon, d_bon), ("blb", blb, d_blb), ("c", c_sb, d_c),
                             ("ident", ident, d_k["ident"]), ("onesD", onesD, d_k["onesD"]),
                             ("ones64", ones64, d_k["ones64"]), ("blk64", blk64, d_k["blk64"]),
                             ("rrot", rrot, d_k["rrot"]), ("bmask", bmask, d_k["bmask"]), ("rst", rst, d_rst)):
            ld(dst[:], src, nm)
        S.op("dve", lambda e: e.memset(eps_t[:], EPS), writes=["eps"])
        S.op("dve", lambda e: e.memset(one_t[:], 1.0), writes=["one"])
        S.op("dve", lambda e: e.memset(hT[:, :, 0:1], 0.0), writes=["hT"])
        S.op("dve", lambda e: e.memset(hT[:, :, T + 1:T + 2], 0.0), writes=["hT"])

        S.op("act", lambda e: e.activation(out=lbv[:], in_=blb[:], func=AF.Exp), reads=["blb"], writes=["lbv"])
        S.op("dve", lambda e: e.tensor_reduce(out=lbs[:], in_=lbv[:], axis=AX.X, op=ALU.add), reads=["lbv"], writes=["lbs"])
        S.op("dve", lambda e: e.reciprocal(out=lbs[:], in_=lbs[:]), reads=["lbs"], writes=["lbs"])
        for l in range(NL):
            S.op("dve", lambda e, l=l: e.tensor_tensor(out=lbv[:, :, l], in0=lbv[:, :, l], in1=lbs[:], op=ALU.mult),
                 reads=["lbv", "lbs"], writes=["lbv"])
        S.op("dve", lambda e: e.memset(lbv[:, :, 0:1], 0.0), reads=["lbv"], writes=["lbv"])
        for l in range(2, NL):
            S.op("dve", lambda e, l=l: e.tensor_tensor(out=lbv[:, :, l], in0=lbv[:, :, l], in1=lbv[:, :, l - 1], op=ALU.add),
                 reads=["lbv"], writes=["lbv"])
        S.op("dve", lambda e: e.tensor_scalar(out=oml[:], in0=lbv[:], scalar1=-1.0, scalar2=1.0, op0=ALU.mult, op1=ALU.add),
             reads=["lbv"], writes=["oml"])

        S.op("act", lambda e: e.activation(out=sc_f[:], in_=c_sb[:], func=AF.Exp, scale=-1.0), reads=["c"], writes=["scf"])
        S.op("dve", lambda e: e.tensor_scalar_add(out=sc_f[:], in0=sc_f[:], scalar1=1.0), reads=["scf"], writes=["scf"])
        S.op("dve", lambda e: e.reciprocal(out=sc_f[:], in_=sc_f[:]), reads=["scf"], writes=["scf"])
        S.op("dve", lambda e: e.tensor_tensor(out=sc_b[:], in0=sc_f[:], in1=c_sb[:], op=ALU.mult), reads=["scf", "c"], writes=["scb"])
        for l in range(depth):
            for og in range(12):
                wv, wk = wload(d_wada[l, :, 512 * og:512 * (og + 1)].rearrange("(c p) n -> p c n", p=128), (8, 512))
                for m in range(4):
                    col = og * 4 + m
                    for k in range(8):
                        S.op("pe", lambda e, wv=wv, m=m, k=k, col=col: e.matmul(
                            P[:, col:col + 1], lhsT=wv[:, k, 128 * m:128 * (m + 1)], rhs=sc_b[:, k:k + 1],
                            start=(k == 0), stop=(k == 7)), reads=[wk, "scb"], writes=[pk(0)])
            S.op("dve", lambda e, l=l: e.tensor_tensor(out=mod[:, l, :], in0=P[:, 0:48], in1=bada[:, l, :], op=ALU.add),
                 reads=[pk(0), "bada"], writes=["mod"])
            for w_, c0 in ((0, 8), (1, 32)):
                S.op("dve", lambda e, l=l, w_=w_, c0=c0: e.scalar_tensor_tensor(
                    out=gs[:, l, w_, :], in0=mod[:, l, c0:c0 + 8], scalar=1.0, in1=ng[:, l, w_, :], op0=ALU.add, op1=ALU.mult),
                    reads=["mod", "ng"], writes=["gs"])

        def norm_mod(l, w_):
            S.barrier()
            cv = Carver("norm")
            NB = 4
            sq = [cv.get(f"sq{i}", [128, 512], BF16) for i in range(NB)]
            rs = [cv.get(f"rs{i}", [128, 512]) for i in range(2)]
            tm = [cv.get(f"tm{i}", [128, 512]) for i in range(NB)]
            sh0 = 0 if w_ == 0 else 24
            cnt = 0
            for blk in range(4):
                tsl = slice(512 * blk, 512 * (blk + 1))
                pb = 6 + (blk % 2)
                for c in range(8):
                    i = (cnt + c) % NB
                    S.op("pool" if c % 2 == 0 else "dve", lambda e: e.tensor_tensor(out=sq[i], in0=xT[:, c, tsl], in1=xT[:, c, tsl], op=ALU.mult),
                         reads=["xT"], writes=[("nsq", i)])
                    S.op("pe", lambda e: e.matmul(bank(pb), lhsT=onesD[:], rhs=sq[i], start=(c == 0), stop=(c == 7)),
                         reads=[("nsq", i), "onesD"], writes=[pk(pb)])
                r = rs[blk % 2]
                S.op("act", lambda e: e.activation(out=r, in_=bank(pb), func=AF.Ln, bias=eps_t[:, 0:1], scale=1.0),
                     reads=[pk(pb), "eps"], writes=[("nrs", blk % 2)])
                S.op("act", lambda e: e.activation(out=r, in_=r, func=AF.Exp, scale=-0.5),
                     reads=[("nrs", blk % 2)], writes=[("nrs", blk % 2)])
                for c in range(8):
                    i = (cnt + c) % NB
                    S.op("dve", lambda e: e.scalar_tensor_tensor(
                        out=tm[i], in0=xT[:, c, tsl], scalar=gs[:, l, w_, c:c + 1], in1=r, op0=ALU.mult, op1=ALU.mult),
                        reads=["xT", "gs", ("nrs", blk % 2)], writes=[("ntm", i)])
                    S.op("act", lambda e: e.activation(
                        out=hT[:, c, 1 + 512 * blk:1 + 512 * (blk + 1)], in_=tm[i], func=AF.Identity,
                        bias=mod[:, l, sh0 + c:sh0 + c + 1], scale=1.0),
                        reads=[("ntm", i), "mod"], writes=["hT"])
                cnt += 8
            S.barrier()
            tap("mod", mod[:, l, :], "mod", n=48)
            tap("hT0", hT[:, 0, 1:T + 1], "hT")
            tap("hT7", hT[:, 7, 1:T + 1], "hT")

        def proj_fm(wv, wk, c0, blk, pb):
            for k in range(8):
                S.op("pe", lambda e, k=k: e.matmul(bank(pb), lhsT=wv[:, k, c0:c0 + 128],
                                                   rhs=hT[:, k, 1 + 512 * blk:1 + 512 * (blk + 1)],
                                                   start=(k == 0), stop=(k == 7)),
                     reads=[wk, "hT"], writes=[pk(pb)])

        def outproj_partial(l, row0, nk, src, srckey, g_col0):
            wv, wk = wload(d_wout[l, row0:row0 + 128 * nk, :].rearrange("(c p) n -> p c n", p=128), (nk, 1024))
            i = 0
            for m in range(8):
                for blk in range(4):
                    pb = 6 + (i % 2)
                    i += 1
                    tsl = slice(512 * blk, 512 * (blk + 1))
                    for k in range(nk):
                        S.op("pe", lambda e, k=k, m=m, pb=pb, tsl=tsl: e.matmul(
                            bank(pb), lhsT=wv[:, k, 128 * m:128 * (m + 1)], rhs=src[:, k, tsl], start=(k == 0), stop=(k == nk - 1)),
                            reads=[wk, srckey], writes=[pk(pb)])
                    S.op("dve", lambda e, m=m, pb=pb, tsl=tsl: e.scalar_tensor_tensor(
                        out=xT[:, m, tsl], in0=bank(pb), scalar=mod[:, l, g_col0 + m:g_col0 + m + 1], in1=xT[:, m, tsl],
                        op0=ALU.mult, op1=ALU.add), reads=[pk(pb), "mod", "xT"], writes=["xT"])

        prep_ctr = [0]

        def qk_prep(l, wv, wk, c0, nchunks, gcol, dst_f, dstkey, tsets, cs=None):
            for hc in range(nchunks):
                for blk in range(4):
                    tsl = slice(512 * blk, 512 * (blk + 1))
                    i = prep_ctr[0] % 2
                    prep_ctr[0] += 1
                    qg, sq, r, t1, t2 = tsets[i]
                    pb, mb, rb = i, 2 + i, 4 + i
                    proj_fm(wv, wk, c0 + 128 * hc, blk, pb)
                    S.op("act", lambda e: e.activation(out=sq, in_=bank(pb), func=AF.Square), reads=[pk(pb)], writes=[("p_sq", i)])
                    S.op("pe", lambda e: e.matmul(bank(mb), lhsT=blk64[:], rhs=sq, start=True, stop=True),
                         reads=[("p_sq", i), "blk64"], writes=[pk(mb)])
                    if cs is not None:
                        S.op("act", lambda e: e.activation(out=qg, in_=bank(pb), func=AF.Identity, scale=hn[:, l, gcol:gcol + 1]),
                             reads=[pk(pb), "hn"], writes=[("p_qg", i)])
                        S.op("pe", lambda e: e.matmul(bank(rb), lhsT=rrot[:], rhs=qg, start=True, stop=True),
                             reads=[("p_qg", i), "rrot"], writes=[pk(rb)])
                    S.op("act", lambda e: e.activation(out=r, in_=bank(mb), func=AF.Ln, bias=eps_t[:, 0:1], scale=1.0),
                         reads=[pk(mb), "eps"], writes=[("p_r", i)])
                    S.op("act", lambda e: e.activation(out=r, in_=r, func=AF.Exp, scale=-0.5), reads=[("p_r", i)], writes=[("p_r", i)])
                    if cs is not None:
                        S.op("dve", lambda e: e.tensor_tensor(out=t1, in0=qg, in1=cs[:, 0, tsl], op=ALU.mult),
                             reads=[("p_qg", i), "cs"], writes=[("p_t1", i)])
                        S.op("dve", lambda e: e.tensor_tensor(out=t2, in0=bank(rb), in1=cs[:, 1, tsl], op=ALU.mult),
                             reads=[pk(rb), "cs"], writes=[("p_t2", i)])
                        S.op("dve", lambda e: e.tensor_tensor(out=t1, in0=t1, in1=t2, op=ALU.add),
                             reads=[("p_t1", i), ("p_t2", i)], writes=[("p_t1", i)])
                        for hh in range(2):
                            S.op("dve", lambda e: e.tensor_tensor(out=dst_f[0:64, 2 * hc + hh, tsl], in0=t1[64 * hh:64 * hh + 64],
                                                                  in1=r[64 * hh:64 * hh + 64], op=ALU.mult),
                                 reads=[("p_t1", i), ("p_r", i)], writes=[dstkey])
                    else:
                        for hh in range(2):
                            S.op("dve", lambda e: e.scalar_tensor_tensor(
                                out=dst_f[0:64, 2 * hc + hh, tsl], in0=P[64 * hh:64 * hh + 64, 512 * pb:512 * (pb + 1)],
                                scalar=hn[64 * hh:64 * hh + 64, l, gcol:gcol + 1], in1=r[64 * hh:64 * hh + 64], op0=ALU.mult, op1=ALU.mult),
                                reads=[pk(pb), "hn", ("p_r", i)], writes=[dstkey])

        def build_vext(wv, wk, c0, vext, r):
            S.op("dve", lambda e: e.memset(vext[:, :, :, 64:128], 1.0), writes=["vext"])
            for g4 in range(4):
                pb = 4 + (g4 % 2)
                for j in range(4):
                    tb = 4 * g4 + j
                    nb = 16 // r
                    c, b = tb // nb, tb % nb
                    start = 1 + c + r * 128 * b
                    for k in range(8):
                        S.op("pe", lambda e, k=k, j=j, pb=pb, start=start: e.matmul(
                            P[:, 512 * pb + 128 * j:512 * pb + 128 * (j + 1)],
                            lhsT=hT[:, k, start:start + 128 * r:r] if r > 1 else hT[:, k, start:start + 128],
                            rhs=wv[:, k, c0:c0 + 128], start=(k == 0), stop=(k == 7)),
                            reads=[wk, "hT"], writes=[pk(pb)])
                S.op("act", lambda e, g4=g4, pb=pb: e.activation(
                    out=vext[:, 4 * g4:4 * g4 + 4, :, 0:64],
                    in_=bank(pb).rearrange("p (a k d) -> p a k d", a=4, k=2), func=AF.Copy),
                    reads=[pk(pb)], writes=["vext"])

        def finalize_head(src_num, src_den, srckeys, dst_fn, dstkey, tmp):
            den, tmpo = tmp
            for blk in range(4):
                S.op("act", lambda e, blk=blk: e.activation(out=den, in_=src_den(blk), func=AF.Copy), reads=srckeys(blk), writes=["f_den"])
                S.op("dve", lambda e: e.reciprocal(out=den, in_=den), reads=["f_den"], writes=["f_den"])
                S.op("dve", lambda e, blk=blk: e.tensor_tensor(out=tmpo, in0=src_num(blk), in1=den, op=ALU.mult),
                     reads=srckeys(blk) + ["f_den"], writes=["f_tmp"])
                S.op("act", lambda e, blk=blk: e.activation(out=dst_fn(blk), in_=tmpo, func=AF.Copy), reads=["f_tmp"], writes=[dstkey])

        def mixer_a(l):
            cv = Carver("A")
            qr_f = cv.get("qr", [128, 6, T], BF16)
            kr_f = cv.get("kr", [128, 2, T], BF16)
            qr, kr = qr_f[0:64], kr_f[0:64]
            S.op("dve", lambda e: e.memset(qr_f[64:128], 0.0), writes=["qr"])
            S.op("dve", lambda e: e.memset(kr_f[64:128], 0.0), writes=["kr"])
            vext = cv.get("vext", [128, 16, 2, 128], BF16)
            o_a = cv.get("o_a", [128, 3, T], BF16)
            off0 = cv.off
            cs = cv.get("cs", [128, 2, T], BF16)
            tsets = [(cv.get(f"qg{i}", [128, 512], BF16), cv.get(f"sq{i}", [128, 512], BF16), cv.get(f"r{i}", [128, 512]),
                      cv.get(f"t1{i}", [128, 512]), cv.get(f"t2{i}", [128, 512])) for i in range(2)]
            S.dma("sp", "ld", lambda e: e.dma_start(out=cs, in_=d_k["cossin"]), writes=["cs"])
            wv, wk = wload(d_win[l, :, 0:512].rearrange("(c p) n -> p c n", p=128), (8, 512))
            qk_prep(l, wv, wk, 0, 3, 0, qr_f, "qr", tsets, cs=cs)
            qk_prep(l, wv, wk, 384, 1, 1, kr_f, "kr", tsets, cs=cs)
            wv2, wk2 = wload(d_win[l, :, 512:640].rearrange("(c p) n -> p c n", p=128), (8, 128))
            build_vext(wv2, wk2, 0, vext, 1)
            tap("qr0", qr[:, 0, :], "qr", parts=64)
            tap("qr5", qr[:, 5, :], "qr", parts=64)
            tap("kr1", kr[:, 1, :], "kr", parts=64)
            S.barrier()
            cv.off = off0
            pT = [cv.get(f"pT{i}", [128, 1024], BF16) for i in range(2)]
            ftmp = [cv.get(f"den{i}", [64, 512]) for i in range(2)]
            pending = []
            it = 0
            for h in range(6):
                kv = h // 3
                for qb in range(4):
                    qsl = slice(512 * qb, 512 * (qb + 1))
                    ob = 4 + (it % 2)
                    it += 1

                    def score(pp):
                        for u in range(2):
                            kc = 2 * pp + u
                            sbk = 2 * (pp % 2) + u
                            S.op("pe", lambda e: e.matmul(bank(sbk), lhsT=kr_f[:, kv, 128 * kc:128 * (kc + 1)], rhs=qr_f[:, h, qsl],
                                                          start=True, stop=True), reads=["kr", "qr"], writes=[pk(sbk)])

                    def pv(pp):
                        sb0 = 2 * (pp % 2)
                        pi = pp % 2
                        S.op("act", lambda e: e.activation(out=pT[pi], in_=P[:, 512 * sb0:512 * sb0 + 1024], func=AF.Exp, scale=0.125),
                             reads=[pk(sb0), pk(sb0 + 1)], writes=[("pT", pi)])
                        for u in range(2):
                            kc = 2 * pp + u
                            S.op("pe", lambda e: e.matmul(bank(ob), lhsT=vext[:, kc, kv, :], rhs=pT[pi][:, 512 * u:512 * (u + 1)],
                                                          start=(kc == 0), stop=(kc == 15)),
                                 reads=[("pT", pi), "vext"], writes=[pk(ob)])
                    def fin(ob=ob, h=h, qsl=qsl, fi=it % 2):
                        po = 64 * (h % 2)
                        S.op("act", lambda e: e.activation(out=ftmp[fi], in_=P[64:128, 512 * ob:512 * (ob + 1)], func=AF.Ln),
                             reads=[pk(ob)], writes=[("f_den", fi)])
                        S.op("act", lambda e: e.activation(out=ftmp[fi], in_=ftmp[fi], func=AF.Exp, scale=-1.0),
                             reads=[("f_den", fi)], writes=[("f_den", fi)])
                        S.op("dve", lambda e: e.tensor_tensor(out=o_a[po:po + 64, h // 2, qsl], in0=P[0:64, 512 * ob:512 * (ob + 1)], in1=ftmp[fi],
                                                              op=ALU.mult), reads=[pk(ob), ("f_den", fi)], writes=["o_a"])
                    score(0)
                    score(1)
                    for pp in range(8):
                        pv(pp)
                        if pp + 2 < 8:
                            score(pp + 2)
                        if pp == 1 and pending:
                            pending.pop()()
                    pending.append(fin)
            pending.pop()()
            tap("oa0", o_a[:, 0, :], "o_a")
            tap("oa2", o_a[:, 2, :], "o_a")
            outproj_partial(l, 0, 3, o_a, "o_a", 16)
            S.barrier()


        def mixer_b(l):
            cv = Carver("B")
            o_b = cv.get("o_b", [128, 2, T], BF16)
            qb_ = cv.get("qb", [128, T], BF16)
            qh = cv.get("qh", [128, T], BF16)
            q1 = cv.get("q1", [128, T], BF16)
            q2 = cv.get("q2", [128, T], BF16)
            k1 = cv.get("k1", [128, T], BF16)
            k2 = cv.get("k2", [128, T], BF16)
            vtok = cv.get("vtok", [128, 16, 128], BF16)
            khT = cv.get("khT", [128, T], BF16)
            khtok = cv.get("khtok", [128, 16, 128], BF16)
            Sbf = cv.get("Sbf", [128, 32, 64], BF16)
            Dd = cv.get("Dd", [128, 32])
            S32 = [cv.get(f"S32{i}", [128, 64]) for i in range(2)]
            f1 = cv.get("f1", [128, 512])
            lf = cv.get("lf", [128, 512])
            bb = cv.get("bb", [128, 512])
            cc = cv.get("cc", [128, 512])
            bm = cv.get("bm", [128, 512])
            exr = [cv.get(f"ex{i}", [128, 512]) for i in range(3)]
            exc = [0]
            kk = cv.get("kk", [128, 512], BF16)
            att = [[cv.get(f"att{m}{i}", [128, 128], BF16) for i in range(3)] for m in range(2)]
            o32 = cv.get("o32", [128, T], BF16)
            sqb = cv.get("sqb", [128, 512], BF16)
            rsd, on, gg = f1, lf, bb
            wA, kA = wload(d_win[l, :, 640:1152].rearrange("(c p) n -> p c n", p=128), (8, 512))
            wB, kB = wload(d_win[l, :, 1152:1664].rearrange("(c p) n -> p c n", p=128), (8, 512))
            wC, kC = wload(d_win[l, :, 1664:1920].rearrange("(c p) n -> p c n", p=128), (8, 256))
            v64 = lambda ap: ap.rearrange("p (n i) -> p n i", i=64)
            v32 = lambda ap: ap.rearrange("p (n i) -> p n i", i=32)
            si = 0
            for hp in range(2):
                for blk in range(4):
                    pb = blk % 2
                    proj_fm(wA, kA, 128 * hp, blk, pb)
                    S.op("act", lambda e: e.activation(out=qb_[:, 512 * blk:512 * (blk + 1)], in_=bank(pb), func=AF.Copy),
                         reads=[pk(pb)], writes=["qb"])
                for g4 in range(4):
                    pb = 4 + (g4 % 2)
                    for j in range(4):
                        tb = 4 * g4 + j
                        for k in range(8):
                            S.op("pe", lambda e: e.matmul(P[:, 512 * pb + 128 * j:512 * pb + 128 * (j + 1)], lhsT=hT[:, k, 1 + 128 * tb:1 + 128 * (tb + 1)],
                                                          rhs=wB[:, k, 256 + 128 * hp:256 + 128 * (hp + 1)], start=(k == 0), stop=(k == 7)),
                                 reads=[kB, "hT"], writes=[pk(pb)])
                    S.op("act", lambda e: e.activation(out=vtok[:, 4 * g4:4 * g4 + 4, :], in_=bank(pb).rearrange("p (a d) -> p a d", a=4), func=AF.Copy),
                         reads=[pk(pb)], writes=["vtok"])
                for d in range(2):
                    wz, kz, zc0 = (wA, kA, 256 + 128 * hp) if d == 0 else (wB, kB, 128 * hp)
                    lbi = d * 2 + hp
                    r32, r64, last = (15, 31, 63) if d == 0 else (16, 32, 0)
                    for blk in range(4):
                        tsl = slice(512 * blk, 512 * (blk + 1))
                        csl = slice(8 * blk, 8 * blk + 8)
                        pb = blk % 2
                        proj_fm(wz, kz, zc0, blk, pb)
                        S.op("act", lambda e: e.activation(out=f1, in_=bank(pb), func=AF.Exp, scale=-1.0), reads=[pk(pb)], writes=["f1"])
                        S.op("act", lambda e: e.activation(out=f1, in_=f1, func=AF.Ln, bias=one_t[:, 0:1], scale=1.0), reads=["f1", "one"], writes=["f1"])
                        S.op("act", lambda e: e.activation(out=f1, in_=f1, func=AF.Exp, scale=-1.0), reads=["f1"], writes=["f1"])
                        S.op("dve", lambda e: e.tensor_scalar(out=f1, in0=f1, scalar1=oml[:, lbi, l:l + 1], scalar2=lbv[:, lbi, l:l + 1],
                                                              op0=ALU.mult, op1=ALU.add), reads=["f1", "oml", "lbv"], writes=["f1"])
                        S.op("dve", lambda e: e.tensor_scalar_max(out=f1, in0=f1, scalar1=1e-6), reads=["f1"], writes=["f1"])
                        S.op("pool", lambda e: e.tensor_scalar(out=kk, in0=f1, scalar1=-1.0, scalar2=1.0, op0=ALU.mult, op1=ALU.add),
                             reads=["f1"], writes=["kk"])
                        S.op("act", lambda e: e.activation(out=lf, in_=f1, func=AF.Ln), reads=["f1"], writes=["lf"])
                        S.op("dve", lambda e: e.tensor_tensor_scan(out=bb, data0=rst[:], data1=lf, initial=0.0, op0=ALU.mult, op1=ALU.add),
                             reads=["lf", "rst"], writes=["bb"])
                        if d == 0:
                            cur, curk = bb, "bb"
                        else:
                            S.op("dve", lambda e: e.tensor_tensor(out=cc, in0=lf, in1=bb, op=ALU.subtract), reads=["lf", "bb"], writes=["cc"])
                            S.op("dve", lambda e: e.tensor_tensor(out=v64(cc), in0=v64(cc), in1=v64(bb)[:, :, 63:64].to_broadcast([128, 8, 64]), op=ALU.add),
                                 reads=["cc", "bb"], writes=["cc"])
                            cur, curk = cc, "cc"
                        c64 = v64(cur)
                        c32 = v32(cur)
                        exi = exc[0] % 3
                        exc[0] += 1
                        S.op("act", lambda e: e.activation(out=exr[exi], in_=cur, func=AF.Exp), reads=[curk], writes=[("ex", exi)])
                        S.op("pool", lambda e: e.tensor_tensor(out=qh[:, tsl], in0=qb_[:, tsl], in1=exr[exi], op=ALU.mult), reads=["qb", ("ex", exi)], writes=["qh"])
                        S.op("dve", lambda e: e.tensor_tensor(out=v32(bm), in0=c32, in1=c32[:, :, r32:r32 + 1].to_broadcast([128, 16, 32]), op=ALU.subtract),
                             reads=[curk], writes=["bm"])
                        S.op("dve", lambda e: e.tensor_scalar(out=bm, in0=bm, scalar1=40.0, scalar2=-40.0, op0=ALU.min, op1=ALU.max),
                             reads=["bm"], writes=["bm"])
                        exi = exc[0] % 3
                        exc[0] += 1
                        S.op("act", lambda e: e.activation(out=exr[exi], in_=bm, func=AF.Exp), reads=["bm"], writes=[("ex", exi)])
                        S.op("pool", lambda e: e.tensor_tensor(out=q1[:, tsl], in0=qb_[:, tsl], in1=exr[exi], op=ALU.mult), reads=["qb", ("ex", exi)], writes=["q1"])
                        exi = exc[0] % 3
                        exc[0] += 1
                        S.op("act", lambda e: e.activation(out=exr[exi], in_=bm, func=AF.Exp, scale=-1.0), reads=["bm"], writes=[("ex", exi)])
                        S.op("pool", lambda e: e.tensor_tensor(out=k1[:, tsl], in0=kk, in1=exr[exi], op=ALU.mult), reads=["kk", ("ex", exi)], writes=["k1"])
                        S.op("dve", lambda e: e.tensor_tensor(out=v64(bm), in0=c64, in1=c64[:, :, r64:r64 + 1].to_broadcast([128, 8, 64]), op=ALU.subtract),
                             reads=[curk], writes=["bm"])
                        S.op("dve", lambda e: e.tensor_scalar_min(out=f1, in0=bm, scalar1=0.0), reads=["bm"], writes=["f1"])
                        exi = exc[0] % 3
                        exc[0] += 1
                        S.op("act", lambda e: e.activation(out=exr[exi], in_=f1, func=AF.Exp), reads=["f1"], writes=[("ex", exi)])
                        S.op("pool", lambda e: e.tensor_tensor(out=q2[:, tsl], in0=qb_[:, tsl], in1=exr[exi], op=ALU.mult), reads=["qb", ("ex", exi)], writes=["q2"])
                        S.op("dve", lambda e: e.tensor_scalar_max(out=f1, in0=bm, scalar1=0.0), reads=["bm"], writes=["f1"])
                        exi = exc[0] % 3
                        exc[0] += 1
                        S.op("act", lambda e: e.activation(out=exr[exi], in_=f1, func=AF.Exp, scale=-1.0), reads=["f1"], writes=[("ex", exi)])
                        S.op("pool", lambda e: e.tensor_tensor(out=k2[:, tsl], in0=kk, in1=exr[exi], op=ALU.mult), reads=["kk", ("ex", exi)], writes=["k2"])
                        S.op("dve", lambda e: e.tensor_tensor(out=v64(bm), in0=c64[:, :, last:last + 1].to_broadcast([128, 8, 64]), in1=c64, op=ALU.subtract),
                             reads=[curk], writes=["bm"])
                        exi = exc[0] % 3
                        exc[0] += 1
                        S.op("act", lambda e: e.activation(out=exr[exi], in_=bm, func=AF.Exp), reads=["bm"], writes=[("ex", exi)])
                        S.op("pool", lambda e: e.tensor_tensor(out=khT[:, tsl], in0=kk, in1=exr[exi], op=ALU.mult), reads=["kk", ("ex", exi)], writes=["khT"])
                        S.op("act", lambda e: e.activation(out=Dd[:, csl], in_=c64[:, :, last], func=AF.Exp), reads=[curk], writes=["Dd"])
                    for g in range(2):
                        pb = 4 + (g % 2)
                        bkb = bank(pb).bitcast(BF16)
                        for j in range(8):
                            tb = 8 * g + j
                            S.op("pe", lambda e: e.transpose(bkb[:, 128 * j:128 * (j + 1)], khT[:, 128 * tb:128 * (tb + 1)], ident[:]),
                                 reads=["khT", "ident"], writes=[pk(pb)])
                        S.op("act", lambda e: e.activation(out=khtok[:, 8 * g:8 * g + 8, :], in_=bkb.rearrange("p (a d) -> p a d", a=8), func=AF.Copy),
                             reads=[pk(pb)], writes=["khtok"])
                    order = list(range(32)) if d == 0 else list(range(31, -1, -1))
                    for idx, n in enumerate(order):
                        ub = 2 + (idx // 8) % 2
                        ucol = 512 * ub + 64 * (idx % 8)
                        tb, hf = n // 2, n % 2
                        for hh in range(2):
                            S.op("pe", lambda e: e.matmul(P[64 * hh:64 * hh + 64, ucol:ucol + 64], lhsT=khtok[64 * hf:64 * hf + 64, tb, 64 * hh:64 * hh + 64],
                                                          rhs=vtok[64 * hf:64 * hf + 64, tb, 64 * hh:64 * hh + 64], start=True, stop=True),
                                 reads=["khtok", "vtok"], writes=[pk(ub)])
                        cs_, ps_ = S32[idx % 2], S32[(idx + 1) % 2]
                        if idx == 0:
                            S.op("dve", lambda e: e.tensor_copy(out=cs_, in_=P[:, ucol:ucol + 64]), reads=[pk(ub)], writes=[("S32", idx % 2)])
                        else:
                            S.op("dve", lambda e: e.scalar_tensor_tensor(out=cs_, in0=ps_, scalar=Dd[:, n:n + 1], in1=P[:, ucol:ucol + 64],
                                                                         op0=ALU.mult, op1=ALU.add),
                                 reads=[pk(ub), ("S32", (idx + 1) % 2), "Dd"], writes=[("S32", idx % 2)])
                        nxt = n + 1 if d == 0 else n - 1
                        if 0 <= nxt < 32:
                            S.op("act", lambda e: e.activation(out=Sbf[:, nxt, :], in_=cs_, func=AF.Copy),
                                 reads=[("S32", idx % 2)], writes=["Sbf"])
                    items = [(q4, j, hh) for q4 in range(4) for j in range(4) for hh in range(2)]

                    def b_scores(k):
                        q4, j, hh = items[k]
                        J = 4 * q4 + j
                        po = 64 * hh
                        ai = k % 3
                        for m, (ka, qa, kkey, qkey) in enumerate(((k1, q1, "k1", "q1"), (k2, q2, "k2", "q2"))):
                            sbk = 2 * (k % 2) + m if False else (0, 1, 6, 7)[2 * (k % 2) + m]
                            S.op("pe", lambda e: e.matmul(P[:, 512 * sbk:512 * sbk + 128], lhsT=ka[po:po + 64, 128 * J:128 * (J + 1)],
                                                          rhs=qa[po:po + 64, 128 * J:128 * (J + 1)], start=True, stop=True),
                                 reads=[kkey, qkey], writes=[pk(sbk)])
                            S.op("dve", lambda e: e.tensor_tensor(out=att[m][ai], in0=P[:, 512 * sbk:512 * sbk + 128],
                                                                  in1=bmask[:, 2 * d + m, :], op=ALU.mult),
                                 reads=[pk(sbk), "bmask"], writes=[("att", m, ai)])

                    def b_rest(k):
                        q4, j, hh = items[k]
                        J = 4 * q4 + j
                        po = 64 * hh
                        ai = k % 3
                        ob = 4 + (q4 % 2)
                        inter = []
                        for hf in range(2):
                            n = 2 * J + hf
                            if (d == 0 and n >= 1) or (d == 1 and n <= 30):
                                inter.append((n, hf))
                        c0 = 512 * ob + 128 * j
                        S.op("pe", lambda e: e.matmul(P[po:po + 64, c0:c0 + 128], lhsT=vtok[:, J, po:po + 64], rhs=att[0][ai], start=True, stop=False),
                             reads=["vtok", ("att", 0, ai)], writes=[pk(ob)])
                        S.op("pe", lambda e: e.matmul(P[po:po + 64, c0:c0 + 128], lhsT=vtok[:, J, po:po + 64], rhs=att[1][ai], start=False,
                                                      stop=(len(inter) == 0)),
                             reads=["vtok", ("att", 1, ai)], writes=[pk(ob)])
                        for ii, (n, hf) in enumerate(inter):
                            S.op("pe", lambda e: e.matmul(P[po:po + 64, c0 + 64 * hf:c0 + 64 * hf + 64], lhsT=Sbf[po:po + 64, n, :],
                                                          rhs=qh[po:po + 64, 128 * J + 64 * hf:128 * J + 64 * hf + 64],
                                                          start=False, stop=(ii == len(inter) - 1)),
                                 reads=["Sbf", "qh"], writes=[pk(ob)])
                        if not (j == 3 and hh == 1):
                            return
                        tsl = slice(512 * q4, 512 * (q4 + 1))
                        if d == 0:
                            S.op("act", lambda e: e.activation(out=o32[:, tsl], in_=bank(ob), func=AF.Copy), reads=[pk(ob)], writes=["o32"])
                            return
                        S.op("dve", lambda e: e.tensor_tensor(out=on, in0=o32[:, tsl], in1=bank(ob), op=ALU.add), reads=[pk(ob), "o32"], writes=["lf"])
                        S.op("act", lambda e: e.activation(out=sqb, in_=on, func=AF.Square), reads=["lf"], writes=["sqb"])
                        S.op("pe", lambda e: e.matmul(bank(2), lhsT=blk64[:], rhs=sqb, start=True, stop=True), reads=["sqb", "blk64"], writes=[pk(2)])
                        S.op("act", lambda e: e.activation(out=rsd, in_=bank(2), func=AF.Ln, bias=eps_t[:, 0:1], scale=1.0), reads=[pk(2), "eps"], writes=["f1"])
                        S.op("act", lambda e: e.activation(out=rsd, in_=rsd, func=AF.Exp, scale=-0.5), reads=["f1"], writes=["f1"])
                        S.op("dve", lambda e: e.scalar_tensor_tensor(out=on, in0=on, scalar=bon[:, l:l + 1], in1=rsd, op0=ALU.mult, op1=ALU.mult),
                             reads=["lf", "bon", "f1"], writes=["lf"])
                        proj_fm(wC, kC, 128 * hp, q4, 3)
                        S.op("act", lambda e: e.activation(out=gg, in_=bank(3), func=AF.Exp, scale=-1.0), reads=[pk(3)], writes=["bb"])
                        S.op("act", lambda e: e.activation(out=gg, in_=gg, func=AF.Ln, bias=one_t[:, 0:1], scale=1.0), reads=["bb", "one"], writes=["bb"])
                        S.op("act", lambda e: e.activation(out=gg, in_=gg, func=AF.Exp, scale=-1.0), reads=["bb"], writes=["bb"])
                        S.op("dve", lambda e: e.tensor_tensor(out=gg, in0=bank(3), in1=gg, op=ALU.mult), reads=[pk(3), "bb"], writes=["bb"])
                        S.op("dve", lambda e: e.tensor_tensor(out=o_b[:, hp, tsl], in0=on, in1=gg, op=ALU.mult), reads=["lf", "bb"], writes=["o_b"])
                    LA = 1
                    for k in range(len(items) + LA):
                        if k < len(items):
                            b_scores(k)
                        if k >= LA:
                            b_rest(k - LA)
            outproj_partial(l, 384, 2, o_b, "o_b", 16)
            S.barrier()

        def ss(start, r):
            return slice(start, start + 127 * r + 1, r)

        def mixer_c(l):
            cv = Carver("C")
            cm = [cv.get(f"cm{i}", [128, 9, 128], BF16) for i in range(2)]
            qn_f = cv.get("qn", [128, 6, T], BF16)
            kn_f = cv.get("kn", [128, 2, T], BF16)
            qn, kn = qn_f[0:64], kn_f[0:64]
            S.op("dve", lambda e: e.memset(qn_f[64:128], 0.0), writes=["qn"])
            S.op("dve", lambda e: e.memset(kn_f[64:128], 0.0), writes=["kn"])
            vx = [cv.get(f"vx{ri}", [128, 16, 128], BF16) for ri in range(3)]
            o_c = cv.get("o_c", [128, 3, T], BF16)
            acc_off = cv.off
            acc = cv.get("acc", [128, T])
            NPB = 4
            pT = [cv.get(f"pT{i}", [128, 384], BF16) for i in range(NPB)]
            pm = [cv.get(f"pm{i}", [128, 384], BF16) for i in range(NPB)]
            den = cv.get("den", [64, 512])
            off1 = cv.off
            cv.off = acc_off
            tsets = [(None, cv.get(f"sq{i}", [128, 512], BF16), cv.get(f"r{i}", [128, 512]), None, None) for i in range(2)]
            cv.off = off1
            wv, wk = wload(d_win[l, :, 1920:2432].rearrange("(c p) n -> p c n", p=128), (8, 512))
            qk_prep(l, wv, wk, 0, 3, 2, qn_f, "qn", tsets)
            qk_prep(l, wv, wk, 384, 1, 3, kn_f, "kn", tsets)
            S.barrier()
            wv2, wk2 = wload(d_win[l, :, 2432:2560].rearrange("(c p) n -> p c n", p=128), (8, 128))
            cmask_d = d_k["cmask"].rearrange("p (h a m) -> p h a m", h=6, a=9)
            it = 0
            si = 0
            pi = 0
            for kv in range(2):
                for ri, r in enumerate(BR):
                    nb = 16 // r
                    S.op("dve", lambda e: e.memset(vx[ri][:, :, 64:128], 1.0), writes=[("vx", ri)])
                    for g in range(2):
                        pb = 4 + (g % 2)
                        for j in range(8):
                            tb = 8 * g + j
                            c, b = tb // nb, tb % nb
                            st_ = 1 + c + r * 128 * b
                            for k in range(8):
                                S.op("pe", lambda e: e.matmul(P[:, 512 * pb + 64 * j:512 * pb + 64 * (j + 1)], lhsT=hT[:, k, ss(st_, r)],
                                                              rhs=wv2[:, k, 64 * kv:64 * (kv + 1)], start=(k == 0), stop=(k == 7)),
                                     reads=[wk2, "hT"], writes=[pk(pb)])
                        S.op("act", lambda e: e.activation(out=vx[ri][:, 8 * g:8 * g + 8, 0:64],
                                                           in_=bank(pb).rearrange("p (a d) -> p a d", a=8), func=AF.Copy),
                             reads=[pk(pb)], writes=[("vx", ri)])
                items = []
                for hh in range(3):
                    h = 3 * kv + hh
                    for ri, r in enumerate(BR):
                        nb = 16 // r
                        for g4 in range(4):
                            ob = 4 + (it % 2)
                            it += 1
                            for jj in range(4):
                                tbq = 4 * g4 + jj
                                c, qb = tbq // nb, tbq % nb
                                kbs = [kb for kb in (qb - 1, qb, qb + 1) if 0 <= kb < nb]
                                items.append(dict(h=h, ri=ri, r=r, nb=nb, g4=g4, jj=jj, c=c, qb=qb, kbs=kbs, ob=ob,
                                                  first_h=(ri == 0 and g4 == 0 and jj == 0), last_g=(jj == 3),
                                                  last_h=(ri == 2 and g4 == 3 and jj == 3), sbk=(0, 1, 2, 3)[si % 4], p_i=si % NPB))
                                si += 1

                def c_scores(I):
                    h, r, c, qb = I["h"], I["r"], I["c"], I["qb"]
                    if I["first_h"]:
                        S.dma("sp", "ld", lambda e: e.dma_start(out=cm[h % 2], in_=cmask_d[:, h, :, :]), writes=[("cm", h % 2)])
                    qs = c + r * 128 * qb
                    for ii, kb in enumerate(I["kbs"]):
                        ks = c + r * 128 * kb
                        S.op("pe", lambda e: e.matmul(P[:, 512 * I["sbk"] + 128 * ii:512 * I["sbk"] + 128 * (ii + 1)], lhsT=kn_f[:, kv, ss(ks, r)],
                                                      rhs=qn_f[:, h, ss(qs, r)], start=True, stop=True), reads=["kn", "qn"], writes=[pk(I["sbk"])])

                def c_rest(I):
                    h, ri, r, nb, g4, jj, c, qb, kbs, ob, sbk, p_i = (I[k] for k in ("h", "ri", "r", "nb", "g4", "jj", "c", "qb", "kbs", "ob", "sbk", "p_i"))
                    nk = len(kbs)
                    d0 = kbs[0] - qb + 1
                    S.op("act", lambda e: e.activation(out=pT[p_i][:, 0:128 * nk], in_=P[:, 512 * sbk:512 * sbk + 128 * nk], func=AF.Exp, scale=0.125),
                         reads=[pk(sbk)], writes=[("cpT", p_i)])
                    S.op("dve", lambda e: e.tensor_tensor(out=pm[p_i][:, 0:128 * nk].rearrange("p (a m) -> p a m", a=nk),
                                                          in0=pT[p_i][:, 0:128 * nk].rearrange("p (a m) -> p a m", a=nk),
                                                          in1=cm[h % 2][:, ri * 3 + d0:ri * 3 + d0 + nk, :], op=ALU.mult),
                         reads=[("cpT", p_i), ("cm", h % 2)], writes=[("cpm", p_i)])
                    for ii, kb in enumerate(kbs):
                        S.op("pe", lambda e: e.matmul(P[:, 512 * ob + 128 * jj:512 * ob + 128 * (jj + 1)], lhsT=vx[ri][:, c * nb + kb, :],
                                                      rhs=pm[p_i][:, 128 * ii:128 * (ii + 1)], start=(ii == 0), stop=(ii == nk - 1)),
                             reads=[("cpm", p_i), ("vx", ri)], writes=[pk(ob)])
                    if I["last_g"]:
                        if ri == 0:
                            S.op("act", lambda e: e.activation(out=acc[:, 512 * g4:512 * (g4 + 1)], in_=bank(ob), func=AF.Copy),
                                 reads=[pk(ob)], writes=["acc"])
                        elif r == 4:
                            av = acc.rearrange("p (i c) -> p c i", c=4)[:, g4, :]
                            S.op("dve", lambda e: e.tensor_tensor(out=av, in0=av, in1=bank(ob), op=ALU.add), reads=[pk(ob), "acc"], writes=["acc"])
                        else:
                            av = acc.rearrange("p (i c) -> p c i", c=16)[:, 4 * g4:4 * g4 + 4, :]
                            S.op("dve", lambda e: e.tensor_tensor(out=av, in0=av, in1=bank(ob).rearrange("p (a i) -> p a i", a=4), op=ALU.add),
                                 reads=[pk(ob), "acc"], writes=["acc"])
                    if I["last_h"]:
                        po = 64 * (h % 2)
                        for blk in range(4):
                            tsl = slice(512 * blk, 512 * (blk + 1))
                            S.op("act", lambda e: e.activation(out=den, in_=acc[64:128, tsl], func=AF.Ln), reads=["acc"], writes=["f_den"])
                            S.op("act", lambda e: e.activation(out=den, in_=den, func=AF.Exp, scale=-1.0), reads=["f_den"], writes=["f_den"])
                            S.op("dve", lambda e: e.tensor_tensor(out=o_c[po:po + 64, h // 2, tsl], in0=acc[0:64, tsl], in1=den, op=ALU.mult),
                                 reads=["acc", "f_den"], writes=["o_c"])
                LA = 2
                for k in range(len(items) + LA):
                    if k < len(items):
                        c_scores(items[k])
                    if k >= LA:
                        c_rest(items[k - LA])
            outproj_partial(l, 640, 3, o_c, "o_c", 16)
            S.barrier()

        def ffn(l):
            wup = d_wup[l].rearrange("(c p) (g n) -> p c g n", p=128, g=2)
            for half in range(2):
                cv = Carver("F")
                gT = cv.get("gT", [128, 22, 1024], BF16)
                cab = [[cv.get(f"c{ab}{i}", [128, 1024]) for i in range(2)] for ab in range(2)]
                sa = [cv.get(f"sa{i}", [128, 1024]) for i in range(2)]
                t0 = 1024 * half
                for jg in range(11):
                    i = wctr[0] % NSLOT
                    wctr[0] += 1
                    wv = wring[i][:, 0:4096].rearrange("p (c g n) -> p c g n", c=8, g=2)
                    wk = ("w", i)
                    for ab in range(2):
                        S.dma("pool", f"w{i}", lambda e: e.dma_start(out=wv[:, :, ab, :], in_=wup[:, :, ab, 256 * jg:256 * (jg + 1)],
                                                                   max_dma_last_dim=4096), writes=[wk])
                    for jj in range(2):
                        j = 2 * jg + jj
                        bi = j % 2
                        for ab, base in ((0, 0), (1, 1536)):
                            bks = [pk(base // 512 + q) for q in range(3)]
                            for q, (c0, n) in enumerate(((0, 512), (512, 512), (1024, 2))):
                                for k in range(8):
                                    S.op("pe", lambda e: e.matmul(P[:, base + c0:base + c0 + n], lhsT=wv[:, k, ab, 128 * jj:128 * (jj + 1)],
                                                                  rhs=hT[:, k, t0 + c0:t0 + c0 + n], start=(k == 0), stop=(k == 7)),
                                         reads=[wk, "hT"], writes=[bks[q]])
                            ch = 22 * ab + j
                            cbuf = cab[ab][bi]
                            ck_ = ("cab", ab, bi)
                            S.op("act", lambda e: e.activation(out=cbuf, in_=P[:, base + 1:base + 1025], func=AF.Identity,
                                                               scale=cw[:, l, 1, ch:ch + 1], bias=cbias[:, l, ch:ch + 1]),
                                 reads=bks + ["cw", "cbias"], writes=[ck_])
                            S.op("dve", lambda e: e.scalar_tensor_tensor(out=cbuf, in0=P[:, base:base + 1024], scalar=cw[:, l, 0, ch:ch + 1],
                                                                         in1=cbuf, op0=ALU.mult, op1=ALU.add),
                                 reads=bks + ["cw", ck_], writes=[ck_])
                            S.op("dve", lambda e: e.scalar_tensor_tensor(out=cbuf, in0=P[:, base + 2:base + 1026], scalar=cw[:, l, 2, ch:ch + 1],
                                                                         in1=cbuf, op0=ALU.mult, op1=ALU.add),
                                 reads=bks + ["cw", ck_], writes=[ck_])
                        S.op("act", lambda e: e.activation(out=sa[bi], in_=cab[0][bi], func=AF.Silu), reads=[("cab", 0, bi)], writes=[("sa", bi)])
                        S.op("dve", lambda e: e.tensor_tensor(out=gT[:, j, :], in0=sa[bi], in1=cab[1][bi], op=ALU.mult),
                             reads=[("sa", bi), ("cab", 1, bi)], writes=["gT"])
                it = 0
                for m in range(8):
                    wv, wk = wload(d_wdn[l, :, 128 * m:128 * (m + 1)].rearrange("(c p) n -> p c n", p=128), (22, 128))
                    for n2 in range(2):
                        pb = 6 + (it % 2)
                        it += 1
                        for j in range(22):
                            S.op("pe", lambda e: e.matmul(bank(pb), lhsT=wv[:, j, :], rhs=gT[:, j, 512 * n2:512 * (n2 + 1)],
                                                          start=(j == 0), stop=(j == 21)), reads=[wk, "gT"], writes=[pk(pb)])
                        tsl = slice(t0 + 512 * n2, t0 + 512 * (n2 + 1))
                        S.op("dve", lambda e: e.scalar_tensor_tensor(out=xT[:, m, tsl], in0=bank(pb), scalar=mod[:, l, 40 + m:41 + m],
                                                                     in1=xT[:, m, tsl], op0=ALU.mult, op1=ALU.add),
                             reads=[pk(pb), "mod", "xT"], writes=["xT"])
                S.barrier()

        for l in range(depth):
            norm_mod(l, 0)
            if do_a:
                mixer_a(l)
            if do_b:
                mixer_b(l)
            if do_c:
                mixer_c(l)
            if do_ffn:
                norm_mod(l, 1)
                ffn(l)

        S.dma("sp", "st", lambda e: e.dma_start(out=d_out.rearrange("(c p) t -> p c t", p=128), in_=xT[:]), reads=["xT"])
        S.emit(final_dsems=["st"])
    return nc


def _prep_shared(inp):
    f = lambda a: np.ascontiguousarray(np.asarray(a, dtype=np.float32))
    sh = {}
    sh["w_ada"] = f(inp["w_ada"])
    sh["b_ada_l"] = f(np.asarray(inp["b_ada"]).reshape(NL, 48, 128).transpose(2, 0, 1))
    sh["norm_g_l"] = f(np.asarray(inp["norm_g"]).reshape(NL, 2, 8, 128).transpose(3, 0, 1, 2))
    sh["w_in"] = f(inp["w_in"])
    hn = np.stack([np.asarray(inp[k]) for k in ("a_q_norm", "a_k_norm", "c_q_norm", "c_k_norm")], -1)
    sh["hn"] = f(np.concatenate([hn.transpose(1, 0, 2)] * 2, 0))
    bo = np.asarray(inp["b_out_norm"])
    sh["bon"] = f(np.concatenate([bo, bo], 1).T)
    bl = np.asarray(inp["b_lb"]).reshape(2, NL, 2, 128)
    sh["b_lb_l"] = f(bl.transpose(3, 0, 2, 1).reshape(128, 4, NL))
    sh["w_out"] = f(inp["w_out"])
    sh["w_up"] = f(inp["w_up"])
    sh["conv_w_l"] = f(np.asarray(inp["conv_w"]).reshape(NL, 3, 44, 128).transpose(3, 0, 1, 2))
    sh["conv_b_l"] = f(np.asarray(inp["conv_b"]).reshape(NL, 44, 128).transpose(2, 0, 1))
    sh["w_down"] = f(inp["w_down"])
    cst = _consts()
    sh["rst"] = cst.pop("rst")
    for k, v in cst.items():
        sh["k_" + k] = np.ascontiguousarray(v)
    return sh


def run(inputs, ncores=8, **bk):
    x = np.asarray(inputs["x"], dtype=np.float32)
    c = np.asarray(inputs["c"], dtype=np.float32)
    nc = build(**bk)
    sh = _prep_shared(inputs)
    in_maps = []
    for b in range(ncores):
        m = dict(sh)
        m["xT"] = np.ascontiguousarray(x[b].T)
        m["c128"] = np.ascontiguousarray(c[b].reshape(8, 128).T)
        in_maps.append(m)
    res = run_bass_kernel_spmd(nc, in_maps, core_ids=list(range(ncores)))
    if bk.get("dbg"):
        return res.results[0]["dbg"]
    return np.stack([np.ascontiguousarray(r["outT"].T) for r in res.results]).astype(np.float32)


def kernel(**inputs):
    return run(inputs, ncores=8)
```

```python
import numpy as np
import ml_dtypes
from contextlib import ExitStack
import concourse.bass as bass
import concourse.mybir as mybir
from concourse.bass_utils import run_bass_kernel_spmd

F32 = mybir.dt.float32
BF16 = mybir.dt.bfloat16
AF = mybir.ActivationFunctionType
ALU = mybir.AluOpType
AX = mybir.AxisListType

ENGS = ("pe", "act", "dve", "pool", "sp")
T = 2048
D = 1024
NL = 4
EPS = 1e-6
SLOPES = [2.0 ** (-8.0 * i / 6) for i in range(1, 7)]
BR = (1, 4, 16)


class _Rec:
    def __getattr__(self, name):
        def f(*a, **k):
            self.call = (name, a, k)
            return None
        return f


class Sched:
    def __init__(self, nc, self_sync=True):
        self.nc = nc
        self.self_sync = self_sync
        self.ops = {e: [] for e in ENGS}
        self.lastw = {}
        self.readers = {}
        self.dma_tot = {}

    def _cur(self, ev):
        if ev[0] == 'D':
            return ('D', ev[1], self.dma_tot[ev[1]])
        return ev

    def _add(self, eng, fn, reads, writes, dma=None, extra=()):
        deps = set(extra)
        for k in reads:
            ev = self.lastw.get(k)
            if ev is not None:
                deps.add(self._cur(ev))
        for k in writes:
            ev = self.lastw.get(k)
            if ev is not None:
                deps.add(self._cur(ev))
            for r in self.readers.get(k, ()):
                deps.add(self._cur(r))
        idx = len(self.ops[eng])
        if dma is not None:
            self.dma_tot[dma] = self.dma_tot.get(dma, 0) + 16
            myev = ('D', dma, None)
        else:
            myev = ('E', eng, idx)
        d2 = set()
        for d in deps:
            if d[0] == 'E' and d[1] == eng:
                if eng == 'pe' or not self.self_sync or dma is not None:
                    continue
            d2.add(d)
        rec = _Rec()
        fn(rec)
        self.ops[eng].append(dict(call=rec.call, deps=d2, dma=dma))
        for k in reads:
            self.readers.setdefault(k, []).append(myev)
        for k in writes:
            self.lastw[k] = myev
            self.readers[k] = []
        return idx

    def op(self, eng, fn, reads=(), writes=()):
        return self._add(eng, fn, list(reads), list(writes))

    def dma(self, eng, dsem, fn, reads=(), writes=()):
        return self._add(eng, fn, list(reads), list(writes), dma=dsem)

    def barrier(self, engs=("pe", "act", "dve", "sp")):
        evs = []
        for e in engs:
            for i in range(len(self.ops[e]) - 1, -1, -1):
                if self.ops[e][i]['dma'] is None:
                    evs.append(('E', e, i))
                    break
        for name in self.dma_tot:
            if not name.startswith("w"):
                evs.append(('D', name, self.dma_tot[name]))
        for e in engs:
            self._add(e, lambda eng: eng.nop(), [], [], extra=[v for v in evs if not (v[0] == 'E' and v[1] == e)])

    def emit(self, final_dsems=()):
        nc = self.nc
        need = {e: set() for e in ENGS}
        for e in ENGS:
            for o in self.ops[e]:
                for d in o['deps']:
                    if d[0] == 'E':
                        need[d[1]].add(d[2])
        LIM = 30000
        count_at = {e: {} for e in ENGS}
        n_epochs = {}
        for e in ENGS:
            c = 0
            ep = 0
            for i in range(len(self.ops[e])):
                if i in need[e]:
                    c += 1
                    if c > LIM:
                        ep += 1
                        c = 1
                    count_at[e][i] = (ep, c)
            n_epochs[e] = ep + 1
        with ExitStack() as st:
            esem = {}
            for e in ENGS:
                for ep in range(n_epochs[e]):
                    esem[(e, ep)] = st.enter_context(nc.semaphore(f"s_{e}_{ep}"))
            dsem = {}
            for name in self.dma_tot:
                dsem[name] = st.enter_context(nc.semaphore(f"d_{name}"))
            block = st.enter_context(nc.Block())
            engobj = {"pe": "tensor", "act": "scalar", "dve": "vector", "pool": "gpsimd", "sp": "sync"}

            def make(e):
                def body(eng):
                    known = {}
                    for i, o in enumerate(self.ops[e]):
                        w = {}
                        for d in o['deps']:
                            if d[0] == 'E':
                                ep, v = count_at[d[1]][d[2]]
                                kk = ('E', d[1], ep)
                                s = esem[(d[1], ep)]
                            else:
                                kk = ('D', d[1])
                                s = dsem[d[1]]
                                v = d[2]
                            if known.get(kk, 0) >= v:
                                continue
                            if kk not in w or w[kk][1] < v:
                                w[kk] = (s, v)
                        for kk, (s, v) in w.items():
                            eng.wait_ge(s, v)
                            known[kk] = v
                        cname, ca, ck = o['call']
                        ins = getattr(eng, cname)(*ca, **ck)
                        if o['dma'] is not None:
                            ins.then_inc(dsem[o['dma']], 16)
                        elif i in need[e]:
                            ep, v = count_at[e][i]
                            ins.then_inc(esem[(e, ep)], 1)
                    if e == 'sp':
                        for name in final_dsems:
                            eng.wait_ge(dsem[name], self.dma_tot[name])
                return body
            for e in ENGS:
                getattr(block, engobj[e])(make(e))


def _consts():
    bf = ml_dtypes.bfloat16
    c = {}
    c["ident"] = np.eye(128, dtype=np.float32).astype(bf)
    c["onesD"] = np.full((128, 128), 1.0 / 1024, np.float32).astype(bf)
    c["ones64"] = np.full((64, 64), 1.0 / 64, np.float32).astype(bf)
    b = np.zeros((128, 128), np.float32)
    b[:64, :64] = 1.0 / 64
    b[64:, 64:] = 1.0 / 64
    c["blk64"] = b.astype(bf)
    R = np.zeros((64, 64), np.float32)
    for d in list(range(0, 16)) + list(range(32, 48)):
        R[d + 16, d] = -1.0
    for d in list(range(16, 32)) + list(range(48, 64)):
        R[d - 16, d] = 1.0
    R2 = np.zeros((128, 128), np.float32)
    R2[:64, :64] = R
    R2[64:, 64:] = R
    c["rrot"] = R2.astype(bf)
    t = np.arange(T)
    row = (t // 64).astype(np.float64)
    col = (t % 64).astype(np.float64)
    inv = 10000.0 ** (-np.arange(0, 32, 2, dtype=np.float64) / 32)
    ang = np.zeros((64, T))
    ang[0:16] = row[None, :] * inv[:, None]
    ang[16:32] = row[None, :] * inv[:, None]
    ang[32:48] = col[None, :] * inv[:, None]
    ang[48:64] = col[None, :] * inv[:, None]
    cs1 = np.stack([np.cos(ang), np.sin(ang)], 1).astype(np.float32)
    c["cossin"] = np.concatenate([cs1, cs1], 0).astype(bf)
    s = np.arange(128)[:, None]
    q = np.arange(128)[None, :]
    same = (s // 64) == (q // 64)
    same32 = (s // 32) == (q // 32)
    s_lo = (s % 64) < 32
    q_lo = (q % 64) < 32
    c["bmask"] = np.stack([(same32 & (s <= q)), (same & s_lo & ~q_lo),
                           (same32 & (s >= q)), (same & ~s_lo & q_lo)], 1).astype(np.float32).astype(bf)
    rst = np.ones((128, 512), np.float32)
    rst[:, ::64] = 0.0
    c["rst"] = rst
    cm = np.zeros((128, 6, 3, 3, 128), np.float32)
    for h in range(6):
        for ri, r in enumerate(BR):
            for di, dl in enumerate((-1, 0, 1)):
                dd = 128 * dl + s - q
                cm[:, h, ri, di, :] = np.where(np.abs(dd) <= 64, np.exp(-SLOPES[h] * r * np.abs(dd)), 0.0)
    c["cmask"] = cm.reshape(128, 6 * 9 * 128).astype(bf)
    return c


_CONST_SHAPES = {"ident": [128, 128], "onesD": [128, 128], "ones64": [64, 64], "blk64": [128, 128], "rrot": [128, 128],
                 "cossin": [128, 2, T], "bmask": [128, 4, 128], "cmask": [128, 6 * 9 * 128]}


def build(depth=NL, do_a=True, do_b=True, do_c=True, do_ffn=True, self_sync=True, dbg=None):
    nc = bass.Bass("TRN2", target_bir_lowering=False)

    def dram(name, shape, dt=F32, kind="ExternalInput"):
        return nc.dram_tensor(name, list(shape), dt, kind=kind).ap()

    d_xT = dram("xT", [D, T])
    d_out = dram("outT", [D, T], kind="ExternalOutput")
    d_c = dram("c128", [128, 8])
    d_wada = dram("w_ada", [NL, D, 6 * D])
    d_bada = dram("b_ada_l", [128, NL, 48])
    d_ng = dram("norm_g_l", [128, NL, 2, 8])
    d_win = dram("w_in", [NL, D, 2560])
    d_hn = dram("hn", [128, NL, 4])
    d_bon = dram("bon", [128, NL])
    d_blb = dram("b_lb_l", [128, 4, NL])
    d_wout = dram("w_out", [NL, D, D])
    d_wup = dram("w_up", [NL, D, 5632])
    d_cw = dram("conv_w_l", [128, NL, 3, 44])
    d_cb = dram("conv_b_l", [128, NL, 44])
    d_wdn = dram("w_down", [NL, 2816, D])
    d_rst = dram("rst", [128, 512])
    d_k = {k: dram("k_" + k, v, BF16) for k, v in _CONST_SHAPES.items()}

    S = Sched(nc, self_sync=self_sync)
    st = ExitStack()
    with st:
        def sb(name, shape, dt=F32):
            return st.enter_context(nc.sbuf_tensor(name, list(shape), dt))

        xT = sb("xT_sb", [128, 8, T])
        hT = sb("hT_sb", [128, 8, T + 2], BF16)
        NSLOT = 2 if dbg else 3
        wring = [sb(f"wring{i}", [128, 4096], BF16) for i in range(NSLOT)]
        mod = sb("mod", [128, NL, 48])
        bada = sb("bada", [128, NL, 48])
        ng = sb("ng", [128, NL, 2, 8])
        gs = sb("gs", [128, NL, 2, 8])
        cw = sb("cw", [128, NL, 3, 44])
        cbias = sb("cbias", [128, NL, 44])
        hn = sb("hn_sb", [128, NL, 4])
        bon = sb("bon_sb", [128, NL])
        blb = sb("blb", [128, 4, NL])
        lbv = sb("lbv", [128, 4, NL])
        oml = sb("oml", [128, 4, NL])
        lbs = sb("lbs", [128, 4])
        c_sb = sb("c_sb", [128, 8])
        sc_f = sb("sc_f", [128, 8])
        sc_b = sb("sc_b", [128, 8], BF16)
        ident = sb("ident", [128, 128], BF16)
        onesD = sb("onesD", [128, 128], BF16)
        ones64 = sb("ones64", [64, 64], BF16)
        blk64 = sb("blk64", [128, 128], BF16)
        rrot = sb("rrot", [128, 128], BF16)
        bmask = sb("bmask", [128, 4, 128], BF16)
        rst = sb("rst_sb", [128, 512])
        eps_t = sb("eps_t", [128, 1])
        one_t = sb("one_t", [128, 1])
        if dbg:
            d_dbg = dram("dbg", [128, T], kind="ExternalOutput")
            dbgt = sb("dbgt", [128, T])
            S.op("dve", lambda e: e.memset(dbgt[:], 0.0), writes=["dbgt"])

        def tap(name, ap, key, parts=128, n=T):
            if dbg != name:
                return
            S.op("act", lambda e: e.activation(out=dbgt[0:parts, 0:n], in_=ap, func=AF.Copy), reads=[key, "dbgt"], writes=["dbgt"])
            S.dma("sp", "st", lambda e: e.dma_start(out=d_dbg, in_=dbgt[:]), reads=["dbgt"])
        RW = nc.sbuf_bytes_remaining // 4 - 64
        assert RW * 4 >= 70000, RW
        R = sb("R", [128, RW])
        P = st.enter_context(nc.psum_tensor("P", [128, 4096], F32))

        def bank(i):
            return P[:, 512 * i:512 * (i + 1)]

        def pk(i):
            return ("ps", i)

        class Carver:
            def __init__(self, tag):
                self.off = 0
                self.tag = tag

            def get(self, name, shape, dt=F32, parts=128):
                n = int(np.prod(shape[1:]))
                nb = n * (4 if dt == F32 else 2)
                nb4 = (nb + 3) // 4
                ap = R[0:shape[0], self.off:self.off + nb4]
                if dt == BF16:
                    ap = ap.bitcast(BF16)[:, 0:n]
                self.off += nb4
                assert self.off <= RW, (self.tag, name, self.off * 4)
                if len(shape) == 3:
                    ap = ap.rearrange("p (a b) -> p a b", a=shape[1])
                elif len(shape) == 4:
                    ap = ap.rearrange("p (a b c) -> p a b c", a=shape[1], b=shape[2])
                return ap

        wctr = [0]

        def wload(src, shape):
            i = wctr[0] % NSLOT
            wctr[0] += 1
            a, b = shape
            view = wring[i][:, 0:a * b].rearrange("p (a b) -> p a b", a=a)
            key = ("w", i)
            S.dma("pool", f"w{i}", lambda e: e.dma_start(out=view, in_=src, max_dma_last_dim=4096), writes=[key])
            return view, key

        def ld(dst, src, key):
            S.dma("sp", "ld", lambda e: e.dma_start(out=dst, in_=src), writes=[key])

        ld(xT[:], d_xT.rearrange("(c p) t -> p c t", p=128), "xT")
        for nm, dst, src in (("bada", bada, d_bada), ("ng", ng, d_ng), ("cw", cw, d_cw), ("cbias", cbias, d_cb),
                             ("hn", hn, d_hn), ("bon", bon, d_bon), ("blb", blb, d_blb), ("c", c_sb, d_c),
                             ("ident", ident, d_k["ident"]), ("onesD", onesD, d_k["onesD"]),
                             ("ones64", ones64, d_k["ones64"]), ("blk64", blk64, d_k["blk64"]),
                             ("rrot", rrot, d_k["rrot"]), ("bmask", bmask, d_k["bmask"]), ("rst", rst, d_rst)):
            ld(dst[:], src, nm)
        S.op("dve", lambda e: e.memset(eps_t[:], EPS), writes=["eps"])
        S.op("dve", lambda e: e.memset(one_t[:], 1.0), writes=["one"])
        S.op("dve", lambda e: e.memset(hT[:, :, 0:1], 0.0), writes=["hT"])
        S.op("dve", lambda e: e.memset(hT[:, :, T + 1:T + 2], 0.0), writes=["hT"])

        S.op("act", lambda e: e.activation(out=lbv[:], in_=blb[:], func=AF.Exp), reads=["blb"], writes=["lbv"])
        S.op("dve", lambda e: e.tensor_reduce(out=lbs[:], in_=lbv[:], axis=AX.X, op=ALU.add), reads=["lbv"], writes=["lbs"])
        S.op("dve", lambda e: e.reciprocal(out=lbs[:], in_=lbs[:]), reads=["lbs"], writes=["lbs"])
        for l in range(NL):
            S.op("dve", lambda e, l=l: e.tensor_tensor(out=lbv[:, :, l], in0=lbv[:, :, l], in1=lbs[:], op=ALU.mult),
                 reads=["lbv", "lbs"], writes=["lbv"])
        S.op("dve", lambda e: e.memset(lbv[:, :, 0:1], 0.0), reads=["lbv"], writes=["lbv"])
        for l in range(2, NL):
            S.op("dve", lambda e, l=l: e.tensor_tensor(out=lbv[:, :, l], in0=lbv[:, :, l], in1=lbv[:, :, l - 1], op=ALU.add),
                 reads=["lbv"], writes=["lbv"])
        S.op("dve", lambda e: e.tensor_scalar(out=oml[:], in0=lbv[:], scalar1=-1.0, scalar2=1.0, op0=ALU.mult, op1=ALU.add),
             reads=["lbv"], writes=["oml"])

        S.op("act", lambda e: e.activation(out=sc_f[:], in_=c_sb[:], func=AF.Exp, scale=-1.0), reads=["c"], writes=["scf"])
        S.op("dve", lambda e: e.tensor_scalar_add(out=sc_f[:], in0=sc_f[:], scalar1=1.0), reads=["scf"], writes=["scf"])
        S.op("dve", lambda e: e.reciprocal(out=sc_f[:], in_=sc_f[:]), reads=["scf"], writes=["scf"])
        S.op("dve", lambda e: e.tensor_tensor(out=sc_b[:], in0=sc_f[:], in1=c_sb[:], op=ALU.mult), reads=["scf", "c"], writes=["scb"])
        def adaln_og(l, og, c00):
            wv, wk = wload(d_wada[l, :, 512 * og:512 * (og + 1)].rearrange("(c p) n -> p c n", p=128), (8, 512))
            for m in range(4):
                col = c00 + og * 4 + m
                for k in range(8):
                    S.op("pe", lambda e: e.matmul(P[:, col:col + 1], lhsT=wv[:, k, 128 * m:128 * (m + 1)], rhs=sc_b[:, k:k + 1],
                                                  start=(k == 0), stop=(k == 7)), reads=[wk, "scb"], writes=[pk(c00 // 512)])

        def adaln_finish(l, c00, lo=0, hi=48):
            S.op("dve", lambda e: e.tensor_tensor(out=mod[:, l, lo:hi], in0=P[:, c00 + lo:c00 + hi], in1=bada[:, l, lo:hi], op=ALU.add),
                 reads=[pk(c00 // 512), "bada"], writes=["mod"])
            if hi < 48:
                return
            for w_, c0 in ((0, 8), (1, 32)):
                S.op("dve", lambda e: e.scalar_tensor_tensor(
                    out=gs[:, l, w_, :], in0=mod[:, l, c0:c0 + 8], scalar=1.0, in1=ng[:, l, w_, :], op0=ALU.add, op1=ALU.mult),
                    reads=["mod", "ng"], writes=["gs"])

        for og in range(12):
            adaln_og(0, og, 0)
        adaln_finish(0, 0)

        def norm_mod(l, w_):
            S.barrier()
            cv = Carver("norm")
            NB = 4
            sq = [cv.get(f"sq{i}", [128, 512], BF16) for i in range(NB)]
            rs = [cv.get(f"rs{i}", [128, 512]) for i in range(2)]
            tm = [cv.get(f"tm{i}", [128, 512]) for i in range(NB)]
            sh0 = 0 if w_ == 0 else 24
            cnt = 0
            for blk in range(4):
                tsl = slice(512 * blk, 512 * (blk + 1))
                pb = 6 + (blk % 2)
                for c in range(8):
                    i = (cnt + c) % NB
                    S.op("pool" if c % 2 == 0 else "dve", lambda e: e.tensor_tensor(out=sq[i], in0=xT[:, c, tsl], in1=xT[:, c, tsl], op=ALU.mult),
                         reads=["xT"], writes=[("nsq", i)])
                    S.op("pe", lambda e: e.matmul(bank(pb), lhsT=onesD[:], rhs=sq[i], start=(c == 0), stop=(c == 7)),
                         reads=[("nsq", i), "onesD"], writes=[pk(pb)])
                r = rs[blk % 2]
                S.op("act", lambda e: e.activation(out=r, in_=bank(pb), func=AF.Ln, bias=eps_t[:, 0:1], scale=1.0),
                     reads=[pk(pb), "eps"], writes=[("nrs", blk % 2)])
                S.op("act", lambda e: e.activation(out=r, in_=r, func=AF.Exp, scale=-0.5),
                     reads=[("nrs", blk % 2)], writes=[("nrs", blk % 2)])
                for c in range(8):
                    i = (cnt + c) % NB
                    S.op("dve", lambda e: e.scalar_tensor_tensor(
                        out=tm[i], in0=xT[:, c, tsl], scalar=gs[:, l, w_, c:c + 1], in1=r, op0=ALU.mult, op1=ALU.mult),
                        reads=["xT", "gs", ("nrs", blk % 2)], writes=[("ntm", i)])
                    S.op("act", lambda e: e.activation(
                        out=hT[:, c, 1 + 512 * blk:1 + 512 * (blk + 1)], in_=tm[i], func=AF.Identity,
                        bias=mod[:, l, sh0 + c:sh0 + c + 1], scale=1.0),
                        reads=[("ntm", i), "mod"], writes=["hT"])
                cnt += 8
            S.barrier()
            tap("mod", mod[:, l, :], "mod", n=48)
            tap("hT0", hT[:, 0, 1:T + 1], "hT")
            tap("hT7", hT[:, 7, 1:T + 1], "hT")

        def proj_fm(wv, wk, c0, blk, pb):
            for k in range(8):
                S.op("pe", lambda e, k=k: e.matmul(bank(pb), lhsT=wv[:, k, c0:c0 + 128],
                                                   rhs=hT[:, k, 1 + 512 * blk:1 + 512 * (blk + 1)],
                                                   start=(k == 0), stop=(k == 7)),
                     reads=[wk, "hT"], writes=[pk(pb)])

        def outproj_partial(l, row0, nk, src, srckey, g_col0):
            wv, wk = wload(d_wout[l, row0:row0 + 128 * nk, :].rearrange("(c p) n -> p c n", p=128), (nk, 1024))
            i = 0
            for m in range(8):
                for blk in range(4):
                    pb = 6 + (i % 2)
                    i += 1
                    tsl = slice(512 * blk, 512 * (blk + 1))
                    for k in range(nk):
                        S.op("pe", lambda e, k=k, m=m, pb=pb, tsl=tsl: e.matmul(
                            bank(pb), lhsT=wv[:, k, 128 * m:128 * (m + 1)], rhs=src[:, k, tsl], start=(k == 0), stop=(k == nk - 1)),
                            reads=[wk, srckey], writes=[pk(pb)])
                    S.op("dve", lambda e, m=m, pb=pb, tsl=tsl: e.scalar_tensor_tensor(
                        out=xT[:, m, tsl], in0=bank(pb), scalar=mod[:, l, g_col0 + m:g_col0 + m + 1], in1=xT[:, m, tsl],
                        op0=ALU.mult, op1=ALU.add), reads=[pk(pb), "mod", "xT"], writes=["xT"])

        prep_ctr = [0]

        def qk_prep(l, wv, wk, c0, nchunks, gcol, dst_f, dstkey, tsets, cs=None):
            for hc in range(nchunks):
                for blk in range(4):
                    tsl = slice(512 * blk, 512 * (blk + 1))
                    i = prep_ctr[0] % 2
                    prep_ctr[0] += 1
                    qg, sq, r, t1, t2 = tsets[i]
                    pb, mb, rb = i, 2 + i, 4 + i
                    proj_fm(wv, wk, c0 + 128 * hc, blk, pb)
                    S.op("act", lambda e: e.activation(out=sq, in_=bank(pb), func=AF.Square), reads=[pk(pb)], writes=[("p_sq", i)])
                    S.op("pe", lambda e: e.matmul(bank(mb), lhsT=blk64[:], rhs=sq, start=True, stop=True),
                         reads=[("p_sq", i), "blk64"], writes=[pk(mb)])
                    if cs is not None:
                        S.op("act", lambda e: e.activation(out=qg, in_=bank(pb), func=AF.Identity, scale=hn[:, l, gcol:gcol + 1]),
                             reads=[pk(pb), "hn"], writes=[("p_qg", i)])
                        S.op("pe", lambda e: e.matmul(bank(rb), lhsT=rrot[:], rhs=qg, start=True, stop=True),
                             reads=[("p_qg", i), "rrot"], writes=[pk(rb)])
                    S.op("act", lambda e: e.activation(out=r, in_=bank(mb), func=AF.Ln, bias=eps_t[:, 0:1], scale=1.0),
                         reads=[pk(mb), "eps"], writes=[("p_r", i)])
                    S.op("act", lambda e: e.activation(out=r, in_=r, func=AF.Exp, scale=-0.5), reads=[("p_r", i)], writes=[("p_r", i)])
                    if cs is not None:
                        S.op("dve", lambda e: e.tensor_tensor(out=t1, in0=qg, in1=cs[:, 0, tsl], op=ALU.mult),
                             reads=[("p_qg", i), "cs"], writes=[("p_t1", i)])
                        S.op("dve", lambda e: e.tensor_tensor(out=t2, in0=bank(rb), in1=cs[:, 1, tsl], op=ALU.mult),
                             reads=[pk(rb), "cs"], writes=[("p_t2", i)])
                        S.op("dve", lambda e: e.tensor_tensor(out=t1, in0=t1, in1=t2, op=ALU.add),
                             reads=[("p_t1", i), ("p_t2", i)], writes=[("p_t1", i)])
                        for hh in range(2):
                            S.op("dve", lambda e: e.tensor_tensor(out=dst_f[0:64, 2 * hc + hh, tsl], in0=t1[64 * hh:64 * hh + 64],
                                                                  in1=r[64 * hh:64 * hh + 64], op=ALU.mult),
                                 reads=[("p_t1", i), ("p_r", i)], writes=[dstkey])
                    else:
                        for hh in range(2):
                            S.op("dve", lambda e: e.scalar_tensor_tensor(
                                out=dst_f[0:64, 2 * hc + hh, tsl], in0=P[64 * hh:64 * hh + 64, 512 * pb:512 * (pb + 1)],
                                scalar=hn[64 * hh:64 * hh + 64, l, gcol:gcol + 1], in1=r[64 * hh:64 * hh + 64], op0=ALU.mult, op1=ALU.mult),
                                reads=[pk(pb), "hn", ("p_r", i)], writes=[dstkey])

        def build_vext(wv, wk, c0, vext, r):
            S.op("dve", lambda e: e.memset(vext[:, :, :, 64:128], 1.0), writes=["vext"])
            for g4 in range(4):
                pb = 4 + (g4 % 2)
                for j in range(4):
                    tb = 4 * g4 + j
                    nb = 16 // r
                    c, b = tb // nb, tb % nb
                    start = 1 + c + r * 128 * b
                    for k in range(8):
                        S.op("pe", lambda e, k=k, j=j, pb=pb, start=start: e.matmul(
                            P[:, 512 * pb + 128 * j:512 * pb + 128 * (j + 1)],
                            lhsT=hT[:, k, start:start + 128 * r:r] if r > 1 else hT[:, k, start:start + 128],
                            rhs=wv[:, k, c0:c0 + 128], start=(k == 0), stop=(k == 7)),
                            reads=[wk, "hT"], writes=[pk(pb)])
                S.op("act", lambda e, g4=g4, pb=pb: e.activation(
                    out=vext[:, 4 * g4:4 * g4 + 4, :, 0:64],
                    in_=bank(pb).rearrange("p (a k d) -> p a k d", a=4, k=2), func=AF.Copy),
                    reads=[pk(pb)], writes=["vext"])

        def finalize_head(src_num, src_den, srckeys, dst_fn, dstkey, tmp):
            den, tmpo = tmp
            for blk in range(4):
                S.op("act", lambda e, blk=blk: e.activation(out=den, in_=src_den(blk), func=AF.Copy), reads=srckeys(blk), writes=["f_den"])
                S.op("dve", lambda e: e.reciprocal(out=den, in_=den), reads=["f_den"], writes=["f_den"])
                S.op("dve", lambda e, blk=blk: e.tensor_tensor(out=tmpo, in0=src_num(blk), in1=den, op=ALU.mult),
                     reads=srckeys(blk) + ["f_den"], writes=["f_tmp"])
                S.op("act", lambda e, blk=blk: e.activation(out=dst_fn(blk), in_=tmpo, func=AF.Copy), reads=["f_tmp"], writes=[dstkey])

        def mixer_a(l):
            cv = Carver("A")
            qr_f = cv.get("qr", [128, 6, T], BF16)
            kr_f = cv.get("kr", [128, 2, T], BF16)
            qr, kr = qr_f[0:64], kr_f[0:64]
            S.op("dve", lambda e: e.memset(qr_f[64:128], 0.0), writes=["qr"])
            S.op("dve", lambda e: e.memset(kr_f[64:128], 0.0), writes=["kr"])
            vext = cv.get("vext", [128, 16, 2, 128], BF16)
            o_a = cv.get("o_a", [128, 3, T], BF16)
            off0 = cv.off
            cs = cv.get("cs", [128, 2, T], BF16)
            tsets = [(cv.get(f"qg{i}", [128, 512], BF16), cv.get(f"sq{i}", [128, 512], BF16), cv.get(f"r{i}", [128, 512]),
                      cv.get(f"t1{i}", [128, 512]), cv.get(f"t2{i}", [128, 512])) for i in range(2)]
            S.dma("sp", "ld", lambda e: e.dma_start(out=cs, in_=d_k["cossin"]), writes=["cs"])
            wv, wk = wload(d_win[l, :, 0:512].rearrange("(c p) n -> p c n", p=128), (8, 512))
            qk_prep(l, wv, wk, 0, 3, 0, qr_f, "qr", tsets, cs=cs)
            qk_prep(l, wv, wk, 384, 1, 1, kr_f, "kr", tsets, cs=cs)
            wv2, wk2 = wload(d_win[l, :, 512:640].rearrange("(c p) n -> p c n", p=128), (8, 128))
            build_vext(wv2, wk2, 0, vext, 1)
            tap("qr0", qr[:, 0, :], "qr", parts=64)
            tap("qr5", qr[:, 5, :], "qr", parts=64)
            tap("kr1", kr[:, 1, :], "kr", parts=64)
            S.barrier()
            cv.off = off0
            pT = [cv.get(f"pT{i}", [128, 1024], BF16) for i in range(2)]
            ftmp = [cv.get(f"den{i}", [64, 512]) for i in range(2)]
            pending = []
            it = 0
            for h in range(6):
                kv = h // 3
                for qb in range(4):
                    qsl = slice(512 * qb, 512 * (qb + 1))
                    ob = 4 + (it % 2)
                    it += 1

                    def score(pp):
                        for u in range(2):
                            kc = 2 * pp + u
                            sbk = 2 * (pp % 2) + u
                            S.op("pe", lambda e: e.matmul(bank(sbk), lhsT=kr_f[:, kv, 128 * kc:128 * (kc + 1)], rhs=qr_f[:, h, qsl],
                                                          start=True, stop=True), reads=["kr", "qr"], writes=[pk(sbk)])

                    def pv(pp):
                        sb0 = 2 * (pp % 2)
                        pi = pp % 2
                        S.op("act", lambda e: e.activation(out=pT[pi], in_=P[:, 512 * sb0:512 * sb0 + 1024], func=AF.Exp, scale=0.125),
                             reads=[pk(sb0), pk(sb0 + 1)], writes=[("pT", pi)])
                        for u in range(2):
                            kc = 2 * pp + u
                            S.op("pe", lambda e: e.matmul(bank(ob), lhsT=vext[:, kc, kv, :], rhs=pT[pi][:, 512 * u:512 * (u + 1)],
                                                          start=(kc == 0), stop=(kc == 15)),
                                 reads=[("pT", pi), "vext"], writes=[pk(ob)])
                    def fin(ob=ob, h=h, qsl=qsl, fi=it % 2):
                        po = 64 * (h % 2)
                        S.op("act", lambda e: e.activation(out=ftmp[fi], in_=P[64:128, 512 * ob:512 * (ob + 1)], func=AF.Ln),
                             reads=[pk(ob)], writes=[("f_den", fi)])
                        S.op("act", lambda e: e.activation(out=ftmp[fi], in_=ftmp[fi], func=AF.Exp, scale=-1.0),
                             reads=[("f_den", fi)], writes=[("f_den", fi)])
                        S.op("dve", lambda e: e.tensor_tensor(out=o_a[po:po + 64, h // 2, qsl], in0=P[0:64, 512 * ob:512 * (ob + 1)], in1=ftmp[fi],
                                                              op=ALU.mult), reads=[pk(ob), ("f_den", fi)], writes=["o_a"])
                    score(0)
                    score(1)
                    for pp in range(8):
                        pv(pp)
                        if pp + 2 < 8:
                            score(pp + 2)
                        if pp == 1 and pending:
                            pending.pop()()
                    pending.append(fin)
            pending.pop()()
            tap("oa0", o_a[:, 0, :], "o_a")
            tap("oa2", o_a[:, 2, :], "o_a")
            outproj_partial(l, 0, 3, o_a, "o_a", 16)
            S.barrier()


        def mixer_b(l):
            cv = Carver("B")
            o_b = cv.get("o_b", [128, 2, T], BF16)
            qb_ = cv.get("qb", [128, T], BF16)
            qh = cv.get("qh", [128, T], BF16)
            q1 = cv.get("q1", [128, T], BF16)
            q2 = cv.get("q2", [128, T], BF16)
            k1 = cv.get("k1", [128, T], BF16)
            k2 = cv.get("k2", [128, T], BF16)
            vtok = cv.get("vtok", [128, 16, 128], BF16)
            khT = cv.get("khT", [128, T], BF16)
            khtok = cv.get("khtok", [128, 16, 128], BF16)
            Sbf = cv.get("Sbf", [128, 32, 64], BF16)
            Dd = cv.get("Dd", [128, 32])
            S32 = [cv.get(f"S32{i}", [128, 64]) for i in range(2)]
            f1 = cv.get("f1", [128, 512])
            lf = cv.get("lf", [128, 512])
            bb = cv.get("bb", [128, 512])
            cc = cv.get("cc", [128, 512])
            bm = cv.get("bm", [128, 512])
            exr = [cv.get(f"ex{i}", [128, 512]) for i in range(3)]
            exc = [0]
            kk = cv.get("kk", [128, 512], BF16)
            att = [[cv.get(f"att{m}{i}", [128, 128], BF16) for i in range(3)] for m in range(2)]
            o32 = cv.get("o32", [128, T], BF16)
            sqb = cv.get("sqb", [128, 512], BF16)
            rsd, on, gg = f1, lf, bb
            wA, kA = wload(d_win[l, :, 640:1152].rearrange("(c p) n -> p c n", p=128), (8, 512))
            wB, kB = wload(d_win[l, :, 1152:1664].rearrange("(c p) n -> p c n", p=128), (8, 512))
            wC, kC = wload(d_win[l, :, 1664:1920].rearrange("(c p) n -> p c n", p=128), (8, 256))
            v64 = lambda ap: ap.rearrange("p (n i) -> p n i", i=64)
            v32 = lambda ap: ap.rearrange("p (n i) -> p n i", i=32)
            si = 0
            for hp in range(2):
                for blk in range(4):
                    pb = blk % 2
                    proj_fm(wA, kA, 128 * hp, blk, pb)
                    S.op("act", lambda e: e.activation(out=qb_[:, 512 * blk:512 * (blk + 1)], in_=bank(pb), func=AF.Copy),
                         reads=[pk(pb)], writes=["qb"])
                for g4 in range(4):
                    pb = 4 + (g4 % 2)
                    for j in range(4):
                        tb = 4 * g4 + j
                        for k in range(8):
                            S.op("pe", lambda e: e.matmul(P[:, 512 * pb + 128 * j:512 * pb + 128 * (j + 1)], lhsT=hT[:, k, 1 + 128 * tb:1 + 128 * (tb + 1)],
                                                          rhs=wB[:, k, 256 + 128 * hp:256 + 128 * (hp + 1)], start=(k == 0), stop=(k == 7)),
                                 reads=[kB, "hT"], writes=[pk(pb)])
                    S.op("act", lambda e: e.activation(out=vtok[:, 4 * g4:4 * g4 + 4, :], in_=bank(pb).rearrange("p (a d) -> p a d", a=4), func=AF.Copy),
                         reads=[pk(pb)], writes=["vtok"])
                for d in range(2):
                    wz, kz, zc0 = (wA, kA, 256 + 128 * hp) if d == 0 else (wB, kB, 128 * hp)
                    lbi = d * 2 + hp
                    r32, r64, last = (15, 31, 63) if d == 0 else (16, 32, 0)
                    for blk in range(4):
                        tsl = slice(512 * blk, 512 * (blk + 1))
                        csl = slice(8 * blk, 8 * blk + 8)
                        pb = blk % 2
                        proj_fm(wz, kz, zc0, blk, pb)
                        S.op("act", lambda e: e.activation(out=f1, in_=bank(pb), func=AF.Exp, scale=-1.0), reads=[pk(pb)], writes=["f1"])
                        S.op("act", lambda e: e.activation(out=f1, in_=f1, func=AF.Ln, bias=one_t[:, 0:1], scale=1.0), reads=["f1", "one"], writes=["f1"])
                        S.op("act", lambda e: e.activation(out=f1, in_=f1, func=AF.Exp, scale=-1.0), reads=["f1"], writes=["f1"])
                        S.op("dve", lambda e: e.tensor_scalar(out=f1, in0=f1, scalar1=oml[:, lbi, l:l + 1], scalar2=lbv[:, lbi, l:l + 1],
                                                              op0=ALU.mult, op1=ALU.add), reads=["f1", "oml", "lbv"], writes=["f1"])
                        S.op("dve", lambda e: e.tensor_scalar_max(out=f1, in0=f1, scalar1=1e-6), reads=["f1"], writes=["f1"])
                        S.op("pool", lambda e: e.tensor_scalar(out=kk, in0=f1, scalar1=-1.0, scalar2=1.0, op0=ALU.mult, op1=ALU.add),
                             reads=["f1"], writes=["kk"])
                        S.op("act", lambda e: e.activation(out=lf, in_=f1, func=AF.Ln), reads=["f1"], writes=["lf"])
                        S.op("dve", lambda e: e.tensor_tensor_scan(out=bb, data0=rst[:], data1=lf, initial=0.0, op0=ALU.mult, op1=ALU.add),
                             reads=["lf", "rst"], writes=["bb"])
                        if d == 0:
                            cur, curk = bb, "bb"
                        else:
                            S.op("dve", lambda e: e.tensor_tensor(out=cc, in0=lf, in1=bb, op=ALU.subtract), reads=["lf", "bb"], writes=["cc"])
                            S.op("dve", lambda e: e.tensor_tensor(out=v64(cc), in0=v64(cc), in1=v64(bb)[:, :, 63:64].to_broadcast([128, 8, 64]), op=ALU.add),
                                 reads=["cc", "bb"], writes=["cc"])
                            cur, curk = cc, "cc"
                        c64 = v64(cur)
                        c32 = v32(cur)
                        exi = exc[0] % 3
                        exc[0] += 1
                        S.op("act", lambda e: e.activation(out=exr[exi], in_=cur, func=AF.Exp), reads=[curk], writes=[("ex", exi)])
                        S.op("pool", lambda e: e.tensor_tensor(out=qh[:, tsl], in0=qb_[:, tsl], in1=exr[exi], op=ALU.mult), reads=["qb", ("ex", exi)], writes=["qh"])
                        S.op("dve", lambda e: e.tensor_tensor(out=v32(bm), in0=c32, in1=c32[:, :, r32:r32 + 1].to_broadcast([128, 16, 32]), op=ALU.subtract),
                             reads=[curk], writes=["bm"])
                        S.op("dve", lambda e: e.tensor_scalar(out=bm, in0=bm, scalar1=40.0, scalar2=-40.0, op0=ALU.min, op1=ALU.max),
                             reads=["bm"], writes=["bm"])
                        exi = exc[0] % 3
                        exc[0] += 1
                        S.op("act", lambda e: e.activation(out=exr[exi], in_=bm, func=AF.Exp), reads=["bm"], writes=[("ex", exi)])
                        S.op("pool", lambda e: e.tensor_tensor(out=q1[:, tsl], in0=qb_[:, tsl], in1=exr[exi], op=ALU.mult), reads=["qb", ("ex", exi)], writes=["q1"])
                        exi = exc[0] % 3
                        exc[0] += 1
                        S.op("act", lambda e: e.activation(out=exr[exi], in_=bm, func=AF.Exp, scale=-1.0), reads=["bm"], writes=[("ex", exi)])
                        S.op("pool", lambda e: e.tensor_tensor(out=k1[:, tsl], in0=kk, in1=exr[exi], op=ALU.mult), reads=["kk", ("ex", exi)], writes=["k1"])
                        S.op("dve", lambda e: e.tensor_tensor(out=v64(bm), in0=c64, in1=c64[:, :, r64:r64 + 1].to_broadcast([128, 8, 64]), op=ALU.subtract),
                             reads=[curk], writes=["bm"])
                        exi = exc[0] % 3
                        exc[0] += 1
                        S.op("dve", lambda e: e.tensor_scalar_min(out=exr[exi], in0=bm, scalar1=0.0), reads=["bm"], writes=[("ex", exi)])
                        S.op("act", lambda e: e.activation(out=exr[exi], in_=exr[exi], func=AF.Exp), reads=[("ex", exi)], writes=[("ex", exi)])
                        S.op("pool", lambda e: e.tensor_tensor(out=q2[:, tsl], in0=qb_[:, tsl], in1=exr[exi], op=ALU.mult), reads=["qb", ("ex", exi)], writes=["q2"])
                        exi = exc[0] % 3
                        exc[0] += 1
                        S.op("dve", lambda e: e.tensor_scalar_max(out=exr[exi], in0=bm, scalar1=0.0), reads=["bm"], writes=[("ex", exi)])
                        S.op("act", lambda e: e.activation(out=exr[exi], in_=exr[exi], func=AF.Exp, scale=-1.0), reads=[("ex", exi)], writes=[("ex", exi)])
                        S.op("pool", lambda e: e.tensor_tensor(out=k2[:, tsl], in0=kk, in1=exr[exi], op=ALU.mult), reads=["kk", ("ex", exi)], writes=["k2"])
                        S.op("dve", lambda e: e.tensor_tensor(out=v64(bm), in0=c64[:, :, last:last + 1].to_broadcast([128, 8, 64]), in1=c64, op=ALU.subtract),
                             reads=[curk], writes=["bm"])
                        exi = exc[0] % 3
                        exc[0] += 1
                        S.op("act", lambda e: e.activation(out=exr[exi], in_=bm, func=AF.Exp), reads=["bm"], writes=[("ex", exi)])
                        S.op("pool", lambda e: e.tensor_tensor(out=khT[:, tsl], in0=kk, in1=exr[exi], op=ALU.mult), reads=["kk", ("ex", exi)], writes=["khT"])
                        S.op("act", lambda e: e.activation(out=Dd[:, csl], in_=c64[:, :, last], func=AF.Exp), reads=[curk], writes=["Dd"])
                    for g in range(2):
                        pb = 4 + (g % 2)
                        bkb = bank(pb).bitcast(BF16)
                        for j in range(8):
                            tb = 8 * g + j
                            S.op("pe", lambda e: e.transpose(bkb[:, 128 * j:128 * (j + 1)], khT[:, 128 * tb:128 * (tb + 1)], ident[:]),
                                 reads=["khT", "ident"], writes=[pk(pb)])
                        S.op("act", lambda e: e.activation(out=khtok[:, 8 * g:8 * g + 8, :], in_=bkb.rearrange("p (a d) -> p a d", a=8), func=AF.Copy),
                             reads=[pk(pb)], writes=["khtok"])
                    order = list(range(32)) if d == 0 else list(range(31, -1, -1))
                    for idx, n in enumerate(order):
                        ub = 2 + (idx // 8) % 2
                        ucol = 512 * ub + 64 * (idx % 8)
                        tb, hf = n // 2, n % 2
                        for hh in range(2):
                            S.op("pe", lambda e: e.matmul(P[64 * hh:64 * hh + 64, ucol:ucol + 64], lhsT=khtok[64 * hf:64 * hf + 64, tb, 64 * hh:64 * hh + 64],
                                                          rhs=vtok[64 * hf:64 * hf + 64, tb, 64 * hh:64 * hh + 64], start=True, stop=True),
                                 reads=["khtok", "vtok"], writes=[pk(ub)])
                        cs_, ps_ = S32[idx % 2], S32[(idx + 1) % 2]
                        if idx == 0:
                            S.op("dve", lambda e: e.tensor_copy(out=cs_, in_=P[:, ucol:ucol + 64]), reads=[pk(ub)], writes=[("S32", idx % 2)])
                        else:
                            S.op("dve", lambda e: e.scalar_tensor_tensor(out=cs_, in0=ps_, scalar=Dd[:, n:n + 1], in1=P[:, ucol:ucol + 64],
                                                                         op0=ALU.mult, op1=ALU.add),
                                 reads=[pk(ub), ("S32", (idx + 1) % 2), "Dd"], writes=[("S32", idx % 2)])
                        nxt = n + 1 if d == 0 else n - 1
                        if 0 <= nxt < 32:
                            S.op("act", lambda e: e.activation(out=Sbf[:, nxt, :], in_=cs_, func=AF.Copy),
                                 reads=[("S32", idx % 2)], writes=["Sbf"])
                    items = [(q4, j, hh) for q4 in range(4) for j in range(4) for hh in range(2)]

                    def b_scores(k):
                        q4, j, hh = items[k]
                        J = 4 * q4 + j
                        po = 64 * hh
                        ai = k % 3
                        for m, (ka, qa, kkey, qkey) in enumerate(((k1, q1, "k1", "q1"), (k2, q2, "k2", "q2"))):
                            sbk = 2 * (k % 2) + m if False else (0, 1, 6, 7)[2 * (k % 2) + m]
                            S.op("pe", lambda e: e.matmul(P[:, 512 * sbk:512 * sbk + 128], lhsT=ka[po:po + 64, 128 * J:128 * (J + 1)],
                                                          rhs=qa[po:po + 64, 128 * J:128 * (J + 1)], start=True, stop=True),
                                 reads=[kkey, qkey], writes=[pk(sbk)])
                            S.op("dve", lambda e: e.tensor_tensor(out=att[m][ai], in0=P[:, 512 * sbk:512 * sbk + 128],
                                                                  in1=bmask[:, 2 * d + m, :], op=ALU.mult),
                                 reads=[pk(sbk), "bmask"], writes=[("att", m, ai)])

                    def b_rest(k):
                        q4, j, hh = items[k]
                        J = 4 * q4 + j
                        po = 64 * hh
                        ai = k % 3
                        ob = 4 + (q4 % 2)
                        inter = []
                        for hf in range(2):
                            n = 2 * J + hf
                            if (d == 0 and n >= 1) or (d == 1 and n <= 30):
                                inter.append((n, hf))
                        c0 = 512 * ob + 128 * j
                        S.op("pe", lambda e: e.matmul(P[po:po + 64, c0:c0 + 128], lhsT=vtok[:, J, po:po + 64], rhs=att[0][ai], start=True, stop=False),
                             reads=["vtok", ("att", 0, ai)], writes=[pk(ob)])
                        S.op("pe", lambda e: e.matmul(P[po:po + 64, c0:c0 + 128], lhsT=vtok[:, J, po:po + 64], rhs=att[1][ai], start=False,
                                                      stop=(len(inter) == 0)),
                             reads=["vtok", ("att", 1, ai)], writes=[pk(ob)])
                        for ii, (n, hf) in enumerate(inter):
                            S.op("pe", lambda e: e.matmul(P[po:po + 64, c0 + 64 * hf:c0 + 64 * hf + 64], lhsT=Sbf[po:po + 64, n, :],
                                                          rhs=qh[po:po + 64, 128 * J + 64 * hf:128 * J + 64 * hf + 64],
                                                          start=False, stop=(ii == len(inter) - 1)),
                                 reads=["Sbf", "qh"], writes=[pk(ob)])
                        if not (j == 3 and hh == 1):
                            return
                        tsl = slice(512 * q4, 512 * (q4 + 1))
                        if d == 0:
                            S.op("act", lambda e: e.activation(out=o32[:, tsl], in_=bank(ob), func=AF.Copy), reads=[pk(ob)], writes=["o32"])
                            return
                        S.op("dve", lambda e: e.tensor_tensor(out=on, in0=o32[:, tsl], in1=bank(ob), op=ALU.add), reads=[pk(ob), "o32"], writes=["lf"])
                        S.op("act", lambda e: e.activation(out=sqb, in_=on, func=AF.Square), reads=["lf"], writes=["sqb"])
                        S.op("pe", lambda e: e.matmul(bank(2), lhsT=blk64[:], rhs=sqb, start=True, stop=True), reads=["sqb", "blk64"], writes=[pk(2)])
                        S.op("act", lambda e: e.activation(out=rsd, in_=bank(2), func=AF.Ln, bias=eps_t[:, 0:1], scale=1.0), reads=[pk(2), "eps"], writes=["f1"])
                        S.op("act", lambda e: e.activation(out=rsd, in_=rsd, func=AF.Exp, scale=-0.5), reads=["f1"], writes=["f1"])
                        S.op("dve", lambda e: e.scalar_tensor_tensor(out=on, in0=on, scalar=bon[:, l:l + 1], in1=rsd, op0=ALU.mult, op1=ALU.mult),
                             reads=["lf", "bon", "f1"], writes=["lf"])
                        proj_fm(wC, kC, 128 * hp, q4, 3)
                        S.op("act", lambda e: e.activation(out=gg, in_=bank(3), func=AF.Exp, scale=-1.0), reads=[pk(3)], writes=["bb"])
                        S.op("act", lambda e: e.activation(out=gg, in_=gg, func=AF.Ln, bias=one_t[:, 0:1], scale=1.0), reads=["bb", "one"], writes=["bb"])
                        S.op("act", lambda e: e.activation(out=gg, in_=gg, func=AF.Exp, scale=-1.0), reads=["bb"], writes=["bb"])
                        S.op("dve", lambda e: e.tensor_tensor(out=gg, in0=bank(3), in1=gg, op=ALU.mult), reads=[pk(3), "bb"], writes=["bb"])
                        S.op("dve", lambda e: e.tensor_tensor(out=o_b[:, hp, tsl], in0=on, in1=gg, op=ALU.mult), reads=["lf", "bb"], writes=["o_b"])
                    LA = 1
                    for k in range(len(items) + LA):
                        if k < len(items):
                            b_scores(k)
                        if k >= LA:
                            b_rest(k - LA)
            outproj_partial(l, 384, 2, o_b, "o_b", 16)
            S.barrier()

        def ss(start, r):
            return slice(start, start + 127 * r + 1, r)

        def mixer_c(l):
            cv = Carver("C")
            cm = [cv.get(f"cm{i}", [128, 9, 128], BF16) for i in range(2)]
            qn_f = cv.get("qn", [128, 6, T], BF16)
            kn_f = cv.get("kn", [128, 2, T], BF16)
            qn, kn = qn_f[0:64], kn_f[0:64]
            S.op("dve", lambda e: e.memset(qn_f[64:128], 0.0), writes=["qn"])
            S.op("dve", lambda e: e.memset(kn_f[64:128], 0.0), writes=["kn"])
            vx = [cv.get(f"vx{ri}", [128, 16, 128], BF16) for ri in range(3)]
            o_c = cv.get("o_c", [128, 3, T], BF16)
            acc_off = cv.off
            acc = cv.get("acc", [128, T])
            NPB = 4
            pT = [cv.get(f"pT{i}", [128, 384], BF16) for i in range(NPB)]
            pm = [cv.get(f"pm{i}", [128, 384], BF16) for i in range(NPB)]
            den = cv.get("den", [64, 512])
            off1 = cv.off
            cv.off = acc_off
            tsets = [(None, cv.get(f"sq{i}", [128, 512], BF16), cv.get(f"r{i}", [128, 512]), None, None) for i in range(2)]
            cv.off = off1
            wv, wk = wload(d_win[l, :, 1920:2432].rearrange("(c p) n -> p c n", p=128), (8, 512))
            qk_prep(l, wv, wk, 0, 3, 2, qn_f, "qn", tsets)
            qk_prep(l, wv, wk, 384, 1, 3, kn_f, "kn", tsets)
            S.barrier()
            wv2, wk2 = wload(d_win[l, :, 2432:2560].rearrange("(c p) n -> p c n", p=128), (8, 128))
            cmask_d = d_k["cmask"].rearrange("p (h a m) -> p h a m", h=6, a=9)
            it = 0
            si = 0
            pi = 0
            for kv in range(2):
                for ri, r in enumerate(BR):
                    nb = 16 // r
                    S.op("dve", lambda e: e.memset(vx[ri][:, :, 64:128], 1.0), writes=[("vx", ri)])
                    for g in range(2):
                        pb = 4 + (g % 2)
                        for j in range(8):
                            tb = 8 * g + j
                            c, b = tb // nb, tb % nb
                            st_ = 1 + c + r * 128 * b
                            for k in range(8):
                                S.op("pe", lambda e: e.matmul(P[:, 512 * pb + 64 * j:512 * pb + 64 * (j + 1)], lhsT=hT[:, k, ss(st_, r)],
                                                              rhs=wv2[:, k, 64 * kv:64 * (kv + 1)], start=(k == 0), stop=(k == 7)),
                                     reads=[wk2, "hT"], writes=[pk(pb)])
                        S.op("act", lambda e: e.activation(out=vx[ri][:, 8 * g:8 * g + 8, 0:64],
                                                           in_=bank(pb).rearrange("p (a d) -> p a d", a=8), func=AF.Copy),
                             reads=[pk(pb)], writes=[("vx", ri)])
                items = []
                for hh in range(3):
                    h = 3 * kv + hh
                    for ri, r in enumerate(BR):
                        nb = 16 // r
                        for g4 in range(4):
                            ob = 4 + (it % 2)
                            it += 1
                            for jj in range(4):
                                tbq = 4 * g4 + jj
                                c, qb = tbq // nb, tbq % nb
                                kbs = [kb for kb in (qb - 1, qb, qb + 1) if 0 <= kb < nb]
                                items.append(dict(h=h, ri=ri, r=r, nb=nb, g4=g4, jj=jj, c=c, qb=qb, kbs=kbs, ob=ob,
                                                  first_h=(ri == 0 and g4 == 0 and jj == 0), last_g=(jj == 3),
                                                  last_h=(ri == 2 and g4 == 3 and jj == 3), sbk=(0, 1, 2, 3)[si % 4], p_i=si % NPB))
                                si += 1

                def c_scores(I):
                    h, r, c, qb = I["h"], I["r"], I["c"], I["qb"]
                    if I["first_h"]:
                        S.dma("sp", "ld", lambda e: e.dma_start(out=cm[h % 2], in_=cmask_d[:, h, :, :]), writes=[("cm", h % 2)])
                    qs = c + r * 128 * qb
                    for ii, kb in enumerate(I["kbs"]):
                        ks = c + r * 128 * kb
                        S.op("pe", lambda e: e.matmul(P[:, 512 * I["sbk"] + 128 * ii:512 * I["sbk"] + 128 * (ii + 1)], lhsT=kn_f[:, kv, ss(ks, r)],
                                                      rhs=qn_f[:, h, ss(qs, r)], start=True, stop=True), reads=["kn", "qn"], writes=[pk(I["sbk"])])

                def c_rest(I):
                    h, ri, r, nb, g4, jj, c, qb, kbs, ob, sbk, p_i = (I[k] for k in ("h", "ri", "r", "nb", "g4", "jj", "c", "qb", "kbs", "ob", "sbk", "p_i"))
                    nk = len(kbs)
                    d0 = kbs[0] - qb + 1
                    S.op("act", lambda e: e.activation(out=pT[p_i][:, 0:128 * nk], in_=P[:, 512 * sbk:512 * sbk + 128 * nk], func=AF.Exp, scale=0.125),
                         reads=[pk(sbk)], writes=[("cpT", p_i)])
                    S.op("dve", lambda e: e.tensor_tensor(out=pm[p_i][:, 0:128 * nk].rearrange("p (a m) -> p a m", a=nk),
                                                          in0=pT[p_i][:, 0:128 * nk].rearrange("p (a m) -> p a m", a=nk),
                                                          in1=cm[h % 2][:, ri * 3 + d0:ri * 3 + d0 + nk, :], op=ALU.mult),
                         reads=[("cpT", p_i), ("cm", h % 2)], writes=[("cpm", p_i)])
                    for ii, kb in enumerate(kbs):
                        S.op("pe", lambda e: e.matmul(P[:, 512 * ob + 128 * jj:512 * ob + 128 * (jj + 1)], lhsT=vx[ri][:, c * nb + kb, :],
                                                      rhs=pm[p_i][:, 128 * ii:128 * (ii + 1)], start=(ii == 0), stop=(ii == nk - 1)),
                             reads=[("cpm", p_i), ("vx", ri)], writes=[pk(ob)])
                    if I["last_g"]:
                        if ri == 0:
                            S.op("act", lambda e: e.activation(out=acc[:, 512 * g4:512 * (g4 + 1)], in_=bank(ob), func=AF.Copy),
                                 reads=[pk(ob)], writes=["acc"])
                        elif r == 4:
                            av = acc.rearrange("p (i c) -> p c i", c=4)[:, g4, :]
                            S.op("dve", lambda e: e.tensor_tensor(out=av, in0=av, in1=bank(ob), op=ALU.add), reads=[pk(ob), "acc"], writes=["acc"])
                        else:
                            av = acc.rearrange("p (i c) -> p c i", c=16)[:, 4 * g4:4 * g4 + 4, :]
                            S.op("dve", lambda e: e.tensor_tensor(out=av, in0=av, in1=bank(ob).rearrange("p (a i) -> p a i", a=4), op=ALU.add),
                                 reads=[pk(ob), "acc"], writes=["acc"])
                    if I["last_h"]:
                        po = 64 * (h % 2)
                        for blk in range(4):
                            tsl = slice(512 * blk, 512 * (blk + 1))
                            S.op("act", lambda e: e.activation(out=den, in_=acc[64:128, tsl], func=AF.Ln), reads=["acc"], writes=["f_den"])
                            S.op("act", lambda e: e.activation(out=den, in_=den, func=AF.Exp, scale=-1.0), reads=["f_den"], writes=["f_den"])
                            S.op("dve", lambda e: e.tensor_tensor(out=o_c[po:po + 64, h // 2, tsl], in0=acc[0:64, tsl], in1=den, op=ALU.mult),
                                 reads=["acc", "f_den"], writes=["o_c"])
                LA = 2
                for k in range(len(items) + LA):
                    if k < len(items):
                        c_scores(items[k])
                    if k >= LA:
                        c_rest(items[k - LA])
            outproj_partial(l, 640, 3, o_c, "o_c", 16)
            S.barrier()

        def ffn(l):
            wup = d_wup[l].rearrange("(c p) (g n) -> p c g n", p=128, g=2)
            for half in range(2):
                cv = Carver("F")
                gT = cv.get("gT", [128, 22, 1024], BF16)
                cab = [[cv.get(f"c{ab}{i}", [128, 1024]) for i in range(2)] for ab in range(2)]
                sa = [cv.get(f"sa{i}", [128, 1024]) for i in range(2)]
                t0 = 1024 * half
                for jg in range(11):
                    i = wctr[0] % NSLOT
                    wctr[0] += 1
                    wv = wring[i][:, 0:4096].rearrange("p (c g n) -> p c g n", c=8, g=2)
                    wk = ("w", i)
                    for ab in range(2):
                        S.dma("pool", f"w{i}", lambda e: e.dma_start(out=wv[:, :, ab, :], in_=wup[:, :, ab, 256 * jg:256 * (jg + 1)],
                                                                   max_dma_last_dim=4096), writes=[wk])
                    for jj in range(2):
                        j = 2 * jg + jj
                        bi = j % 2
                        for ab, base in ((0, 0), (1, 1536)):
                            bks = [pk(base // 512 + q) for q in range(3)]
                            for q, (c0, n) in enumerate(((0, 512), (512, 512), (1024, 2))):
                                for k in range(8):
                                    S.op("pe", lambda e: e.matmul(P[:, base + c0:base + c0 + n], lhsT=wv[:, k, ab, 128 * jj:128 * (jj + 1)],
                                                                  rhs=hT[:, k, t0 + c0:t0 + c0 + n], start=(k == 0), stop=(k == 7)),
                                         reads=[wk, "hT"], writes=[bks[q]])
                            ch = 22 * ab + j
                            cbuf = cab[ab][bi]
                            ck_ = ("cab", ab, bi)
                            S.op("act", lambda e: e.activation(out=cbuf, in_=P[:, base + 1:base + 1025], func=AF.Identity,
                                                               scale=cw[:, l, 1, ch:ch + 1], bias=cbias[:, l, ch:ch + 1]),
                                 reads=bks + ["cw", "cbias"], writes=[ck_])
                            S.op("dve", lambda e: e.scalar_tensor_tensor(out=cbuf, in0=P[:, base:base + 1024], scalar=cw[:, l, 0, ch:ch + 1],
                                                                         in1=cbuf, op0=ALU.mult, op1=ALU.add),
                                 reads=bks + ["cw", ck_], writes=[ck_])
                            S.op("dve", lambda e: e.scalar_tensor_tensor(out=cbuf, in0=P[:, base + 2:base + 1026], scalar=cw[:, l, 2, ch:ch + 1],
                                                                         in1=cbuf, op0=ALU.mult, op1=ALU.add),
                                 reads=bks + ["cw", ck_], writes=[ck_])
                        S.op("act", lambda e: e.activation(out=sa[bi], in_=cab[0][bi], func=AF.Silu), reads=[("cab", 0, bi)], writes=[("sa", bi)])
                        S.op("dve", lambda e: e.tensor_tensor(out=gT[:, j, :], in0=sa[bi], in1=cab[1][bi], op=ALU.mult),
                             reads=[("sa", bi), ("cab", 1, bi)], writes=["gT"])
                    if l + 1 < depth and jg % 2 == 1 or (l + 1 < depth and jg == 10):
                        og = 6 * half + jg // 2 + (1 if jg == 10 else 0) - (0 if jg != 10 else 1)
                        og = 6 * half + (jg // 2 if jg != 10 else 5)
                        adaln_og(l + 1, og, 3584)
                        if jg == 10:
                            adaln_finish(l + 1, 3584, 24 * half, 24 * half + 24)
                it = 0
                for m in range(8):
                    wv, wk = wload(d_wdn[l, :, 128 * m:128 * (m + 1)].rearrange("(c p) n -> p c n", p=128), (22, 128))
                    for n2 in range(2):
                        pb = 6 + (it % 2)
                        it += 1
                        for j in range(22):
                            S.op("pe", lambda e: e.matmul(bank(pb), lhsT=wv[:, j, :], rhs=gT[:, j, 512 * n2:512 * (n2 + 1)],
                                                          start=(j == 0), stop=(j == 21)), reads=[wk, "gT"], writes=[pk(pb)])
                        tsl = slice(t0 + 512 * n2, t0 + 512 * (n2 + 1))
                        S.op("dve", lambda e: e.scalar_tensor_tensor(out=xT[:, m, tsl], in0=bank(pb), scalar=mod[:, l, 40 + m:41 + m],
                                                                     in1=xT[:, m, tsl], op0=ALU.mult, op1=ALU.add),
                             reads=[pk(pb), "mod", "xT"], writes=["xT"])
                S.barrier()

        for l in range(depth):
            norm_mod(l, 0)
            if do_a:
                mixer_a(l)
            if do_b:
                mixer_b(l)
            if do_c:
                mixer_c(l)
            if do_ffn:
                norm_mod(l, 1)
                ffn(l)

        S.dma("sp", "st", lambda e: e.dma_start(out=d_out.rearrange("(c p) t -> p c t", p=128), in_=xT[:]), reads=["xT"])
        S.emit(final_dsems=["st"])
    return nc


def _prep_shared(inp):
    f = lambda a: np.ascontiguousarray(np.asarray(a, dtype=np.float32))
    sh = {}
    sh["w_ada"] = f(inp["w_ada"])
    sh["b_ada_l"] = f(np.asarray(inp["b_ada"]).reshape(NL, 48, 128).transpose(2, 0, 1))
    sh["norm_g_l"] = f(np.asarray(inp["norm_g"]).reshape(NL, 2, 8, 128).transpose(3, 0, 1, 2))
    sh["w_in"] = f(inp["w_in"])
    hn = np.stack([np.asarray(inp[k]) for k in ("a_q_norm", "a_k_norm", "c_q_norm", "c_k_norm")], -1)
    sh["hn"] = f(np.concatenate([hn.transpose(1, 0, 2)] * 2, 0))
    bo = np.asarray(inp["b_out_norm"])
    sh["bon"] = f(np.concatenate([bo, bo], 1).T)
    bl = np.asarray(inp["b_lb"]).reshape(2, NL, 2, 128)
    sh["b_lb_l"] = f(bl.transpose(3, 0, 2, 1).reshape(128, 4, NL))
    sh["w_out"] = f(inp["w_out"])
    sh["w_up"] = f(inp["w_up"])
    sh["conv_w_l"] = f(np.asarray(inp["conv_w"]).reshape(NL, 3, 44, 128).transpose(3, 0, 1, 2))
    sh["conv_b_l"] = f(np.asarray(inp["conv_b"]).reshape(NL, 44, 128).transpose(2, 0, 1))
    sh["w_down"] = f(inp["w_down"])
    cst = _consts()
    sh["rst"] = cst.pop("rst")
    for k, v in cst.items():
        sh["k_" + k] = np.ascontiguousarray(v)
    return sh


def run(inputs, ncores=8, **bk):
    x = np.asarray(inputs["x"], dtype=np.float32)
    c = np.asarray(inputs["c"], dtype=np.float32)
    nc = build(**bk)
    sh = _prep_shared(inputs)
    in_maps = []
    for b in range(ncores):
        m = dict(sh)
        m["xT"] = np.ascontiguousarray(x[b].T)
        m["c128"] = np.ascontiguousarray(c[b].reshape(8, 128).T)
        in_maps.append(m)
    res = run_bass_kernel_spmd(nc, in_maps, core_ids=list(range(ncores)))
    if bk.get("dbg"):
        return res.results[0]["dbg"]
    return np.stack([np.ascontiguousarray(r["outT"].T) for r in res.results]).astype(np.float32)


def kernel(**inputs):
    return run(inputs, ncores=8)
```

```python
import numpy as np
import ml_dtypes
from contextlib import ExitStack
import concourse.bass as bass
import concourse.mybir as mybir
from concourse.bass_utils import run_bass_kernel_spmd

F32 = mybir.dt.float32
BF16 = mybir.dt.bfloat16
AF = mybir.ActivationFunctionType
ALU = mybir.AluOpType
AX = mybir.AxisListType

ENGS = ("pe", "act", "dve", "pool", "sp")
T = 2048
D = 1024
NL = 4
EPS = 1e-6
SLOPES = [2.0 ** (-8.0 * i / 6) for i in range(1, 7)]
BR = (1, 4, 16)


class _Rec:
    def __getattr__(self, name):
        def f(*a, **k):
            self.call = (name, a, k)
            return None
        return f


class Sched:
    def __init__(self, nc, self_sync=True):
        self.nc = nc
        self.self_sync = self_sync
        self.ops = {e: [] for e in ENGS}
        self.lastw = {}
        self.readers = {}
        self.dma_tot = {}

    def _cur(self, ev):
        if ev[0] == 'D':
            return ('D', ev[1], self.dma_tot[ev[1]])
        return ev

    def _add(self, eng, fn, reads, writes, dma=None, extra=()):
        deps = set(extra)
        for k in reads:
            ev = self.lastw.get(k)
            if ev is not None:
                deps.add(self._cur(ev))
        for k in writes:
            ev = self.lastw.get(k)
            if ev is not None:
                deps.add(self._cur(ev))
            for r in self.readers.get(k, ()):
                deps.add(self._cur(r))
        idx = len(self.ops[eng])
        if dma is not None:
            self.dma_tot[dma] = self.dma_tot.get(dma, 0) + 16
            myev = ('D', dma, None)
        else:
            myev = ('E', eng, idx)
        d2 = set()
        for d in deps:
            if d[0] == 'E' and d[1] == eng:
                if eng == 'pe' or not self.self_sync or dma is not None:
                    continue
            d2.add(d)
        rec = _Rec()
        fn(rec)
        self.ops[eng].append(dict(call=rec.call, deps=d2, dma=dma))
        for k in reads:
            self.readers.setdefault(k, []).append(myev)
        for k in writes:
            self.lastw[k] = myev
            self.readers[k] = []
        return idx

    def op(self, eng, fn, reads=(), writes=()):
        return self._add(eng, fn, list(reads), list(writes))

    def dma(self, eng, dsem, fn, reads=(), writes=()):
        return self._add(eng, fn, list(reads), list(writes), dma=dsem)

    def barrier(self, engs=("pe", "act", "dve", "sp")):
        evs = []
        for e in engs:
            for i in range(len(self.ops[e]) - 1, -1, -1):
                if self.ops[e][i]['dma'] is None:
                    evs.append(('E', e, i))
                    break
        for name in self.dma_tot:
            if not name.startswith("w"):
                evs.append(('D', name, self.dma_tot[name]))
        for e in engs:
            self._add(e, lambda eng: eng.nop(), [], [], extra=[v for v in evs if not (v[0] == 'E' and v[1] == e)])

    def emit(self, final_dsems=()):
        nc = self.nc
        need = {e: set() for e in ENGS}
        for e in ENGS:
            for o in self.ops[e]:
                for d in o['deps']:
                    if d[0] == 'E':
                        need[d[1]].add(d[2])
        LIM = 30000
        count_at = {e: {} for e in ENGS}
        n_epochs = {}
        for e in ENGS:
            c = 0
            ep = 0
            for i in range(len(self.ops[e])):
                if i in need[e]:
                    c += 1
                    if c > LIM:
                        ep += 1
                        c = 1
                    count_at[e][i] = (ep, c)
            n_epochs[e] = ep + 1
        with ExitStack() as st:
            esem = {}
            for e in ENGS:
                for ep in range(n_epochs[e]):
                    esem[(e, ep)] = st.enter_context(nc.semaphore(f"s_{e}_{ep}"))
            dsem = {}
            for name in self.dma_tot:
                dsem[name] = st.enter_context(nc.semaphore(f"d_{name}"))
            block = st.enter_context(nc.Block())
            engobj = {"pe": "tensor", "act": "scalar", "dve": "vector", "pool": "gpsimd", "sp": "sync"}

            def make(e):
                def body(eng):
                    known = {}
                    for i, o in enumerate(self.ops[e]):
                        w = {}
                        for d in o['deps']:
                            if d[0] == 'E':
                                ep, v = count_at[d[1]][d[2]]
                                kk = ('E', d[1], ep)
                                s = esem[(d[1], ep)]
                            else:
                                kk = ('D', d[1])
                                s = dsem[d[1]]
                                v = d[2]
                            if known.get(kk, 0) >= v:
                                continue
                            if kk not in w or w[kk][1] < v:
                                w[kk] = (s, v)
                        for kk, (s, v) in w.items():
                            eng.wait_ge(s, v)
                            known[kk] = v
                        cname, ca, ck = o['call']
                        ins = getattr(eng, cname)(*ca, **ck)
                        if o['dma'] is not None:
                            ins.then_inc(dsem[o['dma']], 16)
                        elif i in need[e]:
                            ep, v = count_at[e][i]
                            ins.then_inc(esem[(e, ep)], 1)
                    if e == 'sp':
                        for name in final_dsems:
                            eng.wait_ge(dsem[name], self.dma_tot[name])
                return body
            for e in ENGS:
                getattr(block, engobj[e])(make(e))


def _consts():
    bf = ml_dtypes.bfloat16
    c = {}
    c["ident"] = np.eye(128, dtype=np.float32).astype(bf)
    c["onesD"] = np.full((128, 128), 1.0 / 1024, np.float32).astype(bf)
    c["ones64"] = np.full((64, 64), 1.0 / 64, np.float32).astype(bf)
    b = np.zeros((128, 128), np.float32)
    b[:64, :64] = 1.0 / 64
    b[64:, 64:] = 1.0 / 64
    c["blk64"] = b.astype(bf)
    R = np.zeros((64, 64), np.float32)
    for d in list(range(0, 16)) + list(range(32, 48)):
        R[d + 16, d] = -1.0
    for d in list(range(16, 32)) + list(range(48, 64)):
        R[d - 16, d] = 1.0
    R2 = np.zeros((128, 128), np.float32)
    R2[:64, :64] = R
    R2[64:, 64:] = R
    c["rrot"] = R2.astype(bf)
    t = np.arange(T)
    row = (t // 64).astype(np.float64)
    col = (t % 64).astype(np.float64)
    inv = 10000.0 ** (-np.arange(0, 32, 2, dtype=np.float64) / 32)
    ang = np.zeros((64, T))
    ang[0:16] = row[None, :] * inv[:, None]
    ang[16:32] = row[None, :] * inv[:, None]
    ang[32:48] = col[None, :] * inv[:, None]
    ang[48:64] = col[None, :] * inv[:, None]
    cs1 = np.stack([np.cos(ang), np.sin(ang)], 1).astype(np.float32)
    c["cossin"] = np.concatenate([cs1, cs1], 0).astype(bf)
    s = np.arange(128)[:, None]
    q = np.arange(128)[None, :]
    same = (s // 64) == (q // 64)
    same32 = (s // 32) == (q // 32)
    s_lo = (s % 64) < 32
    q_lo = (q % 64) < 32
    c["bmask"] = np.stack([(same32 & (s <= q)), (same & s_lo & ~q_lo),
                           (same32 & (s >= q)), (same & ~s_lo & q_lo)], 1).astype(np.float32).astype(bf)
    rst = np.ones((128, 512), np.float32)
    rst[:, ::64] = 0.0
    c["rst"] = rst
    cm = np.zeros((128, 6, 3, 3, 128), np.float32)
    for h in range(6):
        for ri, r in enumerate(BR):
            for di, dl in enumerate((-1, 0, 1)):
                dd = 128 * dl + s - q
                cm[:, h, ri, di, :] = np.where(np.abs(dd) <= 64, np.exp(-SLOPES[h] * r * np.abs(dd)), 0.0)
    c["cmask"] = cm.reshape(128, 6 * 9 * 128).astype(bf)
    return c


_CONST_SHAPES = {"ident": [128, 128], "onesD": [128, 128], "ones64": [64, 64], "blk64": [128, 128], "rrot": [128, 128],
                 "cossin": [128, 2, T], "bmask": [128, 4, 128], "cmask": [128, 6 * 9 * 128]}


def build(depth=NL, do_a=True, do_b=True, do_c=True, do_ffn=True, self_sync=True, dbg=None):
    nc = bass.Bass("TRN2", target_bir_lowering=False)

    def dram(name, shape, dt=F32, kind="ExternalInput"):
        return nc.dram_tensor(name, list(shape), dt, kind=kind).ap()

    d_xT = dram("xT", [D, T])
    d_out = dram("outT", [D, T], kind="ExternalOutput")
    d_c = dram("c128", [128, 8])
    d_wada = dram("w_ada", [NL, D, 6 * D])
    d_bada = dram("b_ada_l", [128, NL, 48])
    d_ng = dram("norm_g_l", [128, NL, 2, 8])
    d_win = dram("w_in", [NL, D, 2560])
    d_hn = dram("hn", [128, NL, 4])
    d_bon = dram("bon", [128, NL])
    d_blb = dram("b_lb_l", [128, 4, NL])
    d_wout = dram("w_out", [NL, D, D])
    d_wup = dram("w_up", [NL, D, 5632])
    d_cw = dram("conv_w_l", [128, NL, 3, 44])
    d_cb = dram("conv_b_l", [128, NL, 44])
    d_wdn = dram("w_down", [NL, 2816, D])
    d_rst = dram("rst", [128, 512])
    d_k = {k: dram("k_" + k, v, BF16) for k, v in _CONST_SHAPES.items()}

    S = Sched(nc, self_sync=self_sync)
    st = ExitStack()
    with st:
        def sb(name, shape, dt=F32):
            return st.enter_context(nc.sbuf_tensor(name, list(shape), dt))

        xT = sb("xT_sb", [128, 8, T])
        hT = sb("hT_sb", [128, 8, T + 2], BF16)
        NSLOT = 2 if dbg else 3
        wring = [sb(f"wring{i}", [128, 4096], BF16) for i in range(NSLOT)]
        mod = sb("mod", [128, NL, 48])
        bada = sb("bada", [128, NL, 48])
        ng = sb("ng", [128, NL, 2, 8])
        gs = sb("gs", [128, NL, 2, 8])
        cw = sb("cw", [128, NL, 3, 44])
        cbias = sb("cbias", [128, NL, 44])
        hn = sb("hn_sb", [128, NL, 4])
        bon = sb("bon_sb", [128, NL])
        blb = sb("blb", [128, 4, NL])
        lbv = sb("lbv", [128, 4, NL])
        oml = sb("oml", [128, 4, NL])
        lbs = sb("lbs", [128, 4])
        c_sb = sb("c_sb", [128, 8])
        sc_f = sb("sc_f", [128, 8])
        sc_b = sb("sc_b", [128, 8], BF16)
        ident = sb("ident", [128, 128], BF16)
        onesD = sb("onesD", [128, 128], BF16)
        ones64 = sb("ones64", [64, 64], BF16)
        blk64 = sb("blk64", [128, 128], BF16)
        rrot = sb("rrot", [128, 128], BF16)
        bmask = sb("bmask", [128, 4, 128], BF16)
        rst = sb("rst_sb", [128, 512])
        eps_t = sb("eps_t", [128, 1])
        one_t = sb("one_t", [128, 1])
        if dbg:
            d_dbg = dram("dbg", [128, T], kind="ExternalOutput")
            dbgt = sb("dbgt", [128, T])
            S.op("dve", lambda e: e.memset(dbgt[:], 0.0), writes=["dbgt"])

        def tap(name, ap, key, parts=128, n=T):
            if dbg != name:
                return
            S.op("act", lambda e: e.activation(out=dbgt[0:parts, 0:n], in_=ap, func=AF.Copy), reads=[key, "dbgt"], writes=["dbgt"])
            S.dma("sp", "st", lambda e: e.dma_start(out=d_dbg, in_=dbgt[:]), reads=["dbgt"])
        RW = nc.sbuf_bytes_remaining // 4 - 64
        assert RW * 4 >= 70000, RW
        R = sb("R", [128, RW])
        P = st.enter_context(nc.psum_tensor("P", [128, 4096], F32))

        def bank(i):
            return P[:, 512 * i:512 * (i + 1)]

        def pk(i):
            return ("ps", i)

        class Carver:
            def __init__(self, tag):
                self.off = 0
                self.tag = tag

            def get(self, name, shape, dt=F32, parts=128):
                n = int(np.prod(shape[1:]))
                nb = n * (4 if dt == F32 else 2)
                nb4 = (nb + 3) // 4
                ap = R[0:shape[0], self.off:self.off + nb4]
                if dt == BF16:
                    ap = ap.bitcast(BF16)[:, 0:n]
                self.off += nb4
                assert self.off <= RW, (self.tag, name, self.off * 4)
                if len(shape) == 3:
                    ap = ap.rearrange("p (a b) -> p a b", a=shape[1])
                elif len(shape) == 4:
                    ap = ap.rearrange("p (a b c) -> p a b c", a=shape[1], b=shape[2])
                return ap

        wctr = [0]

        def wload(src, shape):
            i = wctr[0] % NSLOT
            wctr[0] += 1
            a, b = shape
            view = wring[i][:, 0:a * b].rearrange("p (a b) -> p a b", a=a)
            key = ("w", i)
            S.dma("pool", f"w{i}", lambda e: e.dma_start(out=view, in_=src, max_dma_last_dim=4096), writes=[key])
            return view, key

        def ld(dst, src, key):
            S.dma("sp", "ld", lambda e: e.dma_start(out=dst, in_=src), writes=[key])

        ld(xT[:], d_xT.rearrange("(c p) t -> p c t", p=128), "xT")
        for nm, dst, src in (("bada", bada, d_bada), ("ng", ng, d_ng), ("cw", cw, d_cw), ("cbias", cbias, d_cb),
                             ("hn", hn, d_hn), ("bon", bon, d_bon), ("blb", blb, d_blb), ("c", c_sb, d_c),
                             ("ident", ident, d_k["ident"]), ("onesD", onesD, d_k["onesD"]),
                             ("ones64", ones64, d_k["ones64"]), ("blk64", blk64, d_k["blk64"]),
                             ("rrot", rrot, d_k["rrot"]), ("bmask", bmask, d_k["bmask"]), ("rst", rst, d_rst)):
            ld(dst[:], src, nm)
        S.op("dve", lambda e: e.memset(eps_t[:], EPS), writes=["eps"])
        S.op("dve", lambda e: e.memset(one_t[:], 1.0), writes=["one"])
        S.op("dve", lambda e: e.memset(hT[:, :, 0:1], 0.0), writes=["hT"])
        S.op("dve", lambda e: e.memset(hT[:, :, T + 1:T + 2], 0.0), writes=["hT"])

        S.op("act", lambda e: e.activation(out=lbv[:], in_=blb[:], func=AF.Exp), reads=["blb"], writes=["lbv"])
        S.op("dve", lambda e: e.tensor_reduce(out=lbs[:], in_=lbv[:], axis=AX.X, op=ALU.add), reads=["lbv"], writes=["lbs"])
        S.op("dve", lambda e: e.reciprocal(out=lbs[:], in_=lbs[:]), reads=["lbs"], writes=["lbs"])
        for l in range(NL):
            S.op("dve", lambda e, l=l: e.tensor_tensor(out=lbv[:, :, l], in0=lbv[:, :, l], in1=lbs[:], op=ALU.mult),
                 reads=["lbv", "lbs"], writes=["lbv"])
        S.op("dve", lambda e: e.memset(lbv[:, :, 0:1], 0.0), reads=["lbv"], writes=["lbv"])
        for l in range(2, NL):
            S.op("dve", lambda e, l=l: e.tensor_tensor(out=lbv[:, :, l], in0=lbv[:, :, l], in1=lbv[:, :, l - 1], op=ALU.add),
                 reads=["lbv"], writes=["lbv"])
        S.op("dve", lambda e: e.tensor_scalar(out=oml[:], in0=lbv[:], scalar1=-1.0, scalar2=1.0, op0=ALU.mult, op1=ALU.add),
             reads=["lbv"], writes=["oml"])

        S.op("act", lambda e: e.activation(out=sc_f[:], in_=c_sb[:], func=AF.Exp, scale=-1.0), reads=["c"], writes=["scf"])
        S.op("dve", lambda e: e.tensor_scalar_add(out=sc_f[:], in0=sc_f[:], scalar1=1.0), reads=["scf"], writes=["scf"])
        S.op("dve", lambda e: e.reciprocal(out=sc_f[:], in_=sc_f[:]), reads=["scf"], writes=["scf"])
        S.op("dve", lambda e: e.tensor_tensor(out=sc_b[:], in0=sc_f[:], in1=c_sb[:], op=ALU.mult), reads=["scf", "c"], writes=["scb"])
        def adaln_og(l, og, c00):
            wv, wk = wload(d_wada[l, :, 512 * og:512 * (og + 1)].rearrange("(c p) n -> p c n", p=128), (8, 512))
            for m in range(4):
                col = c00 + og * 4 + m
                for k in range(8):
                    S.op("pe", lambda e: e.matmul(P[:, col:col + 1], lhsT=wv[:, k, 128 * m:128 * (m + 1)], rhs=sc_b[:, k:k + 1],
                                                  start=(k == 0), stop=(k == 7)), reads=[wk, "scb"], writes=[pk(c00 // 512)])

        def adaln_finish(l, c00, lo=0, hi=48):
            S.op("dve", lambda e: e.tensor_tensor(out=mod[:, l, lo:hi], in0=P[:, c00 + lo:c00 + hi], in1=bada[:, l, lo:hi], op=ALU.add),
                 reads=[pk(c00 // 512), "bada"], writes=["mod"])
            if hi < 48:
                return
            for w_, c0 in ((0, 8), (1, 32)):
                S.op("dve", lambda e: e.scalar_tensor_tensor(
                    out=gs[:, l, w_, :], in0=mod[:, l, c0:c0 + 8], scalar=1.0, in1=ng[:, l, w_, :], op0=ALU.add, op1=ALU.mult),
                    reads=["mod", "ng"], writes=["gs"])

        for og in range(12):
            adaln_og(0, og, 0)
        adaln_finish(0, 0)

        def norm_mod(l, w_):
            S.barrier()
            cv = Carver("norm")
            NB = 4
            sq = [cv.get(f"sq{i}", [128, 512], BF16) for i in range(NB)]
            rs = [cv.get(f"rs{i}", [128, 512]) for i in range(2)]
            tm = [cv.get(f"tm{i}", [128, 512]) for i in range(NB)]
            sh0 = 0 if w_ == 0 else 24
            cnt = 0
            for blk in range(4):
                tsl = slice(512 * blk, 512 * (blk + 1))
                pb = 6 + (blk % 2)
                for c in range(8):
                    i = (cnt + c) % NB
                    S.op("pool" if c % 2 == 0 else "dve", lambda e: e.tensor_tensor(out=sq[i], in0=xT[:, c, tsl], in1=xT[:, c, tsl], op=ALU.mult),
                         reads=["xT"], writes=[("nsq", i)])
                    S.op("pe", lambda e: e.matmul(bank(pb), lhsT=onesD[:], rhs=sq[i], start=(c == 0), stop=(c == 7)),
                         reads=[("nsq", i), "onesD"], writes=[pk(pb)])
                r = rs[blk % 2]
                S.op("act", lambda e: e.activation(out=r, in_=bank(pb), func=AF.Ln, bias=eps_t[:, 0:1], scale=1.0),
                     reads=[pk(pb), "eps"], writes=[("nrs", blk % 2)])
                S.op("act", lambda e: e.activation(out=r, in_=r, func=AF.Exp, scale=-0.5),
                     reads=[("nrs", blk % 2)], writes=[("nrs", blk % 2)])
                for c in range(8):
                    i = (cnt + c) % NB
                    S.op("dve", lambda e: e.scalar_tensor_tensor(
                        out=tm[i], in0=xT[:, c, tsl], scalar=gs[:, l, w_, c:c + 1], in1=r, op0=ALU.mult, op1=ALU.mult),
                        reads=["xT", "gs", ("nrs", blk % 2)], writes=[("ntm", i)])
                    S.op("act", lambda e: e.activation(
                        out=hT[:, c, 1 + 512 * blk:1 + 512 * (blk + 1)], in_=tm[i], func=AF.Identity,
                        bias=mod[:, l, sh0 + c:sh0 + c + 1], scale=1.0),
                        reads=[("ntm", i), "mod"], writes=["hT"])
                cnt += 8
            S.barrier()
            tap("mod", mod[:, l, :], "mod", n=48)
            tap("hT0", hT[:, 0, 1:T + 1], "hT")
            tap("hT7", hT[:, 7, 1:T + 1], "hT")

        def proj_fm(wv, wk, c0, blk, pb):
            for k in range(8):
                S.op("pe", lambda e, k=k: e.matmul(bank(pb), lhsT=wv[:, k, c0:c0 + 128],
                                                   rhs=hT[:, k, 1 + 512 * blk:1 + 512 * (blk + 1)],
                                                   start=(k == 0), stop=(k == 7)),
                     reads=[wk, "hT"], writes=[pk(pb)])

        def outproj_partial(l, row0, nk, src, srckey, g_col0):
            wv, wk = wload(d_wout[l, row0:row0 + 128 * nk, :].rearrange("(c p) n -> p c n", p=128), (nk, 1024))
            i = 0
            for m in range(8):
                for blk in range(4):
                    pb = 6 + (i % 2)
                    i += 1
                    tsl = slice(512 * blk, 512 * (blk + 1))
                    for k in range(nk):
                        S.op("pe", lambda e, k=k, m=m, pb=pb, tsl=tsl: e.matmul(
                            bank(pb), lhsT=wv[:, k, 128 * m:128 * (m + 1)], rhs=src[:, k, tsl], start=(k == 0), stop=(k == nk - 1)),
                            reads=[wk, srckey], writes=[pk(pb)])
                    S.op("dve", lambda e, m=m, pb=pb, tsl=tsl: e.scalar_tensor_tensor(
                        out=xT[:, m, tsl], in0=bank(pb), scalar=mod[:, l, g_col0 + m:g_col0 + m + 1], in1=xT[:, m, tsl],
                        op0=ALU.mult, op1=ALU.add), reads=[pk(pb), "mod", "xT"], writes=["xT"])

        prep_ctr = [0]

        def qk_prep(l, wv, wk, c0, nchunks, gcol, dst_f, dstkey, tsets, cs=None):
            for hc in range(nchunks):
                for blk in range(4):
                    tsl = slice(512 * blk, 512 * (blk + 1))
                    i = prep_ctr[0] % 2
                    prep_ctr[0] += 1
                    qg, sq, r, t1, t2 = tsets[i]
                    pb, mb, rb = i, 2 + i, 4 + i
                    proj_fm(wv, wk, c0 + 128 * hc, blk, pb)
                    S.op("act", lambda e: e.activation(out=sq, in_=bank(pb), func=AF.Square), reads=[pk(pb)], writes=[("p_sq", i)])
                    S.op("pe", lambda e: e.matmul(bank(mb), lhsT=blk64[:], rhs=sq, start=True, stop=True),
                         reads=[("p_sq", i), "blk64"], writes=[pk(mb)])
                    if cs is not None:
                        S.op("act", lambda e: e.activation(out=qg, in_=bank(pb), func=AF.Identity, scale=hn[:, l, gcol:gcol + 1]),
                             reads=[pk(pb), "hn"], writes=[("p_qg", i)])
                        S.op("pe", lambda e: e.matmul(bank(rb), lhsT=rrot[:], rhs=qg, start=True, stop=True),
                             reads=[("p_qg", i), "rrot"], writes=[pk(rb)])
                    S.op("act", lambda e: e.activation(out=r, in_=bank(mb), func=AF.Ln, bias=eps_t[:, 0:1], scale=1.0),
                         reads=[pk(mb), "eps"], writes=[("p_r", i)])
                    S.op("act", lambda e: e.activation(out=r, in_=r, func=AF.Exp, scale=-0.5), reads=[("p_r", i)], writes=[("p_r", i)])
                    if cs is not None:
                        S.op("dve", lambda e: e.tensor_tensor(out=t1, in0=qg, in1=cs[:, 0, tsl], op=ALU.mult),
                             reads=[("p_qg", i), "cs"], writes=[("p_t1", i)])
                        S.op("dve", lambda e: e.tensor_tensor(out=t2, in0=bank(rb), in1=cs[:, 1, tsl], op=ALU.mult),
                             reads=[pk(rb), "cs"], writes=[("p_t2", i)])
                        S.op("dve", lambda e: e.tensor_tensor(out=t1, in0=t1, in1=t2, op=ALU.add),
                             reads=[("p_t1", i), ("p_t2", i)], writes=[("p_t1", i)])
                        for hh in range(2):
                            S.op("dve", lambda e: e.tensor_tensor(out=dst_f[0:64, 2 * hc + hh, tsl], in0=t1[64 * hh:64 * hh + 64],
                                                                  in1=r[64 * hh:64 * hh + 64], op=ALU.mult),
                                 reads=[("p_t1", i), ("p_r", i)], writes=[dstkey])
                    else:
                        for hh in range(2):
                            S.op("dve", lambda e: e.scalar_tensor_tensor(
                                out=dst_f[0:64, 2 * hc + hh, tsl], in0=P[64 * hh:64 * hh + 64, 512 * pb:512 * (pb + 1)],
                                scalar=hn[64 * hh:64 * hh + 64, l, gcol:gcol + 1], in1=r[64 * hh:64 * hh + 64], op0=ALU.mult, op1=ALU.mult),
                                reads=[pk(pb), "hn", ("p_r", i)], writes=[dstkey])

        def build_vext(wv, wk, c0, vext, r):
            S.op("dve", lambda e: e.memset(vext[:, :, :, 64:128], 1.0), writes=["vext"])
            for g4 in range(4):
                pb = 4 + (g4 % 2)
                for j in range(4):
                    tb = 4 * g4 + j
                    nb = 16 // r
                    c, b = tb // nb, tb % nb
                    start = 1 + c + r * 128 * b
                    for k in range(8):
                        S.op("pe", lambda e, k=k, j=j, pb=pb, start=start: e.matmul(
                            P[:, 512 * pb + 128 * j:512 * pb + 128 * (j + 1)],
                            lhsT=hT[:, k, start:start + 128 * r:r] if r > 1 else hT[:, k, start:start + 128],
                            rhs=wv[:, k, c0:c0 + 128], start=(k == 0), stop=(k == 7)),
                            reads=[wk, "hT"], writes=[pk(pb)])
                S.op("act", lambda e, g4=g4, pb=pb: e.activation(
                    out=vext[:, 4 * g4:4 * g4 + 4, :, 0:64],
                    in_=bank(pb).rearrange("p (a k d) -> p a k d", a=4, k=2), func=AF.Copy),
                    reads=[pk(pb)], writes=["vext"])

        def finalize_head(src_num, src_den, srckeys, dst_fn, dstkey, tmp):
            den, tmpo = tmp
            for blk in range(4):
                S.op("act", lambda e, blk=blk: e.activation(out=den, in_=src_den(blk), func=AF.Copy), reads=srckeys(blk), writes=["f_den"])
                S.op("dve", lambda e: e.reciprocal(out=den, in_=den), reads=["f_den"], writes=["f_den"])
                S.op("dve", lambda e, blk=blk: e.tensor_tensor(out=tmpo, in0=src_num(blk), in1=den, op=ALU.mult),
                     reads=srckeys(blk) + ["f_den"], writes=["f_tmp"])
                S.op("act", lambda e, blk=blk: e.activation(out=dst_fn(blk), in_=tmpo, func=AF.Copy), reads=["f_tmp"], writes=[dstkey])

        def mixer_a(l):
            cv = Carver("A")
            qr_f = cv.get("qr", [128, 6, T], BF16)
            kr_f = cv.get("kr", [128, 2, T], BF16)
            qr, kr = qr_f[0:64], kr_f[0:64]
            S.op("dve", lambda e: e.memset(qr_f[64:128], 0.0), writes=["qr"])
            S.op("dve", lambda e: e.memset(kr_f[64:128], 0.0), writes=["kr"])
            vext = cv.get("vext", [128, 16, 2, 128], BF16)
            o_a = cv.get("o_a", [128, 3, T], BF16)
            off0 = cv.off
            cs = cv.get("cs", [128, 2, T], BF16)
            tsets = [(cv.get(f"qg{i}", [128, 512], BF16), cv.get(f"sq{i}", [128, 512], BF16), cv.get(f"r{i}", [128, 512]),
                      cv.get(f"t1{i}", [128, 512]), cv.get(f"t2{i}", [128, 512])) for i in range(2)]
            S.dma("sp", "ld", lambda e: e.dma_start(out=cs, in_=d_k["cossin"]), writes=["cs"])
            wv, wk = wload(d_win[l, :, 0:512].rearrange("(c p) n -> p c n", p=128), (8, 512))
            qk_prep(l, wv, wk, 0, 3, 0, qr_f, "qr", tsets, cs=cs)
            qk_prep(l, wv, wk, 384, 1, 1, kr_f, "kr", tsets, cs=cs)
            wv2, wk2 = wload(d_win[l, :, 512:640].rearrange("(c p) n -> p c n", p=128), (8, 128))
            build_vext(wv2, wk2, 0, vext, 1)
            tap("qr0", qr[:, 0, :], "qr", parts=64)
            tap("qr5", qr[:, 5, :], "qr", parts=64)
            tap("kr1", kr[:, 1, :], "kr", parts=64)
            S.barrier()
            cv.off = off0
            pT = [cv.get(f"pT{i}", [128, 1024], BF16) for i in range(2)]
            ftmp = [cv.get(f"den{i}", [64, 512]) for i in range(2)]
            pending = []
            it = 0
            for h in range(6):
                kv = h // 3
                for qb in range(4):
                    qsl = slice(512 * qb, 512 * (qb + 1))
                    ob = 4 + (it % 2)
                    it += 1

                    def score(pp):
                        for u in range(2):
                            kc = 2 * pp + u
                            sbk = 2 * (pp % 2) + u
                            S.op("pe", lambda e: e.matmul(bank(sbk), lhsT=kr_f[:, kv, 128 * kc:128 * (kc + 1)], rhs=qr_f[:, h, qsl],
                                                          start=True, stop=True), reads=["kr", "qr"], writes=[pk(sbk)])

                    def pv(pp):
                        sb0 = 2 * (pp % 2)
                        pi = pp % 2
                        S.op("act", lambda e: e.activation(out=pT[pi], in_=P[:, 512 * sb0:512 * sb0 + 1024], func=AF.Exp, scale=0.125),
                             reads=[pk(sb0), pk(sb0 + 1)], writes=[("pT", pi)])
                        for u in range(2):
                            kc = 2 * pp + u
                            S.op("pe", lambda e: e.matmul(bank(ob), lhsT=vext[:, kc, kv, :], rhs=pT[pi][:, 512 * u:512 * (u + 1)],
                                                          start=(kc == 0), stop=(kc == 15)),
                                 reads=[("pT", pi), "vext"], writes=[pk(ob)])
                    def fin(ob=ob, h=h, qsl=qsl, fi=it % 2):
                        po = 64 * (h % 2)
                        S.op("act", lambda e: e.activation(out=ftmp[fi], in_=P[64:128, 512 * ob:512 * (ob + 1)], func=AF.Ln),
                             reads=[pk(ob)], writes=[("f_den", fi)])
                        S.op("act", lambda e: e.activation(out=ftmp[fi], in_=ftmp[fi], func=AF.Exp, scale=-1.0),
                             reads=[("f_den", fi)], writes=[("f_den", fi)])
                        S.op("dve", lambda e: e.tensor_tensor(out=o_a[po:po + 64, h // 2, qsl], in0=P[0:64, 512 * ob:512 * (ob + 1)], in1=ftmp[fi],
                                                              op=ALU.mult), reads=[pk(ob), ("f_den", fi)], writes=["o_a"])
                    score(0)
                    score(1)
                    for pp in range(8):
                        pv(pp)
                        if pp + 2 < 8:
                            score(pp + 2)
                        if pp == 1 and pending:
                            pending.pop()()
                    pending.append(fin)
            pending.pop()()
            tap("oa0", o_a[:, 0, :], "o_a")
            tap("oa2", o_a[:, 2, :], "o_a")
            outproj_partial(l, 0, 3, o_a, "o_a", 16)
            S.barrier()


        def mixer_b(l):
            cv = Carver("B")
            o_b = cv.get("o_b", [128, 2, T], BF16)
            qb_ = cv.get("qb", [128, T], BF16)
            qh = cv.get("qh", [128, T], BF16)
            q1 = cv.get("q1", [128, T], BF16)
            q2 = cv.get("q2", [128, T], BF16)
            k1 = cv.get("k1", [128, T], BF16)
            k2 = cv.get("k2", [128, T], BF16)
            vtok = cv.get("vtok", [128, 16, 128], BF16)
            khT = cv.get("khT", [128, T], BF16)
            khtok = cv.get("khtok", [128, 16, 128], BF16)
            Sbf = cv.get("Sbf", [128, 32, 64], BF16)
            Dd = cv.get("Dd", [128, 32])
            S32 = [cv.get(f"S32{i}", [128, 64]) for i in range(2)]
            f1 = cv.get("f1", [128, 512])
            lf = cv.get("lf", [128, 512])
            bb = cv.get("bb", [128, 512])
            cc = cv.get("cc", [128, 512])
            bm = cv.get("bm", [128, 512])
            exr = [cv.get(f"ex{i}", [128, 512]) for i in range(3)]
            exc = [0]
            kk = cv.get("kk", [128, 512], BF16)
            att = [[cv.get(f"att{m}{i}", [128, 128], BF16) for i in range(3)] for m in range(2)]
            o32 = cv.get("o32", [128, T], BF16)
            sqb = cv.get("sqb", [128, 512], BF16)
            rsd, on, gg = f1, lf, bb
            wA, kA = wload(d_win[l, :, 640:1152].rearrange("(c p) n -> p c n", p=128), (8, 512))
            wB, kB = wload(d_win[l, :, 1152:1664].rearrange("(c p) n -> p c n", p=128), (8, 512))
            wC, kC = wload(d_win[l, :, 1664:1920].rearrange("(c p) n -> p c n", p=128), (8, 256))
            v64 = lambda ap: ap.rearrange("p (n i) -> p n i", i=64)
            v32 = lambda ap: ap.rearrange("p (n i) -> p n i", i=32)
            si = 0
            for hp in range(2):
                for blk in range(4):
                    pb = blk % 2
                    proj_fm(wA, kA, 128 * hp, blk, pb)
                    S.op("act", lambda e: e.activation(out=qb_[:, 512 * blk:512 * (blk + 1)], in_=bank(pb), func=AF.Copy),
                         reads=[pk(pb)], writes=["qb"])
                for g4 in range(4):
                    pb = 4 + (g4 % 2)
                    for j in range(4):
                        tb = 4 * g4 + j
                        for k in range(8):
                            S.op("pe", lambda e: e.matmul(P[:, 512 * pb + 128 * j:512 * pb + 128 * (j + 1)], lhsT=hT[:, k, 1 + 128 * tb:1 + 128 * (tb + 1)],
                                                          rhs=wB[:, k, 256 + 128 * hp:256 + 128 * (hp + 1)], start=(k == 0), stop=(k == 7)),
                                 reads=[kB, "hT"], writes=[pk(pb)])
                    S.op("act", lambda e: e.activation(out=vtok[:, 4 * g4:4 * g4 + 4, :], in_=bank(pb).rearrange("p (a d) -> p a d", a=4), func=AF.Copy),
                         reads=[pk(pb)], writes=["vtok"])
                for d in range(2):
                    wz, kz, zc0 = (wA, kA, 256 + 128 * hp) if d == 0 else (wB, kB, 128 * hp)
                    lbi = d * 2 + hp
                    r32, r64, last = (15, 31, 63) if d == 0 else (16, 32, 0)
                    for blk in range(4):
                        tsl = slice(512 * blk, 512 * (blk + 1))
                        csl = slice(8 * blk, 8 * blk + 8)
                        pb = blk % 2
                        proj_fm(wz, kz, zc0, blk, pb)
                        S.op("act", lambda e: e.activation(out=f1, in_=bank(pb), func=AF.Exp, scale=-1.0), reads=[pk(pb)], writes=["f1"])
                        S.op("act", lambda e: e.activation(out=f1, in_=f1, func=AF.Ln, bias=one_t[:, 0:1], scale=1.0), reads=["f1", "one"], writes=["f1"])
                        S.op("act", lambda e: e.activation(out=f1, in_=f1, func=AF.Exp, scale=-1.0), reads=["f1"], writes=["f1"])
                        S.op("dve", lambda e: e.tensor_scalar(out=f1, in0=f1, scalar1=oml[:, lbi, l:l + 1], scalar2=lbv[:, lbi, l:l + 1],
                                                              op0=ALU.mult, op1=ALU.add), reads=["f1", "oml", "lbv"], writes=["f1"])
                        S.op("dve", lambda e: e.tensor_scalar_max(out=f1, in0=f1, scalar1=1e-6), reads=["f1"], writes=["f1"])
                        S.op("pool", lambda e: e.tensor_scalar(out=kk, in0=f1, scalar1=-1.0, scalar2=1.0, op0=ALU.mult, op1=ALU.add),
                             reads=["f1"], writes=["kk"])
                        S.op("act", lambda e: e.activation(out=lf, in_=f1, func=AF.Ln), reads=["f1"], writes=["lf"])
                        S.op("dve", lambda e: e.tensor_tensor_scan(out=bb, data0=rst[:], data1=lf, initial=0.0, op0=ALU.mult, op1=ALU.add),
                             reads=["lf", "rst"], writes=["bb"])
                        if d == 0:
                            cur, curk = bb, "bb"
                        else:
                            S.op("dve", lambda e: e.tensor_tensor(out=cc, in0=lf, in1=bb, op=ALU.subtract), reads=["lf", "bb"], writes=["cc"])
                            S.op("dve", lambda e: e.tensor_tensor(out=v64(cc), in0=v64(cc), in1=v64(bb)[:, :, 63:64].to_broadcast([128, 8, 64]), op=ALU.add),
                                 reads=["cc", "bb"], writes=["cc"])
                            cur, curk = cc, "cc"
                        c64 = v64(cur)
                        c32 = v32(cur)
                        exi = exc[0] % 3
                        exc[0] += 1
                        S.op("act", lambda e: e.activation(out=exr[exi], in_=cur, func=AF.Exp), reads=[curk], writes=[("ex", exi)])
                        S.op("pool", lambda e: e.tensor_tensor(out=qh[:, tsl], in0=qb_[:, tsl], in1=exr[exi], op=ALU.mult), reads=["qb", ("ex", exi)], writes=["qh"])
                        S.op("dve", lambda e: e.tensor_tensor(out=v32(bm), in0=c32, in1=c32[:, :, r32:r32 + 1].to_broadcast([128, 16, 32]), op=ALU.subtract),
                             reads=[curk], writes=["bm"])
                        S.op("dve", lambda e: e.tensor_scalar(out=bm, in0=bm, scalar1=40.0, scalar2=-40.0, op0=ALU.min, op1=ALU.max),
                             reads=["bm"], writes=["bm"])
                        exi = exc[0] % 3
                        exc[0] += 1
                        S.op("act", lambda e: e.activation(out=exr[exi], in_=bm, func=AF.Exp), reads=["bm"], writes=[("ex", exi)])
                        S.op("pool", lambda e: e.tensor_tensor(out=q1[:, tsl], in0=qb_[:, tsl], in1=exr[exi], op=ALU.mult), reads=["qb", ("ex", exi)], writes=["q1"])
                        exi = exc[0] % 3
                        exc[0] += 1
                        S.op("act", lambda e: e.activation(out=exr[exi], in_=bm, func=AF.Exp, scale=-1.0), reads=["bm"], writes=[("ex", exi)])
                        S.op("pool", lambda e: e.tensor_tensor(out=k1[:, tsl], in0=kk, in1=exr[exi], op=ALU.mult), reads=["kk", ("ex", exi)], writes=["k1"])
                        S.op("dve", lambda e: e.tensor_tensor(out=v64(bm), in0=c64, in1=c64[:, :, r64:r64 + 1].to_broadcast([128, 8, 64]), op=ALU.subtract),
                             reads=[curk], writes=["bm"])
                        exi = exc[0] % 3
                        exc[0] += 1
                        S.op("dve", lambda e: e.tensor_scalar_min(out=exr[exi], in0=bm, scalar1=0.0), reads=["bm"], writes=[("ex", exi)])
                        S.op("act", lambda e: e.activation(out=exr[exi], in_=exr[exi], func=AF.Exp), reads=[("ex", exi)], writes=[("ex", exi)])
                        S.op("pool", lambda e: e.tensor_tensor(out=q2[:, tsl], in0=qb_[:, tsl], in1=exr[exi], op=ALU.mult), reads=["qb", ("ex", exi)], writes=["q2"])
                        exi = exc[0] % 3
                        exc[0] += 1
                        S.op("dve", lambda e: e.tensor_scalar_max(out=exr[exi], in0=bm, scalar1=0.0), reads=["bm"], writes=[("ex", exi)])
                        S.op("act", lambda e: e.activation(out=exr[exi], in_=exr[exi], func=AF.Exp, scale=-1.0), reads=[("ex", exi)], writes=[("ex", exi)])
                        S.op("pool", lambda e: e.tensor_tensor(out=k2[:, tsl], in0=kk, in1=exr[exi], op=ALU.mult), reads=["kk", ("ex", exi)], writes=["k2"])
                        S.op("dve", lambda e: e.tensor_tensor(out=v64(bm), in0=c64[:, :, last:last + 1].to_broadcast([128, 8, 64]), in1=c64, op=ALU.subtract),
                             reads=[curk], writes=["bm"])
                        exi = exc[0] % 3
                        exc[0] += 1
                        S.op("act", lambda e: e.activation(out=exr[exi], in_=bm, func=AF.Exp), reads=["bm"], writes=[("ex", exi)])
                        S.op("pool", lambda e: e.tensor_tensor(out=khT[:, tsl], in0=kk, in1=exr[exi], op=ALU.mult), reads=["kk", ("ex", exi)], writes=["khT"])
                        S.op("act", lambda e: e.activation(out=Dd[:, csl], in_=c64[:, :, last], func=AF.Exp), reads=[curk], writes=["Dd"])
                    for g in range(2):
                        pb = 4 + (g % 2)
                        bkb = bank(pb).bitcast(BF16)
                        for j in range(8):
                            tb = 8 * g + j
                            S.op("pe", lambda e: e.transpose(bkb[:, 128 * j:128 * (j + 1)], khT[:, 128 * tb:128 * (tb + 1)], ident[:]),
                                 reads=["khT", "ident"], writes=[pk(pb)])
                        S.op("act", lambda e: e.activation(out=khtok[:, 8 * g:8 * g + 8, :], in_=bkb.rearrange("p (a d) -> p a d", a=8), func=AF.Copy),
                             reads=[pk(pb)], writes=["khtok"])
                    order = list(range(32)) if d == 0 else list(range(31, -1, -1))

                    def scan_step(idx):
                        n = order[idx]
                        ub = 2 + (idx // 8) % 2
                        ucol = 512 * ub + 64 * (idx % 8)
                        tb, hf = n // 2, n % 2
                        for hh in range(2):
                            S.op("pe", lambda e: e.matmul(P[64 * hh:64 * hh + 64, ucol:ucol + 64], lhsT=khtok[64 * hf:64 * hf + 64, tb, 64 * hh:64 * hh + 64],
                                                          rhs=vtok[64 * hf:64 * hf + 64, tb, 64 * hh:64 * hh + 64], start=True, stop=True),
                                 reads=["khtok", "vtok"], writes=[pk(ub)])
                        cs_, ps_ = S32[idx % 2], S32[(idx + 1) % 2]
                        if idx == 0:
                            S.op("dve", lambda e: e.tensor_copy(out=cs_, in_=P[:, ucol:ucol + 64]), reads=[pk(ub)], writes=[("S32", idx % 2)])
                        else:
                            S.op("dve", lambda e: e.scalar_tensor_tensor(out=cs_, in0=ps_, scalar=Dd[:, n:n + 1], in1=P[:, ucol:ucol + 64],
                                                                         op0=ALU.mult, op1=ALU.add),
                                 reads=[pk(ub), ("S32", (idx + 1) % 2), "Dd"], writes=[("S32", idx % 2)])
                        nxt = n + 1 if d == 0 else n - 1
                        if 0 <= nxt < 32:
                            S.op("act", lambda e: e.activation(out=Sbf[:, nxt, :], in_=cs_, func=AF.Copy),
                                 reads=[("S32", idx % 2)], writes=[("Sbf", nxt)])
                    Js = list(range(16)) if d == 0 else list(range(15, -1, -1))
                    items = [(J // 4, J % 4, hh) for J in Js for hh in range(2)]

                    def b_scores(k):
                        q4, j, hh = items[k]
                        J = 4 * q4 + j
                        po = 64 * hh
                        ai = k % 3
                        for m, (ka, qa, kkey, qkey) in enumerate(((k1, q1, "k1", "q1"), (k2, q2, "k2", "q2"))):
                            sbk = 2 * (k % 2) + m if False else (0, 1, 6, 7)[2 * (k % 2) + m]
                            S.op("pe", lambda e: e.matmul(P[:, 512 * sbk:512 * sbk + 128], lhsT=ka[po:po + 64, 128 * J:128 * (J + 1)],
                                                          rhs=qa[po:po + 64, 128 * J:128 * (J + 1)], start=True, stop=True),
                                 reads=[kkey, qkey], writes=[pk(sbk)])
                            S.op("dve", lambda e: e.tensor_tensor(out=att[m][ai], in0=P[:, 512 * sbk:512 * sbk + 128],
                                                                  in1=bmask[:, 2 * d + m, :], op=ALU.mult),
                                 reads=[pk(sbk), "bmask"], writes=[("att", m, ai)])

                    def b_rest(k):
                        q4, j, hh = items[k]
                        J = 4 * q4 + j
                        po = 64 * hh
                        ai = k % 3
                        ob = 4 + (q4 % 2)
                        inter = []
                        for hf in range(2):
                            n = 2 * J + hf
                            if (d == 0 and n >= 1) or (d == 1 and n <= 30):
                                inter.append((n, hf))
                        c0 = 512 * ob + 128 * j
                        S.op("pe", lambda e: e.matmul(P[po:po + 64, c0:c0 + 128], lhsT=vtok[:, J, po:po + 64], rhs=att[0][ai], start=True, stop=False),
                             reads=["vtok", ("att", 0, ai)], writes=[pk(ob)])
                        S.op("pe", lambda e: e.matmul(P[po:po + 64, c0:c0 + 128], lhsT=vtok[:, J, po:po + 64], rhs=att[1][ai], start=False,
                                                      stop=(len(inter) == 0)),
                             reads=["vtok", ("att", 1, ai)], writes=[pk(ob)])
                        for ii, (n, hf) in enumerate(inter):
                            S.op("pe", lambda e: e.matmul(P[po:po + 64, c0 + 64 * hf:c0 + 64 * hf + 64], lhsT=Sbf[po:po + 64, n, :],
                                                          rhs=qh[po:po + 64, 128 * J + 64 * hf:128 * J + 64 * hf + 64],
                                                          start=False, stop=(ii == len(inter) - 1)),
                                 reads=[("Sbf", n), "qh"], writes=[pk(ob)])
                        if not (j == (3 if d == 0 else 0) and hh == 1):
                            return
                        tsl = slice(512 * q4, 512 * (q4 + 1))
                        if d == 0:
                            S.op("act", lambda e: e.activation(out=o32[:, tsl], in_=bank(ob), func=AF.Copy), reads=[pk(ob)], writes=["o32"])
                            return
                        S.op("dve", lambda e: e.tensor_tensor(out=on, in0=o32[:, tsl], in1=bank(ob), op=ALU.add), reads=[pk(ob), "o32"], writes=["lf"])
                        S.op("act", lambda e: e.activation(out=sqb, in_=on, func=AF.Square), reads=["lf"], writes=["sqb"])
                        S.op("pe", lambda e: e.matmul(bank(2), lhsT=blk64[:], rhs=sqb, start=True, stop=True), reads=["sqb", "blk64"], writes=[pk(2)])
                        S.op("act", lambda e: e.activation(out=rsd, in_=bank(2), func=AF.Ln, bias=eps_t[:, 0:1], scale=1.0), reads=[pk(2), "eps"], writes=["f1"])
                        S.op("act", lambda e: e.activation(out=rsd, in_=rsd, func=AF.Exp, scale=-0.5), reads=["f1"], writes=["f1"])
                        S.op("dve", lambda e: e.scalar_tensor_tensor(out=on, in0=on, scalar=bon[:, l:l + 1], in1=rsd, op0=ALU.mult, op1=ALU.mult),
                             reads=["lf", "bon", "f1"], writes=["lf"])
                        proj_fm(wC, kC, 128 * hp, q4, 3)
                        S.op("act", lambda e: e.activation(out=gg, in_=bank(3), func=AF.Exp, scale=-1.0), reads=[pk(3)], writes=["bb"])
                        S.op("act", lambda e: e.activation(out=gg, in_=gg, func=AF.Ln, bias=one_t[:, 0:1], scale=1.0), reads=["bb", "one"], writes=["bb"])
                        S.op("act", lambda e: e.activation(out=gg, in_=gg, func=AF.Exp, scale=-1.0), reads=["bb"], writes=["bb"])
                        S.op("dve", lambda e: e.tensor_tensor(out=gg, in0=bank(3), in1=gg, op=ALU.mult), reads=[pk(3), "bb"], writes=["bb"])
                        S.op("dve", lambda e: e.tensor_tensor(out=o_b[:, hp, tsl], in0=on, in1=gg, op=ALU.mult), reads=["lf", "bb"], writes=["o_b"])
                    LA = 1
                    for k in range(len(items) + LA):
                        if k < len(items):
                            if k % 2 == 0:
                                scan_step(k)
                                scan_step(k + 1)
                            b_scores(k)
                        if k >= LA:
                            b_rest(k - LA)
            outproj_partial(l, 384, 2, o_b, "o_b", 16)
            S.barrier()

        def ss(start, r):
            return slice(start, start + 127 * r + 1, r)

        def mixer_c(l):
            cv = Carver("C")
            cm = [cv.get(f"cm{i}", [128, 9, 128], BF16) for i in range(2)]
            qn_f = cv.get("qn", [128, 6, T], BF16)
            kn_f = cv.get("kn", [128, 2, T], BF16)
            qn, kn = qn_f[0:64], kn_f[0:64]
            S.op("dve", lambda e: e.memset(qn_f[64:128], 0.0), writes=["qn"])
            S.op("dve", lambda e: e.memset(kn_f[64:128], 0.0), writes=["kn"])
            vx = [cv.get(f"vx{ri}", [128, 16, 128], BF16) for ri in range(3)]
            o_c = cv.get("o_c", [128, 3, T], BF16)
            acc_off = cv.off
            acc = cv.get("acc", [128, T])
            NPB = 4
            pT = [cv.get(f"pT{i}", [128, 384], BF16) for i in range(NPB)]
            pm = [cv.get(f"pm{i}", [128, 384], BF16) for i in range(NPB)]
            den = cv.get("den", [64, 512])
            off1 = cv.off
            cv.off = acc_off
            tsets = [(None, cv.get(f"sq{i}", [128, 512], BF16), cv.get(f"r{i}", [128, 512]), None, None) for i in range(2)]
            cv.off = off1
            wv, wk = wload(d_win[l, :, 1920:2432].rearrange("(c p) n -> p c n", p=128), (8, 512))
            qk_prep(l, wv, wk, 0, 3, 2, qn_f, "qn", tsets)
            qk_prep(l, wv, wk, 384, 1, 3, kn_f, "kn", tsets)
            S.barrier()
            wv2, wk2 = wload(d_win[l, :, 2432:2560].rearrange("(c p) n -> p c n", p=128), (8, 128))
            cmask_d = d_k["cmask"].rearrange("p (h a m) -> p h a m", h=6, a=9)
            it = 0
            si = 0
            pi = 0
            for kv in range(2):
                for ri, r in enumerate(BR):
                    nb = 16 // r
                    S.op("dve", lambda e: e.memset(vx[ri][:, :, 64:128], 1.0), writes=[("vx", ri)])
                    for g in range(2):
                        pb = 4 + (g % 2)
                        for j in range(8):
                            tb = 8 * g + j
                            c, b = tb // nb, tb % nb
                            st_ = 1 + c + r * 128 * b
                            for k in range(8):
                                S.op("pe", lambda e: e.matmul(P[:, 512 * pb + 64 * j:512 * pb + 64 * (j + 1)], lhsT=hT[:, k, ss(st_, r)],
                                                              rhs=wv2[:, k, 64 * kv:64 * (kv + 1)], start=(k == 0), stop=(k == 7)),
                                     reads=[wk2, "hT"], writes=[pk(pb)])
                        S.op("act", lambda e: e.activation(out=vx[ri][:, 8 * g:8 * g + 8, 0:64],
                                                           in_=bank(pb).rearrange("p (a d) -> p a d", a=8), func=AF.Copy),
                             reads=[pk(pb)], writes=[("vx", ri)])
                items = []
                for hh in range(3):
                    h = 3 * kv + hh
                    for ri, r in enumerate(BR):
                        nb = 16 // r
                        for g4 in range(4):
                            ob = 4 + (it % 2)
                            it += 1
                            for jj in range(4):
                                tbq = 4 * g4 + jj
                                c, qb = tbq // nb, tbq % nb
                                kbs = [kb for kb in (qb - 1, qb, qb + 1) if 0 <= kb < nb]
                                items.append(dict(h=h, ri=ri, r=r, nb=nb, g4=g4, jj=jj, c=c, qb=qb, kbs=kbs, ob=ob,
                                                  first_h=(ri == 0 and g4 == 0 and jj == 0), last_g=(jj == 3),
                                                  last_h=(ri == 2 and g4 == 3 and jj == 3), sbk=(0, 1, 2, 3)[si % 4], p_i=si % NPB))
                                si += 1

                def c_scores(I):
                    h, r, c, qb = I["h"], I["r"], I["c"], I["qb"]
                    if I["first_h"]:
                        S.dma("sp", "ld", lambda e: e.dma_start(out=cm[h % 2], in_=cmask_d[:, h, :, :]), writes=[("cm", h % 2)])
                    qs = c + r * 128 * qb
                    for ii, kb in enumerate(I["kbs"]):
                        ks = c + r * 128 * kb
                        S.op("pe", lambda e: e.matmul(P[:, 512 * I["sbk"] + 128 * ii:512 * I["sbk"] + 128 * (ii + 1)], lhsT=kn_f[:, kv, ss(ks, r)],
                                                      rhs=qn_f[:, h, ss(qs, r)], start=True, stop=True), reads=["kn", "qn"], writes=[pk(I["sbk"])])

                def c_rest(I):
                    h, ri, r, nb, g4, jj, c, qb, kbs, ob, sbk, p_i = (I[k] for k in ("h", "ri", "r", "nb", "g4", "jj", "c", "qb", "kbs", "ob", "sbk", "p_i"))
                    nk = len(kbs)
                    d0 = kbs[0] - qb + 1
                    S.op("act", lambda e: e.activation(out=pT[p_i][:, 0:128 * nk], in_=P[:, 512 * sbk:512 * sbk + 128 * nk], func=AF.Exp, scale=0.125),
                         reads=[pk(sbk)], writes=[("cpT", p_i)])
                    S.op("dve", lambda e: e.tensor_tensor(out=pm[p_i][:, 0:128 * nk].rearrange("p (a m) -> p a m", a=nk),
                                                          in0=pT[p_i][:, 0:128 * nk].rearrange("p (a m) -> p a m", a=nk),
                                                          in1=cm[h % 2][:, ri * 3 + d0:ri * 3 + d0 + nk, :], op=ALU.mult),
                         reads=[("cpT", p_i), ("cm", h % 2)], writes=[("cpm", p_i)])
                    for ii, kb in enumerate(kbs):
                        S.op("pe", lambda e: e.matmul(P[:, 512 * ob + 128 * jj:512 * ob + 128 * (jj + 1)], lhsT=vx[ri][:, c * nb + kb, :],
                                                      rhs=pm[p_i][:, 128 * ii:128 * (ii + 1)], start=(ii == 0), stop=(ii == nk - 1)),
                             reads=[("cpm", p_i), ("vx", ri)], writes=[pk(ob)])
                    if I["last_g"]:
                        if ri == 0:
                            S.op("act", lambda e: e.activation(out=acc[:, 512 * g4:512 * (g4 + 1)], in_=bank(ob), func=AF.Copy),
                                 reads=[pk(ob)], writes=["acc"])
                        elif r == 4:
                            av = acc.rearrange("p (i c) -> p c i", c=4)[:, g4, :]
                            S.op("dve", lambda e: e.tensor_tensor(out=av, in0=av, in1=bank(ob), op=ALU.add), reads=[pk(ob), "acc"], writes=["acc"])
                        else:
                            av = acc.rearrange("p (i c) -> p c i", c=16)[:, 4 * g4:4 * g4 + 4, :]
                            S.op("dve", lambda e: e.tensor_tensor(out=av, in0=av, in1=bank(ob).rearrange("p (a i) -> p a i", a=4), op=ALU.add),
                                 reads=[pk(ob), "acc"], writes=["acc"])
                    if I["last_h"]:
                        po = 64 * (h % 2)
                        for blk in range(4):
                            tsl = slice(512 * blk, 512 * (blk + 1))
                            S.op("act", lambda e: e.activation(out=den, in_=acc[64:128, tsl], func=AF.Ln), reads=["acc"], writes=["f_den"])
                            S.op("act", lambda e: e.activation(out=den, in_=den, func=AF.Exp, scale=-1.0), reads=["f_den"], writes=["f_den"])
                            S.op("dve", lambda e: e.tensor_tensor(out=o_c[po:po + 64, h // 2, tsl], in0=acc[0:64, tsl], in1=den, op=ALU.mult),
                                 reads=["acc", "f_den"], writes=["o_c"])
                LA = 2
                for k in range(len(items) + LA):
                    if k < len(items):
                        c_scores(items[k])
                    if k >= LA:
                        c_rest(items[k - LA])
            outproj_partial(l, 640, 3, o_c, "o_c", 16)
            S.barrier()

        def ffn(l):
            wup = d_wup[l].rearrange("(c p) (g n) -> p c g n", p=128, g=2)
            for half in range(2):
                cv = Carver("F")
                gT = cv.get("gT", [128, 22, 1024], BF16)
                cab = [[cv.get(f"c{ab}{i}", [128, 1024]) for i in range(2)] for ab in range(2)]
                sa = [cv.get(f"sa{i}", [128, 1024]) for i in range(2)]
                t0 = 1024 * half
                for jg in range(11):
                    i = wctr[0] % NSLOT
                    wctr[0] += 1
                    wv = wring[i][:, 0:4096].rearrange("p (c g n) -> p c g n", c=8, g=2)
                    wk = ("w", i)
                    for ab in range(2):
                        S.dma("pool", f"w{i}", lambda e: e.dma_start(out=wv[:, :, ab, :], in_=wup[:, :, ab, 256 * jg:256 * (jg + 1)],
                                                                   max_dma_last_dim=4096), writes=[wk])
                    for jj in range(2):
                        j = 2 * jg + jj
                        bi = j % 2
                        for ab, base in ((0, 0), (1, 1536)):
                            bks = [pk(base // 512 + q) for q in range(3)]
                            for q, (c0, n) in enumerate(((0, 512), (512, 512), (1024, 2))):
                                for k in range(8):
                                    S.op("pe", lambda e: e.matmul(P[:, base + c0:base + c0 + n], lhsT=wv[:, k, ab, 128 * jj:128 * (jj + 1)],
                                                                  rhs=hT[:, k, t0 + c0:t0 + c0 + n], start=(k == 0), stop=(k == 7)),
                                         reads=[wk, "hT"], writes=[bks[q]])
                            ch = 22 * ab + j
                            cbuf = cab[ab][bi]
                            ck_ = ("cab", ab, bi)
                            S.op("act", lambda e: e.activation(out=cbuf, in_=P[:, base + 1:base + 1025], func=AF.Identity,
                                                               scale=cw[:, l, 1, ch:ch + 1], bias=cbias[:, l, ch:ch + 1]),
                                 reads=bks + ["cw", "cbias"], writes=[ck_])
                            S.op("dve", lambda e: e.scalar_tensor_tensor(out=cbuf, in0=P[:, base:base + 1024], scalar=cw[:, l, 0, ch:ch + 1],
                                                                         in1=cbuf, op0=ALU.mult, op1=ALU.add),
                                 reads=bks + ["cw", ck_], writes=[ck_])
                            S.op("dve", lambda e: e.scalar_tensor_tensor(out=cbuf, in0=P[:, base + 2:base + 1026], scalar=cw[:, l, 2, ch:ch + 1],
                                                                         in1=cbuf, op0=ALU.mult, op1=ALU.add),
                                 reads=bks + ["cw", ck_], writes=[ck_])
                        S.op("act", lambda e: e.activation(out=sa[bi], in_=cab[0][bi], func=AF.Silu), reads=[("cab", 0, bi)], writes=[("sa", bi)])
                        S.op("dve", lambda e: e.tensor_tensor(out=gT[:, j, :], in0=sa[bi], in1=cab[1][bi], op=ALU.mult),
                             reads=[("sa", bi), ("cab", 1, bi)], writes=["gT"])
                    if l + 1 < depth and jg % 2 == 1 or (l + 1 < depth and jg == 10):
                        og = 6 * half + jg // 2 + (1 if jg == 10 else 0) - (0 if jg != 10 else 1)
                        og = 6 * half + (jg // 2 if jg != 10 else 5)
                        adaln_og(l + 1, og, 3584)
                        if jg == 10:
                            adaln_finish(l + 1, 3584, 24 * half, 24 * half + 24)
                it = 0
                for m in range(8):
                    wv, wk = wload(d_wdn[l, :, 128 * m:128 * (m + 1)].rearrange("(c p) n -> p c n", p=128), (22, 128))
                    for n2 in range(2):
                        pb = 6 + (it % 2)
                        it += 1
                        for j in range(22):
                            S.op("pe", lambda e: e.matmul(bank(pb), lhsT=wv[:, j, :], rhs=gT[:, j, 512 * n2:512 * (n2 + 1)],
                                                          start=(j == 0), stop=(j == 21)), reads=[wk, "gT"], writes=[pk(pb)])
                        tsl = slice(t0 + 512 * n2, t0 + 512 * (n2 + 1))
                        S.op("dve", lambda e: e.scalar_tensor_tensor(out=xT[:, m, tsl], in0=bank(pb), scalar=mod[:, l, 40 + m:41 + m],
                                                                     in1=xT[:, m, tsl], op0=ALU.mult, op1=ALU.add),
                             reads=[pk(pb), "mod", "xT"], writes=["xT"])
                S.barrier()

        for l in range(depth):
            norm_mod(l, 0)
            if do_a:
                mixer_a(l)
            if do_b:
                mixer_b(l)
            if do_c:
                mixer_c(l)
            if do_ffn:
                norm_mod(l, 1)
                ffn(l)

        S.dma("sp", "st", lambda e: e.dma_start(out=d_out.rearrange("(c p) t -> p c t", p=128), in_=xT[:]), reads=["xT"])
        S.emit(final_dsems=["st"])
    return nc


def _prep_shared(inp):
    f = lambda a: np.ascontiguousarray(np.asarray(a, dtype=np.float32))
    sh = {}
    sh["w_ada"] = f(inp["w_ada"])
    sh["b_ada_l"] = f(np.asarray(inp["b_ada"]).reshape(NL, 48, 128).transpose(2, 0, 1))
    sh["norm_g_l"] = f(np.asarray(inp["norm_g"]).reshape(NL, 2, 8, 128).transpose(3, 0, 1, 2))
    sh["w_in"] = f(inp["w_in"])
    hn = np.stack([np.asarray(inp[k]) for k in ("a_q_norm", "a_k_norm", "c_q_norm", "c_k_norm")], -1)
    sh["hn"] = f(np.concatenate([hn.transpose(1, 0, 2)] * 2, 0))
    bo = np.asarray(inp["b_out_norm"])
    sh["bon"] = f(np.concatenate([bo, bo], 1).T)
    bl = np.asarray(inp["b_lb"]).reshape(2, NL, 2, 128)
    sh["b_lb_l"] = f(bl.transpose(3, 0, 2, 1).reshape(128, 4, NL))
    sh["w_out"] = f(inp["w_out"])
    sh["w_up"] = f(inp["w_up"])
    sh["conv_w_l"] = f(np.asarray(inp["conv_w"]).reshape(NL, 3, 44, 128).transpose(3, 0, 1, 2))
    sh["conv_b_l"] = f(np.asarray(inp["conv_b"]).reshape(NL, 44, 128).transpose(2, 0, 1))
    sh["w_down"] = f(inp["w_down"])
    cst = _consts()
    sh["rst"] = cst.pop("rst")
    for k, v in cst.items():
        sh["k_" + k] = np.ascontiguousarray(v)
    return sh


def run(inputs, ncores=8, **bk):
    x = np.asarray(inputs["x"], dtype=np.float32)
    c = np.asarray(inputs["c"], dtype=np.float32)
    nc = build(**bk)
    sh = _prep_shared(inputs)
    in_maps = []
    for b in range(ncores):
        m = dict(sh)
        m["xT"] = np.ascontiguousarray(x[b].T)
        m["c128"] = np.ascontiguousarray(c[b].reshape(8, 128).T)
        in_maps.append(m)
    res = run_bass_kernel_spmd(nc, in_maps, core_ids=list(range(ncores)))
    if bk.get("dbg"):
        return res.results[0]["dbg"]
    return np.stack([np.ascontiguousarray(r["outT"].T) for r in res.results]).astype(np.float32)


def kernel(**inputs):
    return run(inputs, ncores=8)
```

```python
import numpy as np
import ml_dtypes
from contextlib import ExitStack
import concourse.bass as bass
import concourse.mybir as mybir
from concourse.bass_utils import run_bass_kernel_spmd

F32 = mybir.dt.float32
BF16 = mybir.dt.bfloat16
AF = mybir.ActivationFunctionType
ALU = mybir.AluOpType
AX = mybir.AxisListType

ENGS = ("pe", "act", "dve", "pool", "sp")
T = 2048
D = 1024
NL = 4
EPS = 1e-6
SLOPES = [2.0 ** (-8.0 * i / 6) for i in range(1, 7)]
BR = (1, 4, 16)


class _Rec:
    def __getattr__(self, name):
        def f(*a, **k):
            self.call = (name, a, k)
            return None
        return f


class Sched:
    def __init__(self, nc, self_sync=True):
        self.nc = nc
        self.self_sync = self_sync
        self.ops = {e: [] for e in ENGS}
        self.lastw = {}
        self.readers = {}
        self.dma_tot = {}

    def _cur(self, ev):
        if ev[0] == 'D':
            return ('D', ev[1], self.dma_tot[ev[1]])
        return ev

    def _add(self, eng, fn, reads, writes, dma=None, extra=()):
        deps = set(extra)
        for k in reads:
            ev = self.lastw.get(k)
            if ev is not None:
                deps.add(self._cur(ev))
        for k in writes:
            ev = self.lastw.get(k)
            if ev is not None:
                deps.add(self._cur(ev))
            for r in self.readers.get(k, ()):
                deps.add(self._cur(r))
        idx = len(self.ops[eng])
        if dma is not None:
            self.dma_tot[dma] = self.dma_tot.get(dma, 0) + 16
            myev = ('D', dma, None)
        else:
            myev = ('E', eng, idx)
        d2 = set()
        for d in deps:
            if d[0] == 'E' and d[1] == eng:
                if eng == 'pe' or not self.self_sync or dma is not None:
                    continue
            d2.add(d)
        rec = _Rec()
        fn(rec)
        self.ops[eng].append(dict(call=rec.call, deps=d2, dma=dma))
        for k in reads:
            self.readers.setdefault(k, []).append(myev)
        for k in writes:
            self.lastw[k] = myev
            self.readers[k] = []
        return idx

    def op(self, eng, fn, reads=(), writes=()):
        return self._add(eng, fn, list(reads), list(writes))

    def dma(self, eng, dsem, fn, reads=(), writes=()):
        return self._add(eng, fn, list(reads), list(writes), dma=dsem)

    def barrier(self, engs=("pe", "act", "dve", "sp")):
        evs = []
        for e in engs:
            for i in range(len(self.ops[e]) - 1, -1, -1):
                if self.ops[e][i]['dma'] is None:
                    evs.append(('E', e, i))
                    break
        for name in self.dma_tot:
            if not name.startswith("w"):
                evs.append(('D', name, self.dma_tot[name]))
        for e in engs:
            self._add(e, lambda eng: eng.nop(), [], [], extra=[v for v in evs if not (v[0] == 'E' and v[1] == e)])

    def emit(self, final_dsems=()):
        nc = self.nc
        need = {e: set() for e in ENGS}
        for e in ENGS:
            for o in self.ops[e]:
                for d in o['deps']:
                    if d[0] == 'E':
                        need[d[1]].add(d[2])
        LIM = 30000
        count_at = {e: {} for e in ENGS}
        n_epochs = {}
        for e in ENGS:
            c = 0
            ep = 0
            for i in range(len(self.ops[e])):
                if i in need[e]:
                    c += 1
                    if c > LIM:
                        ep += 1
                        c = 1
                    count_at[e][i] = (ep, c)
            n_epochs[e] = ep + 1
        with ExitStack() as st:
            esem = {}
            for e in ENGS:
                for ep in range(n_epochs[e]):
                    esem[(e, ep)] = st.enter_context(nc.semaphore(f"s_{e}_{ep}"))
            dsem = {}
            for name in self.dma_tot:
                dsem[name] = st.enter_context(nc.semaphore(f"d_{name}"))
            block = st.enter_context(nc.Block())
            engobj = {"pe": "tensor", "act": "scalar", "dve": "vector", "pool": "gpsimd", "sp": "sync"}

            def make(e):
                def body(eng):
                    known = {}
                    for i, o in enumerate(self.ops[e]):
                        w = {}
                        for d in o['deps']:
                            if d[0] == 'E':
                                ep, v = count_at[d[1]][d[2]]
                                kk = ('E', d[1], ep)
                                s = esem[(d[1], ep)]
                            else:
                                kk = ('D', d[1])
                                s = dsem[d[1]]
                                v = d[2]
                            if known.get(kk, 0) >= v:
                                continue
                            if kk not in w or w[kk][1] < v:
                                w[kk] = (s, v)
                        for kk, (s, v) in w.items():
                            eng.wait_ge(s, v)
                            known[kk] = v
                        cname, ca, ck = o['call']
                        ins = getattr(eng, cname)(*ca, **ck)
                        if o['dma'] is not None:
                            ins.then_inc(dsem[o['dma']], 16)
                        elif i in need[e]:
                            ep, v = count_at[e][i]
                            ins.then_inc(esem[(e, ep)], 1)
                    if e == 'sp':
                        for name in final_dsems:
                            eng.wait_ge(dsem[name], self.dma_tot[name])
                return body
            for e in ENGS:
                getattr(block, engobj[e])(make(e))


def _consts():
    bf = ml_dtypes.bfloat16
    c = {}
    c["ident"] = np.eye(128, dtype=np.float32).astype(bf)
    c["onesD"] = np.full((128, 128), 1.0 / 1024, np.float32).astype(bf)
    c["ones64"] = np.full((64, 64), 1.0 / 64, np.float32).astype(bf)
    b = np.zeros((128, 128), np.float32)
    b[:64, :64] = 1.0 / 64
    b[64:, 64:] = 1.0 / 64
    c["blk64"] = b.astype(bf)
    R = np.zeros((64, 64), np.float32)
    for d in list(range(0, 16)) + list(range(32, 48)):
        R[d + 16, d] = -1.0
    for d in list(range(16, 32)) + list(range(48, 64)):
        R[d - 16, d] = 1.0
    R2 = np.zeros((128, 128), np.float32)
    R2[:64, :64] = R
    R2[64:, 64:] = R
    c["rrot"] = R2.astype(bf)
    t = np.arange(T)
    row = (t // 64).astype(np.float64)
    col = (t % 64).astype(np.float64)
    inv = 10000.0 ** (-np.arange(0, 32, 2, dtype=np.float64) / 32)
    ang = np.zeros((64, T))
    ang[0:16] = row[None, :] * inv[:, None]
    ang[16:32] = row[None, :] * inv[:, None]
    ang[32:48] = col[None, :] * inv[:, None]
    ang[48:64] = col[None, :] * inv[:, None]
    cs1 = np.stack([np.cos(ang), np.sin(ang)], 1).astype(np.float32)
    c["cossin"] = np.concatenate([cs1, cs1], 0).astype(bf)
    s = np.arange(128)[:, None]
    q = np.arange(128)[None, :]
    same = (s // 64) == (q // 64)
    same32 = (s // 32) == (q // 32)
    s_lo = (s % 64) < 32
    q_lo = (q % 64) < 32
    c["bmask"] = np.stack([(same32 & (s <= q)), (same & s_lo & ~q_lo),
                           (same32 & (s >= q)), (same & ~s_lo & q_lo)], 1).astype(np.float32).astype(bf)
    rst = np.ones((128, 512), np.float32)
    rst[:, ::64] = 0.0
    c["rst"] = rst
    cm = np.zeros((128, 6, 3, 3, 128), np.float32)
    for h in range(6):
        for ri, r in enumerate(BR):
            for di, dl in enumerate((-1, 0, 1)):
                dd = 128 * dl + s - q
                cm[:, h, ri, di, :] = np.where(np.abs(dd) <= 64, np.exp(-SLOPES[h] * r * np.abs(dd)), 0.0)
    c["cmask"] = cm.reshape(128, 6 * 9 * 128).astype(bf)
    return c


_CONST_SHAPES = {"ident": [128, 128], "onesD": [128, 128], "ones64": [64, 64], "blk64": [128, 128], "rrot": [128, 128],
                 "cossin": [128, 2, T], "bmask": [128, 4, 128], "cmask": [128, 6 * 9 * 128]}


def build(depth=NL, do_a=True, do_b=True, do_c=True, do_ffn=True, self_sync=True, dbg=None):
    nc = bass.Bass("TRN2", target_bir_lowering=False)

    def dram(name, shape, dt=F32, kind="ExternalInput"):
        return nc.dram_tensor(name, list(shape), dt, kind=kind).ap()

    d_xT = dram("xT", [D, T])
    d_out = dram("outT", [D, T], kind="ExternalOutput")
    d_c = dram("c128", [128, 8])
    d_wada = dram("w_ada", [NL, D, 6 * D])
    d_bada = dram("b_ada_l", [128, NL, 48])
    d_ng = dram("norm_g_l", [128, NL, 2, 8])
    d_win = dram("w_in", [NL, D, 2560])
    d_hn = dram("hn", [128, NL, 4])
    d_bon = dram("bon", [128, NL])
    d_blb = dram("b_lb_l", [128, 4, NL])
    d_wout = dram("w_out", [NL, D, D])
    d_wup = dram("w_up", [NL, D, 5632])
    d_cw = dram("conv_w_l", [128, NL, 3, 44])
    d_cb = dram("conv_b_l", [128, NL, 44])
    d_wdn = dram("w_down", [NL, 2816, D])
    d_rst = dram("rst", [128, 512])
    d_k = {k: dram("k_" + k, v, BF16) for k, v in _CONST_SHAPES.items()}

    S = Sched(nc, self_sync=self_sync)
    st = ExitStack()
    with st:
        def sb(name, shape, dt=F32):
            return st.enter_context(nc.sbuf_tensor(name, list(shape), dt))

        xT = sb("xT_sb", [128, 8, T])
        hT = sb("hT_sb", [128, 8, T + 2], BF16)
        NSLOT = 2 if dbg else 3
        wring = [sb(f"wring{i}", [128, 4096], BF16) for i in range(NSLOT)]
        mod = sb("mod", [128, NL, 48])
        bada = sb("bada", [128, NL, 48])
        ng = sb("ng", [128, NL, 2, 8])
        gs = sb("gs", [128, NL, 2, 8])
        cw = sb("cw", [128, NL, 3, 44])
        cbias = sb("cbias", [128, NL, 44])
        hn = sb("hn_sb", [128, NL, 4])
        bon = sb("bon_sb", [128, NL])
        blb = sb("blb", [128, 4, NL])
        lbv = sb("lbv", [128, 4, NL])
        oml = sb("oml", [128, 4, NL])
        lbs = sb("lbs", [128, 4])
        c_sb = sb("c_sb", [128, 8])
        sc_f = sb("sc_f", [128, 8])
        sc_b = sb("sc_b", [128, 8], BF16)
        ident = sb("ident", [128, 128], BF16)
        onesD = sb("onesD", [128, 128], BF16)
        ones64 = sb("ones64", [64, 64], BF16)
        blk64 = sb("blk64", [128, 128], BF16)
        rrot = sb("rrot", [128, 128], BF16)
        bmask = sb("bmask", [128, 4, 128], BF16)
        rst = sb("rst_sb", [128, 512])
        eps_t = sb("eps_t", [128, 1])
        one_t = sb("one_t", [128, 1])
        if dbg:
            d_dbg = dram("dbg", [128, T], kind="ExternalOutput")
            dbgt = sb("dbgt", [128, T])
            S.op("dve", lambda e: e.memset(dbgt[:], 0.0), writes=["dbgt"])

        def tap(name, ap, key, parts=128, n=T):
            if dbg != name:
                return
            S.op("act", lambda e: e.activation(out=dbgt[0:parts, 0:n], in_=ap, func=AF.Copy), reads=[key, "dbgt"], writes=["dbgt"])
            S.dma("sp", "st", lambda e: e.dma_start(out=d_dbg, in_=dbgt[:]), reads=["dbgt"])
        RW = nc.sbuf_bytes_remaining // 4 - 64
        assert RW * 4 >= 70000, RW
        R = sb("R", [128, RW])
        P = st.enter_context(nc.psum_tensor("P", [128, 4096], F32))

        def bank(i):
            return P[:, 512 * i:512 * (i + 1)]

        def pk(i):
            return ("ps", i)

        class Carver:
            def __init__(self, tag):
                self.off = 0
                self.tag = tag

            def get(self, name, shape, dt=F32, parts=128):
                n = int(np.prod(shape[1:]))
                nb = n * (4 if dt == F32 else 2)
                nb4 = (nb + 3) // 4
                ap = R[0:shape[0], self.off:self.off + nb4]
                if dt == BF16:
                    ap = ap.bitcast(BF16)[:, 0:n]
                self.off += nb4
                assert self.off <= RW, (self.tag, name, self.off * 4)
                if len(shape) == 3:
                    ap = ap.rearrange("p (a b) -> p a b", a=shape[1])
                elif len(shape) == 4:
                    ap = ap.rearrange("p (a b c) -> p a b c", a=shape[1], b=shape[2])
                return ap

        wctr = [0]

        def wload(src, shape):
            i = wctr[0] % NSLOT
            wctr[0] += 1
            a, b = shape
            view = wring[i][:, 0:a * b].rearrange("p (a b) -> p a b", a=a)
            key = ("w", i)
            S.dma("pool", f"w{i}", lambda e: e.dma_start(out=view, in_=src, max_dma_last_dim=4096), writes=[key])
            return view, key

        def ld(dst, src, key):
            S.dma("sp", "ld", lambda e: e.dma_start(out=dst, in_=src), writes=[key])

        ld(xT[:], d_xT.rearrange("(c p) t -> p c t", p=128), "xT")
        for nm, dst, src in (("bada", bada, d_bada), ("ng", ng, d_ng), ("cw", cw, d_cw), ("cbias", cbias, d_cb),
                             ("hn", hn, d_hn), ("bon", bon, d_bon), ("blb", blb, d_blb), ("c", c_sb, d_c),
                             ("ident", ident, d_k["ident"]), ("onesD", onesD, d_k["onesD"]),
                             ("ones64", ones64, d_k["ones64"]), ("blk64", blk64, d_k["blk64"]),
                             ("rrot", rrot, d_k["rrot"]), ("bmask", bmask, d_k["bmask"]), ("rst", rst, d_rst)):
            ld(dst[:], src, nm)
        S.op("dve", lambda e: e.memset(eps_t[:], EPS), writes=["eps"])
        S.op("dve", lambda e: e.memset(one_t[:], 1.0), writes=["one"])
        S.op("dve", lambda e: e.memset(hT[:, :, 0:1], 0.0), writes=["hT"])
        S.op("dve", lambda e: e.memset(hT[:, :, T + 1:T + 2], 0.0), writes=["hT"])

        S.op("act", lambda e: e.activation(out=lbv[:], in_=blb[:], func=AF.Exp), reads=["blb"], writes=["lbv"])
        S.op("dve", lambda e: e.tensor_reduce(out=lbs[:], in_=lbv[:], axis=AX.X, op=ALU.add), reads=["lbv"], writes=["lbs"])
        S.op("dve", lambda e: e.reciprocal(out=lbs[:], in_=lbs[:]), reads=["lbs"], writes=["lbs"])
        for l in range(NL):
            S.op("dve", lambda e, l=l: e.tensor_tensor(out=lbv[:, :, l], in0=lbv[:, :, l], in1=lbs[:], op=ALU.mult),
                 reads=["lbv", "lbs"], writes=["lbv"])
        S.op("dve", lambda e: e.memset(lbv[:, :, 0:1], 0.0), reads=["lbv"], writes=["lbv"])
        for l in range(2, NL):
            S.op("dve", lambda e, l=l: e.tensor_tensor(out=lbv[:, :, l], in0=lbv[:, :, l], in1=lbv[:, :, l - 1], op=ALU.add),
                 reads=["lbv"], writes=["lbv"])
        S.op("dve", lambda e: e.tensor_scalar(out=oml[:], in0=lbv[:], scalar1=-1.0, scalar2=1.0, op0=ALU.mult, op1=ALU.add),
             reads=["lbv"], writes=["oml"])

        S.op("act", lambda e: e.activation(out=sc_f[:], in_=c_sb[:], func=AF.Exp, scale=-1.0), reads=["c"], writes=["scf"])
        S.op("dve", lambda e: e.tensor_scalar_add(out=sc_f[:], in0=sc_f[:], scalar1=1.0), reads=["scf"], writes=["scf"])
        S.op("dve", lambda e: e.reciprocal(out=sc_f[:], in_=sc_f[:]), reads=["scf"], writes=["scf"])
        S.op("dve", lambda e: e.tensor_tensor(out=sc_b[:], in0=sc_f[:], in1=c_sb[:], op=ALU.mult), reads=["scf", "c"], writes=["scb"])
        def adaln_og(l, og, c00):
            wv, wk = wload(d_wada[l, :, 512 * og:512 * (og + 1)].rearrange("(c p) n -> p c n", p=128), (8, 512))
            for m in range(4):
                col = c00 + og * 4 + m
                for k in range(8):
                    S.op("pe", lambda e: e.matmul(P[:, col:col + 1], lhsT=wv[:, k, 128 * m:128 * (m + 1)], rhs=sc_b[:, k:k + 1],
                                                  start=(k == 0), stop=(k == 7)), reads=[wk, "scb"], writes=[pk(c00 // 512)])

        def adaln_finish(l, c00, lo=0, hi=48):
            S.op("dve", lambda e: e.tensor_tensor(out=mod[:, l, lo:hi], in0=P[:, c00 + lo:c00 + hi], in1=bada[:, l, lo:hi], op=ALU.add),
                 reads=[pk(c00 // 512), "bada"], writes=["mod"])
            if hi < 48:
                return
            for w_, c0 in ((0, 8), (1, 32)):
                S.op("dve", lambda e: e.scalar_tensor_tensor(
                    out=gs[:, l, w_, :], in0=mod[:, l, c0:c0 + 8], scalar=1.0, in1=ng[:, l, w_, :], op0=ALU.add, op1=ALU.mult),
                    reads=["mod", "ng"], writes=["gs"])

        for og in range(12):
            adaln_og(0, og, 0)
        adaln_finish(0, 0)

        def norm_mod(l, w_):
            S.barrier()
            cv = Carver("norm")
            NB = 4
            sq = [cv.get(f"sq{i}", [128, 512], BF16) for i in range(NB)]
            rs = [cv.get(f"rs{i}", [128, 512]) for i in range(2)]
            tm = [cv.get(f"tm{i}", [128, 512]) for i in range(NB)]
            sh0 = 0 if w_ == 0 else 24
            cnt = 0
            for blk in range(4):
                tsl = slice(512 * blk, 512 * (blk + 1))
                pb = 6 + (blk % 2)
                for c in range(8):
                    i = (cnt + c) % NB
                    S.op("pool" if c % 2 == 0 else "dve", lambda e: e.tensor_tensor(out=sq[i], in0=xT[:, c, tsl], in1=xT[:, c, tsl], op=ALU.mult),
                         reads=["xT"], writes=[("nsq", i)])
                    S.op("pe", lambda e: e.matmul(bank(pb), lhsT=onesD[:], rhs=sq[i], start=(c == 0), stop=(c == 7)),
                         reads=[("nsq", i), "onesD"], writes=[pk(pb)])
                r = rs[blk % 2]
                S.op("act", lambda e: e.activation(out=r, in_=bank(pb), func=AF.Ln, bias=eps_t[:, 0:1], scale=1.0),
                     reads=[pk(pb), "eps"], writes=[("nrs", blk % 2)])
                S.op("act", lambda e: e.activation(out=r, in_=r, func=AF.Exp, scale=-0.5),
                     reads=[("nrs", blk % 2)], writes=[("nrs", blk % 2)])
                for c in range(8):
                    i = (cnt + c) % NB
                    S.op("dve", lambda e: e.scalar_tensor_tensor(
                        out=tm[i], in0=xT[:, c, tsl], scalar=gs[:, l, w_, c:c + 1], in1=r, op0=ALU.mult, op1=ALU.mult),
                        reads=["xT", "gs", ("nrs", blk % 2)], writes=[("ntm", i)])
                    S.op("act", lambda e: e.activation(
                        out=hT[:, c, 1 + 512 * blk:1 + 512 * (blk + 1)], in_=tm[i], func=AF.Identity,
                        bias=mod[:, l, sh0 + c:sh0 + c + 1], scale=1.0),
                        reads=[("ntm", i), "mod"], writes=["hT"])
                cnt += 8
            S.barrier()
            tap("mod", mod[:, l, :], "mod", n=48)
            tap("hT0", hT[:, 0, 1:T + 1], "hT")
            tap("hT7", hT[:, 7, 1:T + 1], "hT")

        def proj_fm(wv, wk, c0, blk, pb):
            for k in range(8):
                S.op("pe", lambda e, k=k: e.matmul(bank(pb), lhsT=wv[:, k, c0:c0 + 128],
                                                   rhs=hT[:, k, 1 + 512 * blk:1 + 512 * (blk + 1)],
                                                   start=(k == 0), stop=(k == 7)),
                     reads=[wk, "hT"], writes=[pk(pb)])

        def outproj_partial(l, row0, nk, src, srckey, g_col0):
            wv, wk = wload(d_wout[l, row0:row0 + 128 * nk, :].rearrange("(c p) n -> p c n", p=128), (nk, 1024))
            i = 0
            for m in range(8):
                for blk in range(4):
                    pb = 6 + (i % 2)
                    i += 1
                    tsl = slice(512 * blk, 512 * (blk + 1))
                    for k in range(nk):
                        S.op("pe", lambda e, k=k, m=m, pb=pb, tsl=tsl: e.matmul(
                            bank(pb), lhsT=wv[:, k, 128 * m:128 * (m + 1)], rhs=src[:, k, tsl], start=(k == 0), stop=(k == nk - 1)),
                            reads=[wk, srckey], writes=[pk(pb)])
                    S.op("dve", lambda e, m=m, pb=pb, tsl=tsl: e.scalar_tensor_tensor(
                        out=xT[:, m, tsl], in0=bank(pb), scalar=mod[:, l, g_col0 + m:g_col0 + m + 1], in1=xT[:, m, tsl],
                        op0=ALU.mult, op1=ALU.add), reads=[pk(pb), "mod", "xT"], writes=["xT"])

        prep_ctr = [0]

        def qk_prep(l, wv, wk, c0, nchunks, gcol, dst_f, dstkey, tsets, cs=None):
            for hc in range(nchunks):
                for blk in range(4):
                    tsl = slice(512 * blk, 512 * (blk + 1))
                    i = prep_ctr[0] % 2
                    prep_ctr[0] += 1
                    qg, sq, r, t1, t2 = tsets[i]
                    pb, mb, rb = i, 2 + i, 4 + i
                    proj_fm(wv, wk, c0 + 128 * hc, blk, pb)
                    S.op("act", lambda e: e.activation(out=sq, in_=bank(pb), func=AF.Square), reads=[pk(pb)], writes=[("p_sq", i)])
                    S.op("pe", lambda e: e.matmul(bank(mb), lhsT=blk64[:], rhs=sq, start=True, stop=True),
                         reads=[("p_sq", i), "blk64"], writes=[pk(mb)])
                    if cs is not None:
                        S.op("act", lambda e: e.activation(out=qg, in_=bank(pb), func=AF.Identity, scale=hn[:, l, gcol:gcol + 1]),
                             reads=[pk(pb), "hn"], writes=[("p_qg", i)])
                        S.op("pe", lambda e: e.matmul(bank(rb), lhsT=rrot[:], rhs=qg, start=True, stop=True),
                             reads=[("p_qg", i), "rrot"], writes=[pk(rb)])
                    S.op("act", lambda e: e.activation(out=r, in_=bank(mb), func=AF.Ln, bias=eps_t[:, 0:1], scale=1.0),
                         reads=[pk(mb), "eps"], writes=[("p_r", i)])
                    S.op("act", lambda e: e.activation(out=r, in_=r, func=AF.Exp, scale=-0.5), reads=[("p_r", i)], writes=[("p_r", i)])
                    if cs is not None:
                        S.op("dve", lambda e: e.tensor_tensor(out=t1, in0=qg, in1=cs[:, 0, tsl], op=ALU.mult),
                             reads=[("p_qg", i), "cs"], writes=[("p_t1", i)])
                        S.op("dve", lambda e: e.tensor_tensor(out=t2, in0=bank(rb), in1=cs[:, 1, tsl], op=ALU.mult),
                             reads=[pk(rb), "cs"], writes=[("p_t2", i)])
                        S.op("dve", lambda e: e.tensor_tensor(out=t1, in0=t1, in1=t2, op=ALU.add),
                             reads=[("p_t1", i), ("p_t2", i)], writes=[("p_t1", i)])
                        for hh in range(2):
                            S.op("dve", lambda e: e.tensor_tensor(out=dst_f[0:64, 2 * hc + hh, tsl], in0=t1[64 * hh:64 * hh + 64],
                                                                  in1=r[64 * hh:64 * hh + 64], op=ALU.mult),
                                 reads=[("p_t1", i), ("p_r", i)], writes=[dstkey])
                    else:
                        for hh in range(2):
                            S.op("dve", lambda e: e.scalar_tensor_tensor(
                                out=dst_f[0:64, 2 * hc + hh, tsl], in0=P[64 * hh:64 * hh + 64, 512 * pb:512 * (pb + 1)],
                                scalar=hn[64 * hh:64 * hh + 64, l, gcol:gcol + 1], in1=r[64 * hh:64 * hh + 64], op0=ALU.mult, op1=ALU.mult),
                                reads=[pk(pb), "hn", ("p_r", i)], writes=[dstkey])

        def build_vext(wv, wk, c0, vext, r):
            S.op("dve", lambda e: e.memset(vext[:, :, :, 64:128], 1.0), writes=["vext"])
            for g4 in range(4):
                pb = 4 + (g4 % 2)
                for j in range(4):
                    tb = 4 * g4 + j
                    nb = 16 // r
                    c, b = tb // nb, tb % nb
                    start = 1 + c + r * 128 * b
                    for k in range(8):
                        S.op("pe", lambda e, k=k, j=j, pb=pb, start=start: e.matmul(
                            P[:, 512 * pb + 128 * j:512 * pb + 128 * (j + 1)],
                            lhsT=hT[:, k, start:start + 128 * r:r] if r > 1 else hT[:, k, start:start + 128],
                            rhs=wv[:, k, c0:c0 + 128], start=(k == 0), stop=(k == 7)),
                            reads=[wk, "hT"], writes=[pk(pb)])
                S.op("act", lambda e, g4=g4, pb=pb: e.activation(
                    out=vext[:, 4 * g4:4 * g4 + 4, :, 0:64],
                    in_=bank(pb).rearrange("p (a k d) -> p a k d", a=4, k=2), func=AF.Copy),
                    reads=[pk(pb)], writes=["vext"])

        def finalize_head(src_num, src_den, srckeys, dst_fn, dstkey, tmp):
            den, tmpo = tmp
            for blk in range(4):
                S.op("act", lambda e, blk=blk: e.activation(out=den, in_=src_den(blk), func=AF.Copy), reads=srckeys(blk), writes=["f_den"])
                S.op("dve", lambda e: e.reciprocal(out=den, in_=den), reads=["f_den"], writes=["f_den"])
                S.op("dve", lambda e, blk=blk: e.tensor_tensor(out=tmpo, in0=src_num(blk), in1=den, op=ALU.mult),
                     reads=srckeys(blk) + ["f_den"], writes=["f_tmp"])
                S.op("act", lambda e, blk=blk: e.activation(out=dst_fn(blk), in_=tmpo, func=AF.Copy), reads=["f_tmp"], writes=[dstkey])

        def mixer_a(l):
            cv = Carver("A")
            qr_f = cv.get("qr", [128, 6, T], BF16)
            kr_f = cv.get("kr", [128, 2, T], BF16)
            qr, kr = qr_f[0:64], kr_f[0:64]
            S.op("dve", lambda e: e.memset(qr_f[64:128], 0.0), writes=["qr"])
            S.op("dve", lambda e: e.memset(kr_f[64:128], 0.0), writes=["kr"])
            vext = cv.get("vext", [128, 16, 2, 128], BF16)
            o_a = cv.get("o_a", [128, 3, T], BF16)
            off0 = cv.off
            cs = cv.get("cs", [128, 2, T], BF16)
            tsets = [(cv.get(f"qg{i}", [128, 512], BF16), cv.get(f"sq{i}", [128, 512], BF16), cv.get(f"r{i}", [128, 512]),
                      cv.get(f"t1{i}", [128, 512]), cv.get(f"t2{i}", [128, 512])) for i in range(2)]
            S.dma("sp", "ld", lambda e: e.dma_start(out=cs, in_=d_k["cossin"]), writes=["cs"])
            wv, wk = wload(d_win[l, :, 0:512].rearrange("(c p) n -> p c n", p=128), (8, 512))
            qk_prep(l, wv, wk, 0, 3, 0, qr_f, "qr", tsets, cs=cs)
            qk_prep(l, wv, wk, 384, 1, 1, kr_f, "kr", tsets, cs=cs)
            wv2, wk2 = wload(d_win[l, :, 512:640].rearrange("(c p) n -> p c n", p=128), (8, 128))
            build_vext(wv2, wk2, 0, vext, 1)
            tap("qr0", qr[:, 0, :], "qr", parts=64)
            tap("qr5", qr[:, 5, :], "qr", parts=64)
            tap("kr1", kr[:, 1, :], "kr", parts=64)
            S.barrier()
            cv.off = off0
            pT = [cv.get(f"pT{i}", [128, 1024], BF16) for i in range(2)]
            ftmp = [cv.get(f"den{i}", [64, 512]) for i in range(2)]
            pending = []
            it = 0
            for h in range(6):
                kv = h // 3
                for qb in range(4):
                    qsl = slice(512 * qb, 512 * (qb + 1))
                    ob = 4 + (it % 2)
                    it += 1

                    def score(pp):
                        for u in range(2):
                            kc = 2 * pp + u
                            sbk = 2 * (pp % 2) + u
                            S.op("pe", lambda e: e.matmul(bank(sbk), lhsT=kr_f[:, kv, 128 * kc:128 * (kc + 1)], rhs=qr_f[:, h, qsl],
                                                          start=True, stop=True), reads=["kr", "qr"], writes=[pk(sbk)])

                    def pv(pp):
                        sb0 = 2 * (pp % 2)
                        pi = pp % 2
                        S.op("act", lambda e: e.activation(out=pT[pi], in_=P[:, 512 * sb0:512 * sb0 + 1024], func=AF.Exp, scale=0.125),
                             reads=[pk(sb0), pk(sb0 + 1)], writes=[("pT", pi)])
                        for u in range(2):
                            kc = 2 * pp + u
                            S.op("pe", lambda e: e.matmul(bank(ob), lhsT=vext[:, kc, kv, :], rhs=pT[pi][:, 512 * u:512 * (u + 1)],
                                                          start=(kc == 0), stop=(kc == 15)),
                                 reads=[("pT", pi), "vext"], writes=[pk(ob)])
                    def fin(ob=ob, h=h, qsl=qsl, fi=it % 2):
                        po = 64 * (h % 2)
                        S.op("dve", lambda e: e.reciprocal(out=ftmp[fi], in_=P[64:128, 512 * ob:512 * (ob + 1)]),
                             reads=[pk(ob)], writes=[("f_den", fi)])
                        S.op("dve", lambda e: e.tensor_tensor(out=o_a[po:po + 64, h // 2, qsl], in0=P[0:64, 512 * ob:512 * (ob + 1)], in1=ftmp[fi],
                                                              op=ALU.mult), reads=[pk(ob), ("f_den", fi)], writes=["o_a"])
                    score(0)
                    score(1)
                    for pp in range(8):
                        pv(pp)
                        if pp + 2 < 8:
                            score(pp + 2)
                        if pp == 1 and pending:
                            pending.pop()()
                    pending.append(fin)
            pending.pop()()
            tap("oa0", o_a[:, 0, :], "o_a")
            tap("oa2", o_a[:, 2, :], "o_a")
            outproj_partial(l, 0, 3, o_a, "o_a", 16)
            S.barrier()


        def mixer_b(l):
            cv = Carver("B")
            o_b = cv.get("o_b", [128, 2, T], BF16)
            qb_ = cv.get("qb", [128, T], BF16)
            qh = cv.get("qh", [128, T], BF16)
            q1 = cv.get("q1", [128, T], BF16)
            q2 = cv.get("q2", [128, T], BF16)
            k1 = cv.get("k1", [128, T], BF16)
            k2 = cv.get("k2", [128, T], BF16)
            vtok = cv.get("vtok", [128, 16, 128], BF16)
            khT = cv.get("khT", [128, T], BF16)
            khtok = cv.get("khtok", [128, 16, 128], BF16)
            Sbf = cv.get("Sbf", [128, 32, 64], BF16)
            Dd = cv.get("Dd", [128, 32])
            S32 = [cv.get(f"S32{i}", [128, 64]) for i in range(2)]
            f1 = cv.get("f1", [128, 512])
            lf = cv.get("lf", [128, 512])
            bb = cv.get("bb", [128, 512])
            cc = cv.get("cc", [128, 512])
            bm = cv.get("bm", [128, 512])
            exr = [cv.get(f"ex{i}", [128, 512]) for i in range(3)]
            exc = [0]
            kk = cv.get("kk", [128, 512], BF16)
            att = [[cv.get(f"att{m}{i}", [128, 128], BF16) for i in range(3)] for m in range(2)]
            o32 = cv.get("o32", [128, T], BF16)
            sqb = cv.get("sqb", [128, 512], BF16)
            rsd, on, gg = f1, lf, bb
            wA, kA = wload(d_win[l, :, 640:1152].rearrange("(c p) n -> p c n", p=128), (8, 512))
            wB, kB = wload(d_win[l, :, 1152:1664].rearrange("(c p) n -> p c n", p=128), (8, 512))
            wC, kC = wload(d_win[l, :, 1664:1920].rearrange("(c p) n -> p c n", p=128), (8, 256))
            v64 = lambda ap: ap.rearrange("p (n i) -> p n i", i=64)
            v32 = lambda ap: ap.rearrange("p (n i) -> p n i", i=32)
            si = 0
            for hp in range(2):
                for blk in range(4):
                    pb = blk % 2
                    proj_fm(wA, kA, 128 * hp, blk, pb)
                    S.op("act", lambda e: e.activation(out=qb_[:, 512 * blk:512 * (blk + 1)], in_=bank(pb), func=AF.Copy),
                         reads=[pk(pb)], writes=["qb"])
                for g4 in range(4):
                    pb = 4 + (g4 % 2)
                    for j in range(4):
                        tb = 4 * g4 + j
                        for k in range(8):
                            S.op("pe", lambda e: e.matmul(P[:, 512 * pb + 128 * j:512 * pb + 128 * (j + 1)], lhsT=hT[:, k, 1 + 128 * tb:1 + 128 * (tb + 1)],
                                                          rhs=wB[:, k, 256 + 128 * hp:256 + 128 * (hp + 1)], start=(k == 0), stop=(k == 7)),
                                 reads=[kB, "hT"], writes=[pk(pb)])
                    S.op("act", lambda e: e.activation(out=vtok[:, 4 * g4:4 * g4 + 4, :], in_=bank(pb).rearrange("p (a d) -> p a d", a=4), func=AF.Copy),
                         reads=[pk(pb)], writes=["vtok"])
                for d in range(2):
                    wz, kz, zc0 = (wA, kA, 256 + 128 * hp) if d == 0 else (wB, kB, 128 * hp)
                    lbi = d * 2 + hp
                    r32, r64, last = (15, 31, 63) if d == 0 else (16, 32, 0)
                    for blk in range(4):
                        tsl = slice(512 * blk, 512 * (blk + 1))
                        csl = slice(8 * blk, 8 * blk + 8)
                        pb = blk % 2
                        proj_fm(wz, kz, zc0, blk, pb)
                        S.op("act", lambda e: e.activation(out=f1, in_=bank(pb), func=AF.Exp, scale=-1.0), reads=[pk(pb)], writes=["f1"])
                        S.op("act", lambda e: e.activation(out=f1, in_=f1, func=AF.Ln, bias=one_t[:, 0:1], scale=1.0), reads=["f1", "one"], writes=["f1"])
                        S.op("act", lambda e: e.activation(out=f1, in_=f1, func=AF.Exp, scale=-1.0), reads=["f1"], writes=["f1"])
                        S.op("dve", lambda e: e.tensor_scalar(out=f1, in0=f1, scalar1=oml[:, lbi, l:l + 1], scalar2=lbv[:, lbi, l:l + 1],
                                                              op0=ALU.mult, op1=ALU.add), reads=["f1", "oml", "lbv"], writes=["f1"])
                        S.op("dve", lambda e: e.tensor_scalar_max(out=f1, in0=f1, scalar1=1e-6), reads=["f1"], writes=["f1"])
                        S.op("pool", lambda e: e.tensor_scalar(out=kk, in0=f1, scalar1=-1.0, scalar2=1.0, op0=ALU.mult, op1=ALU.add),
                             reads=["f1"], writes=["kk"])
                        S.op("act", lambda e: e.activation(out=lf, in_=f1, func=AF.Ln), reads=["f1"], writes=["lf"])
                        S.op("dve", lambda e: e.tensor_tensor_scan(out=bb, data0=rst[:], data1=lf, initial=0.0, op0=ALU.mult, op1=ALU.add),
                             reads=["lf", "rst"], writes=["bb"])
                        if d == 0:
                            cur, curk = bb, "bb"
                        else:
                            S.op("dve", lambda e: e.tensor_tensor(out=cc, in0=lf, in1=bb, op=ALU.subtract), reads=["lf", "bb"], writes=["cc"])
                            S.op("dve", lambda e: e.tensor_tensor(out=v64(cc), in0=v64(cc), in1=v64(bb)[:, :, 63:64].to_broadcast([128, 8, 64]), op=ALU.add),
                                 reads=["cc", "bb"], writes=["cc"])
                            cur, curk = cc, "cc"
                        c64 = v64(cur)
                        c32 = v32(cur)
                        exi = exc[0] % 3
                        exc[0] += 1
                        S.op("act", lambda e: e.activation(out=exr[exi], in_=cur, func=AF.Exp), reads=[curk], writes=[("ex", exi)])
                        S.op("pool", lambda e: e.tensor_tensor(out=qh[:, tsl], in0=qb_[:, tsl], in1=exr[exi], op=ALU.mult), reads=["qb", ("ex", exi)], writes=["qh"])
                        S.op("dve", lambda e: e.tensor_tensor(out=v32(bm), in0=c32, in1=c32[:, :, r32:r32 + 1].to_broadcast([128, 16, 32]), op=ALU.subtract),
                             reads=[curk], writes=["bm"])
                        S.op("dve", lambda e: e.tensor_scalar(out=bm, in0=bm, scalar1=40.0, scalar2=-40.0, op0=ALU.min, op1=ALU.max),
                             reads=["bm"], writes=["bm"])
                        exi = exc[0] % 3
                        exc[0] += 1
                        S.op("act", lambda e: e.activation(out=exr[exi], in_=bm, func=AF.Exp), reads=["bm"], writes=[("ex", exi)])
                        S.op("pool", lambda e: e.tensor_tensor(out=q1[:, tsl], in0=qb_[:, tsl], in1=exr[exi], op=ALU.mult), reads=["qb", ("ex", exi)], writes=["q1"])
                        exi = exc[0] % 3
                        exc[0] += 1
                        S.op("act", lambda e: e.activation(out=exr[exi], in_=bm, func=AF.Exp, scale=-1.0), reads=["bm"], writes=[("ex", exi)])
                        S.op("pool", lambda e: e.tensor_tensor(out=k1[:, tsl], in0=kk, in1=exr[exi], op=ALU.mult), reads=["kk", ("ex", exi)], writes=["k1"])
                        S.op("dve", lambda e: e.tensor_tensor(out=v64(bm), in0=c64, in1=c64[:, :, r64:r64 + 1].to_broadcast([128, 8, 64]), op=ALU.subtract),
                             reads=[curk], writes=["bm"])
                        exi = exc[0] % 3
                        exc[0] += 1
                        S.op("dve", lambda e: e.tensor_scalar_min(out=exr[exi], in0=bm, scalar1=0.0), reads=["bm"], writes=[("ex", exi)])
                        S.op("act", lambda e: e.activation(out=exr[exi], in_=exr[exi], func=AF.Exp), reads=[("ex", exi)], writes=[("ex", exi)])
                        S.op("pool", lambda e: e.tensor_tensor(out=q2[:, tsl], in0=qb_[:, tsl], in1=exr[exi], op=ALU.mult), reads=["qb", ("ex", exi)], writes=["q2"])
                        exi = exc[0] % 3
                        exc[0] += 1
                        S.op("dve", lambda e: e.tensor_scalar_max(out=exr[exi], in0=bm, scalar1=0.0), reads=["bm"], writes=[("ex", exi)])
                        S.op("act", lambda e: e.activation(out=exr[exi], in_=exr[exi], func=AF.Exp, scale=-1.0), reads=[("ex", exi)], writes=[("ex", exi)])
                        S.op("pool", lambda e: e.tensor_tensor(out=k2[:, tsl], in0=kk, in1=exr[exi], op=ALU.mult), reads=["kk", ("ex", exi)], writes=["k2"])
                        S.op("dve", lambda e: e.tensor_tensor(out=v64(bm), in0=c64[:, :, last:last + 1].to_broadcast([128, 8, 64]), in1=c64, op=ALU.subtract),
                             reads=[curk], writes=["bm"])
                        exi = exc[0] % 3
                        exc[0] += 1
                        S.op("act", lambda e: e.activation(out=exr[exi], in_=bm, func=AF.Exp), reads=["bm"], writes=[("ex", exi)])
                        S.op("pool", lambda e: e.tensor_tensor(out=khT[:, tsl], in0=kk, in1=exr[exi], op=ALU.mult), reads=["kk", ("ex", exi)], writes=["khT"])
                        S.op("act", lambda e: e.activation(out=Dd[:, csl], in_=c64[:, :, last], func=AF.Exp), reads=[curk], writes=["Dd"])
                    for g in range(2):
                        pb = 4 + (g % 2)
                        bkb = bank(pb).bitcast(BF16)
                        for j in range(8):
                            tb = 8 * g + j
                            S.op("pe", lambda e: e.transpose(bkb[:, 128 * j:128 * (j + 1)], khT[:, 128 * tb:128 * (tb + 1)], ident[:]),
                                 reads=["khT", "ident"], writes=[pk(pb)])
                        S.op("act", lambda e: e.activation(out=khtok[:, 8 * g:8 * g + 8, :], in_=bkb.rearrange("p (a d) -> p a d", a=8), func=AF.Copy),
                             reads=[pk(pb)], writes=["khtok"])
                    order = list(range(32)) if d == 0 else list(range(31, -1, -1))

                    def scan_step(idx):
                        n = order[idx]
                        ub = 2 + (idx // 8) % 2
                        ucol = 512 * ub + 64 * (idx % 8)
                        tb, hf = n // 2, n % 2
                        for hh in range(2):
                            S.op("pe", lambda e: e.matmul(P[64 * hh:64 * hh + 64, ucol:ucol + 64], lhsT=khtok[64 * hf:64 * hf + 64, tb, 64 * hh:64 * hh + 64],
                                                          rhs=vtok[64 * hf:64 * hf + 64, tb, 64 * hh:64 * hh + 64], start=True, stop=True),
                                 reads=["khtok", "vtok"], writes=[pk(ub)])
                        cs_, ps_ = S32[idx % 2], S32[(idx + 1) % 2]
                        if idx == 0:
                            S.op("dve", lambda e: e.tensor_copy(out=cs_, in_=P[:, ucol:ucol + 64]), reads=[pk(ub)], writes=[("S32", idx % 2)])
                        else:
                            S.op("dve", lambda e: e.scalar_tensor_tensor(out=cs_, in0=ps_, scalar=Dd[:, n:n + 1], in1=P[:, ucol:ucol + 64],
                                                                         op0=ALU.mult, op1=ALU.add),
                                 reads=[pk(ub), ("S32", (idx + 1) % 2), "Dd"], writes=[("S32", idx % 2)])
                        nxt = n + 1 if d == 0 else n - 1
                        if 0 <= nxt < 32:
                            S.op("act", lambda e: e.activation(out=Sbf[:, nxt, :], in_=cs_, func=AF.Copy),
                                 reads=[("S32", idx % 2)], writes=[("Sbf", nxt)])
                    Js = list(range(16)) if d == 0 else list(range(15, -1, -1))
                    items = [(J // 4, J % 4, hh) for J in Js for hh in range(2)]

                    def b_scores(k):
                        q4, j, hh = items[k]
                        J = 4 * q4 + j
                        po = 64 * hh
                        ai = k % 3
                        for m, (ka, qa, kkey, qkey) in enumerate(((k1, q1, "k1", "q1"), (k2, q2, "k2", "q2"))):
                            sbk = 2 * (k % 2) + m if False else (0, 1, 6, 7)[2 * (k % 2) + m]
                            S.op("pe", lambda e: e.matmul(P[:, 512 * sbk:512 * sbk + 128], lhsT=ka[po:po + 64, 128 * J:128 * (J + 1)],
                                                          rhs=qa[po:po + 64, 128 * J:128 * (J + 1)], start=True, stop=True),
                                 reads=[kkey, qkey], writes=[pk(sbk)])
                            S.op("dve", lambda e: e.tensor_tensor(out=att[m][ai], in0=P[:, 512 * sbk:512 * sbk + 128],
                                                                  in1=bmask[:, 2 * d + m, :], op=ALU.mult),
                                 reads=[pk(sbk), "bmask"], writes=[("att", m, ai)])

                    def b_rest(k):
                        q4, j, hh = items[k]
                        J = 4 * q4 + j
                        po = 64 * hh
                        ai = k % 3
                        ob = 4 + (q4 % 2)
                        inter = []
                        for hf in range(2):
                            n = 2 * J + hf
                            if (d == 0 and n >= 1) or (d == 1 and n <= 30):
                                inter.append((n, hf))
                        c0 = 512 * ob + 128 * j
                        S.op("pe", lambda e: e.matmul(P[po:po + 64, c0:c0 + 128], lhsT=vtok[:, J, po:po + 64], rhs=att[0][ai], start=True, stop=False),
                             reads=["vtok", ("att", 0, ai)], writes=[pk(ob)])
                        S.op("pe", lambda e: e.matmul(P[po:po + 64, c0:c0 + 128], lhsT=vtok[:, J, po:po + 64], rhs=att[1][ai], start=False,
                                                      stop=(len(inter) == 0)),
                             reads=["vtok", ("att", 1, ai)], writes=[pk(ob)])
                        for ii, (n, hf) in enumerate(inter):
                            S.op("pe", lambda e: e.matmul(P[po:po + 64, c0 + 64 * hf:c0 + 64 * hf + 64], lhsT=Sbf[po:po + 64, n, :],
                                                          rhs=qh[po:po + 64, 128 * J + 64 * hf:128 * J + 64 * hf + 64],
                                                          start=False, stop=(ii == len(inter) - 1)),
                                 reads=[("Sbf", n), "qh"], writes=[pk(ob)])
                        if not (j == (3 if d == 0 else 0) and hh == 1):
                            return
                        tsl = slice(512 * q4, 512 * (q4 + 1))
                        if d == 0:
                            S.op("act", lambda e: e.activation(out=o32[:, tsl], in_=bank(ob), func=AF.Copy), reads=[pk(ob)], writes=["o32"])
                            return
                        S.op("dve", lambda e: e.tensor_tensor(out=on, in0=o32[:, tsl], in1=bank(ob), op=ALU.add), reads=[pk(ob), "o32"], writes=["lf"])
                        S.op("act", lambda e: e.activation(out=sqb, in_=on, func=AF.Square), reads=["lf"], writes=["sqb"])
                        S.op("pe", lambda e: e.matmul(bank(2), lhsT=blk64[:], rhs=sqb, start=True, stop=True), reads=["sqb", "blk64"], writes=[pk(2)])
                        S.op("act", lambda e: e.activation(out=rsd, in_=bank(2), func=AF.Ln, bias=eps_t[:, 0:1], scale=1.0), reads=[pk(2), "eps"], writes=["f1"])
                        S.op("act", lambda e: e.activation(out=rsd, in_=rsd, func=AF.Exp, scale=-0.5), reads=["f1"], writes=["f1"])
                        S.op("dve", lambda e: e.scalar_tensor_tensor(out=on, in0=on, scalar=bon[:, l:l + 1], in1=rsd, op0=ALU.mult, op1=ALU.mult),
                             reads=["lf", "bon", "f1"], writes=["lf"])
                        proj_fm(wC, kC, 128 * hp, q4, 3)
                        S.op("act", lambda e: e.activation(out=gg, in_=bank(3), func=AF.Exp, scale=-1.0), reads=[pk(3)], writes=["bb"])
                        S.op("act", lambda e: e.activation(out=gg, in_=gg, func=AF.Ln, bias=one_t[:, 0:1], scale=1.0), reads=["bb", "one"], writes=["bb"])
                        S.op("act", lambda e: e.activation(out=gg, in_=gg, func=AF.Exp, scale=-1.0), reads=["bb"], writes=["bb"])
                        S.op("dve", lambda e: e.tensor_tensor(out=gg, in0=bank(3), in1=gg, op=ALU.mult), reads=[pk(3), "bb"], writes=["bb"])
                        S.op("dve", lambda e: e.tensor_tensor(out=o_b[:, hp, tsl], in0=on, in1=gg, op=ALU.mult), reads=["lf", "bb"], writes=["o_b"])
                    LA = 1
                    for k in range(len(items) + LA):
                        if k < len(items):
                            if k % 2 == 0:
                                scan_step(k)
                                scan_step(k + 1)
                            b_scores(k)
                        if k >= LA:
                            b_rest(k - LA)
            outproj_partial(l, 384, 2, o_b, "o_b", 16)
            S.barrier()

        def ss(start, r):
            return slice(start, start + 127 * r + 1, r)

        def mixer_c(l):
            cv = Carver("C")
            cm = [cv.get(f"cm{i}", [128, 9, 128], BF16) for i in range(2)]
            qn_f = cv.get("qn", [128, 6, T], BF16)
            kn_f = cv.get("kn", [128, 2, T], BF16)
            qn, kn = qn_f[0:64], kn_f[0:64]
            S.op("dve", lambda e: e.memset(qn_f[64:128], 0.0), writes=["qn"])
            S.op("dve", lambda e: e.memset(kn_f[64:128], 0.0), writes=["kn"])
            vx = [cv.get(f"vx{ri}", [128, 16, 128], BF16) for ri in range(3)]
            o_c = cv.get("o_c", [128, 3, T], BF16)
            acc_off = cv.off
            acc = cv.get("acc", [128, T])
            NPB = 4
            pT = [cv.get(f"pT{i}", [128, 384], BF16) for i in range(NPB)]
            pm = [cv.get(f"pm{i}", [128, 384], BF16) for i in range(NPB)]
            den = cv.get("den", [64, 512])
            off1 = cv.off
            cv.off = acc_off
            tsets = [(None, cv.get(f"sq{i}", [128, 512], BF16), cv.get(f"r{i}", [128, 512]), None, None) for i in range(2)]
            cv.off = off1
            wv, wk = wload(d_win[l, :, 1920:2432].rearrange("(c p) n -> p c n", p=128), (8, 512))
            qk_prep(l, wv, wk, 0, 3, 2, qn_f, "qn", tsets)
            qk_prep(l, wv, wk, 384, 1, 3, kn_f, "kn", tsets)
            S.barrier()
            wv2, wk2 = wload(d_win[l, :, 2432:2560].rearrange("(c p) n -> p c n", p=128), (8, 128))
            cmask_d = d_k["cmask"].rearrange("p (h a m) -> p h a m", h=6, a=9)
            it = 0
            si = 0
            pi = 0
            vT = acc.bitcast(BF16)[:, 0:T]
            for kv in range(2):
                for blk in range(4):
                    pb = blk % 2
                    proj_fm(wv2, wk2, 0, blk, pb)
                    S.op("act", lambda e: e.activation(out=vT[:, 512 * blk:512 * (blk + 1)], in_=bank(pb), func=AF.Copy),
                         reads=[pk(pb)], writes=["acc"])
                for ri, r in enumerate(BR):
                    nb = 16 // r
                    S.op("dve", lambda e: e.memset(vx[ri][:, :, 64:128], 1.0), writes=[("vx", ri)])
                    for g in range(2):
                        pb = 4 + (g % 2)
                        bkb = bank(pb).bitcast(BF16)
                        for j in range(8):
                            tb = 8 * g + j
                            c, b = tb // nb, tb % nb
                            st_ = c + r * 128 * b
                            S.op("pe", lambda e: e.transpose(bkb[:, 128 * j:128 * (j + 1)], vT[:, ss(st_, r)], ident[:]),
                                 reads=["acc", "ident"], writes=[pk(pb)])
                        S.op("act", lambda e: e.activation(out=vx[ri][:, 8 * g:8 * g + 8, 0:64],
                                                           in_=bkb.rearrange("p (a d) -> p a d", a=8)[:, :, 64 * kv:64 * (kv + 1)], func=AF.Copy),
                             reads=[pk(pb)], writes=[("vx", ri)])
                items = []
                for hh in range(3):
                    h = 3 * kv + hh
                    for ri, r in enumerate(BR):
                        nb = 16 // r
                        for g4 in range(4):
                            ob = 4 + (it % 2)
                            it += 1
                            for jj in range(4):
                                tbq = 4 * g4 + jj
                                c, qb = tbq // nb, tbq % nb
                                kbs = [kb for kb in (qb - 1, qb, qb + 1) if 0 <= kb < nb]
                                items.append(dict(h=h, ri=ri, r=r, nb=nb, g4=g4, jj=jj, c=c, qb=qb, kbs=kbs, ob=ob,
                                                  first_h=(ri == 0 and g4 == 0 and jj == 0), last_g=(jj == 3),
                                                  last_h=(ri == 2 and g4 == 3 and jj == 3), sbk=(0, 1, 2, 3)[si % 4], p_i=si % NPB))
                                si += 1

                def c_scores(I):
                    h, r, c, qb = I["h"], I["r"], I["c"], I["qb"]
                    if I["first_h"]:
                        S.dma("sp", "ld", lambda e: e.dma_start(out=cm[h % 2], in_=cmask_d[:, h, :, :]), writes=[("cm", h % 2)])
                    qs = c + r * 128 * qb
                    for ii, kb in enumerate(I["kbs"]):
                        ks = c + r * 128 * kb
                        S.op("pe", lambda e: e.matmul(P[:, 512 * I["sbk"] + 128 * ii:512 * I["sbk"] + 128 * (ii + 1)], lhsT=kn_f[:, kv, ss(ks, r)],
                                                      rhs=qn_f[:, h, ss(qs, r)], start=True, stop=True), reads=["kn", "qn"], writes=[pk(I["sbk"])])

                def c_rest(I):
                    h, ri, r, nb, g4, jj, c, qb, kbs, ob, sbk, p_i = (I[k] for k in ("h", "ri", "r", "nb", "g4", "jj", "c", "qb", "kbs", "ob", "sbk", "p_i"))
                    nk = len(kbs)
                    d0 = kbs[0] - qb + 1
                    S.op("act", lambda e: e.activation(out=pT[p_i][:, 0:128 * nk], in_=P[:, 512 * sbk:512 * sbk + 128 * nk], func=AF.Exp, scale=0.125),
                         reads=[pk(sbk)], writes=[("cpT", p_i)])
                    S.op("dve", lambda e: e.tensor_tensor(out=pm[p_i][:, 0:128 * nk].rearrange("p (a m) -> p a m", a=nk),
                                                          in0=pT[p_i][:, 0:128 * nk].rearrange("p (a m) -> p a m", a=nk),
                                                          in1=cm[h % 2][:, ri * 3 + d0:ri * 3 + d0 + nk, :], op=ALU.mult),
                         reads=[("cpT", p_i), ("cm", h % 2)], writes=[("cpm", p_i)])
                    for ii, kb in enumerate(kbs):
                        S.op("pe", lambda e: e.matmul(P[:, 512 * ob + 128 * jj:512 * ob + 128 * (jj + 1)], lhsT=vx[ri][:, c * nb + kb, :],
                                                      rhs=pm[p_i][:, 128 * ii:128 * (ii + 1)], start=(ii == 0), stop=(ii == nk - 1)),
                             reads=[("cpm", p_i), ("vx", ri)], writes=[pk(ob)])
                    if I["last_g"]:
                        if ri == 0:
                            S.op("act", lambda e: e.activation(out=acc[:, 512 * g4:512 * (g4 + 1)], in_=bank(ob), func=AF.Copy),
                                 reads=[pk(ob)], writes=["acc"])
                        elif r == 4:
                            av = acc.rearrange("p (i c) -> p c i", c=4)[:, g4, :]
                            S.op("dve", lambda e: e.tensor_tensor(out=av, in0=av, in1=bank(ob), op=ALU.add), reads=[pk(ob), "acc"], writes=["acc"])
                        else:
                            av = acc.rearrange("p (i c) -> p c i", c=16)[:, 4 * g4:4 * g4 + 4, :]
                            S.op("dve", lambda e: e.tensor_tensor(out=av, in0=av, in1=bank(ob).rearrange("p (a i) -> p a i", a=4), op=ALU.add),
                                 reads=[pk(ob), "acc"], writes=["acc"])
                    if I["last_h"]:
                        po = 64 * (h % 2)
                        for blk in range(4):
                            tsl = slice(512 * blk, 512 * (blk + 1))
                            S.op("act", lambda e: e.activation(out=den, in_=acc[64:128, tsl], func=AF.Ln), reads=["acc"], writes=["f_den"])
                            S.op("act", lambda e: e.activation(out=den, in_=den, func=AF.Exp, scale=-1.0), reads=["f_den"], writes=["f_den"])
                            S.op("dve", lambda e: e.tensor_tensor(out=o_c[po:po + 64, h // 2, tsl], in0=acc[0:64, tsl], in1=den, op=ALU.mult),
                                 reads=["acc", "f_den"], writes=["o_c"])
                LA = 3
                for k in range(len(items) + LA):
                    if k < len(items):
                        c_scores(items[k])
                    if k >= LA:
                        c_rest(items[k - LA])
            outproj_partial(l, 640, 3, o_c, "o_c", 16)
            S.barrier()

        def ffn(l):
            wup = d_wup[l].rearrange("(c p) (g n) -> p c g n", p=128, g=2)
            for half in range(2):
                cv = Carver("F")
                gT = cv.get("gT", [128, 22, 1024], BF16)
                cab = [[cv.get(f"c{ab}{i}", [128, 1024]) for i in range(2)] for ab in range(2)]
                sa = [cv.get(f"sa{i}", [128, 1024]) for i in range(2)]
                t0 = 1024 * half
                for jg in range(11):
                    i = wctr[0] % NSLOT
                    wctr[0] += 1
                    wv = wring[i][:, 0:4096].rearrange("p (c g n) -> p c g n", c=8, g=2)
                    wk = ("w", i)
                    for ab in range(2):
                        S.dma("pool", f"w{i}", lambda e: e.dma_start(out=wv[:, :, ab, :], in_=wup[:, :, ab, 256 * jg:256 * (jg + 1)],
                                                                   max_dma_last_dim=4096), writes=[wk])
                    for jj in range(2):
                        j = 2 * jg + jj
                        bi = j % 2
                        for ab, base in ((0, 0), (1, 1536)):
                            bks = [pk(base // 512 + q) for q in range(3)]
                            for q, (c0, n) in enumerate(((0, 512), (512, 512), (1024, 2))):
                                for k in range(8):
                                    S.op("pe", lambda e: e.matmul(P[:, base + c0:base + c0 + n], lhsT=wv[:, k, ab, 128 * jj:128 * (jj + 1)],
                                                                  rhs=hT[:, k, t0 + c0:t0 + c0 + n], start=(k == 0), stop=(k == 7)),
                                         reads=[wk, "hT"], writes=[bks[q]])
                            ch = 22 * ab + j
                            cbuf = cab[ab][bi]
                            ck_ = ("cab", ab, bi)
                            S.op("act", lambda e: e.activation(out=cbuf, in_=P[:, base + 1:base + 1025], func=AF.Identity,
                                                               scale=cw[:, l, 1, ch:ch + 1], bias=cbias[:, l, ch:ch + 1]),
                                 reads=bks + ["cw", "cbias"], writes=[ck_])
                            S.op("dve", lambda e: e.scalar_tensor_tensor(out=cbuf, in0=P[:, base:base + 1024], scalar=cw[:, l, 0, ch:ch + 1],
                                                                         in1=cbuf, op0=ALU.mult, op1=ALU.add),
                                 reads=bks + ["cw", ck_], writes=[ck_])
                            S.op("dve", lambda e: e.scalar_tensor_tensor(out=cbuf, in0=P[:, base + 2:base + 1026], scalar=cw[:, l, 2, ch:ch + 1],
                                                                         in1=cbuf, op0=ALU.mult, op1=ALU.add),
                                 reads=bks + ["cw", ck_], writes=[ck_])
                        S.op("act", lambda e: e.activation(out=sa[bi], in_=cab[0][bi], func=AF.Silu), reads=[("cab", 0, bi)], writes=[("sa", bi)])
                        S.op("dve", lambda e: e.tensor_tensor(out=gT[:, j, :], in0=sa[bi], in1=cab[1][bi], op=ALU.mult),
                             reads=[("sa", bi), ("cab", 1, bi)], writes=["gT"])
                    if l + 1 < depth and jg % 2 == 1 or (l + 1 < depth and jg == 10):
                        og = 6 * half + jg // 2 + (1 if jg == 10 else 0) - (0 if jg != 10 else 1)
                        og = 6 * half + (jg // 2 if jg != 10 else 5)
                        adaln_og(l + 1, og, 3584)
                        if jg == 10:
                            adaln_finish(l + 1, 3584, 24 * half, 24 * half + 24)
                it = 0
                for m in range(8):
                    wv, wk = wload(d_wdn[l, :, 128 * m:128 * (m + 1)].rearrange("(c p) n -> p c n", p=128), (22, 128))
                    for n2 in range(2):
                        pb = 6 + (it % 2)
                        it += 1
                        for j in range(22):
                            S.op("pe", lambda e: e.matmul(bank(pb), lhsT=wv[:, j, :], rhs=gT[:, j, 512 * n2:512 * (n2 + 1)],
                                                          start=(j == 0), stop=(j == 21)), reads=[wk, "gT"], writes=[pk(pb)])
                        tsl = slice(t0 + 512 * n2, t0 + 512 * (n2 + 1))
                        S.op("dve", lambda e: e.scalar_tensor_tensor(out=xT[:, m, tsl], in0=bank(pb), scalar=mod[:, l, 40 + m:41 + m],
                                                                     in1=xT[:, m, tsl], op0=ALU.mult, op1=ALU.add),
                             reads=[pk(pb), "mod", "xT"], writes=["xT"])
                S.barrier()

        for l in range(depth):
            norm_mod(l, 0)
            if do_a:
                mixer_a(l)
            if do_b:
                mixer_b(l)
            if do_c:
                mixer_c(l)
            if do_ffn:
                norm_mod(l, 1)
                ffn(l)

        S.dma("sp", "st", lambda e: e.dma_start(out=d_out.rearrange("(c p) t -> p c t", p=128), in_=xT[:]), reads=["xT"])
        S.emit(final_dsems=["st"])
    return nc


def _prep_shared(inp):
    f = lambda a: np.ascontiguousarray(np.asarray(a, dtype=np.float32))
    sh = {}
    sh["w_ada"] = f(inp["w_ada"])
    sh["b_ada_l"] = f(np.asarray(inp["b_ada"]).reshape(NL, 48, 128).transpose(2, 0, 1))
    sh["norm_g_l"] = f(np.asarray(inp["norm_g"]).reshape(NL, 2, 8, 128).transpose(3, 0, 1, 2))
    sh["w_in"] = f(inp["w_in"])
    hn = np.stack([np.asarray(inp[k]) for k in ("a_q_norm", "a_k_norm", "c_q_norm", "c_k_norm")], -1)
    sh["hn"] = f(np.concatenate([hn.transpose(1, 0, 2)] * 2, 0))
    bo = np.asarray(inp["b_out_norm"])
    sh["bon"] = f(np.concatenate([bo, bo], 1).T)
    bl = np.asarray(inp["b_lb"]).reshape(2, NL, 2, 128)
    sh["b_lb_l"] = f(bl.transpose(3, 0, 2, 1).reshape(128, 4, NL))
    sh["w_out"] = f(inp["w_out"])
    sh["w_up"] = f(inp["w_up"])
    sh["conv_w_l"] = f(np.asarray(inp["conv_w"]).reshape(NL, 3, 44, 128).transpose(3, 0, 1, 2))
    sh["conv_b_l"] = f(np.asarray(inp["conv_b"]).reshape(NL, 44, 128).transpose(2, 0, 1))
    sh["w_down"] = f(inp["w_down"])
    cst = _consts()
    sh["rst"] = cst.pop("rst")
    for k, v in cst.items():
        sh["k_" + k] = np.ascontiguousarray(v)
    return sh


def run(inputs, ncores=8, **bk):
    x = np.asarray(inputs["x"], dtype=np.float32)
    c = np.asarray(inputs["c"], dtype=np.float32)
    nc = build(**bk)
    sh = _prep_shared(inputs)
    in_maps = []
    for b in range(ncores):
        m = dict(sh)
        m["xT"] = np.ascontiguousarray(x[b].T)
        m["c128"] = np.ascontiguousarray(c[b].reshape(8, 128).T)
        in_maps.append(m)
    res = run_bass_kernel_spmd(nc, in_maps, core_ids=list(range(ncores)))
    if bk.get("dbg"):
        return res.results[0]["dbg"]
    return np.stack([np.ascontiguousarray(r["outT"].T) for r in res.results]).astype(np.float32)


def kernel(**inputs):
    return run(inputs, ncores=8)
```
